# Optimizing a Trainium2 kernel written in Bass

```python
import math
import jax, jax.numpy as jnp
from jax import lax
import numpy as np


D_MODEL = 2048
BATCH = 4
SEQ = 8192
DEPTH = 1

MLA_HEADS = 8
MLA_NOPE_DIM = 128
MLA_ROPE_DIM = 64
MLA_QK_DIM = MLA_NOPE_DIM + MLA_ROPE_DIM
MLA_V_DIM = 128
MLA_Q_LORA = 512
MLA_KV_LORA = 256
MLA_WIDTH = MLA_HEADS * MLA_V_DIM
GDN_HEADS = 8
GDN_K_DIM = 128
GDN_V_DIM = 128
GDN_WIDTH = GDN_HEADS * GDN_V_DIM
GDN_QKV = 2 * GDN_HEADS * GDN_K_DIM + GDN_WIDTH
CONV_WIDTH = 4
CHUNK = 64
D_MIX = MLA_WIDTH + GDN_WIDTH
ROPE_THETA = 10000.0
NORM_EPS = 1e-6
Q_BLOCK = 128
SPLIT_SIZES = (MLA_Q_LORA, MLA_KV_LORA, MLA_ROPE_DIM, MLA_WIDTH,
               GDN_HEADS * GDN_K_DIM, GDN_HEADS * GDN_K_DIM, GDN_WIDTH,
               GDN_HEADS, GDN_HEADS, GDN_WIDTH)
IN_COLS = sum(SPLIT_SIZES)

kernel_name = 'hymba_mla_gdn_gated_hybrid'


def rms_norm(x, gain):
    xf = x.astype(jnp.float32)
    y = xf * lax.rsqrt(jnp.mean(xf * xf, axis=-1, keepdims=True) + NORM_EPS)
    return (y * gain.astype(jnp.float32)).astype(x.dtype)


def l2_norm(x):
    xf = x.astype(jnp.float32)
    return xf * lax.rsqrt(jnp.sum(xf * xf, axis=-1, keepdims=True) + NORM_EPS)


def rope(x, positions):
    half = x.shape[-1] // 2
    inv_freq = jnp.power(ROPE_THETA, -jnp.arange(half, dtype=jnp.float32) / half)
    ang = positions.astype(jnp.float32)[..., None] * inv_freq
    cos = jnp.cos(ang)[:, :, None, :]
    sin = jnp.sin(ang)[:, :, None, :]
    xf = x.astype(jnp.float32)
    x1, x2 = xf[..., :half], xf[..., half:]
    return jnp.concatenate([x1 * cos - x2 * sin, x2 * cos + x1 * sin], axis=-1).astype(x.dtype)


def mla_branch(cq, ckv, k_rope, positions, q_a_gain, kv_a_gain, w_uq, w_ukv, q_gain, k_gain):
    B, S, _ = cq.shape
    q = (rms_norm(cq, q_a_gain) @ w_uq).reshape(B, S, MLA_HEADS, MLA_QK_DIM)
    kv = (rms_norm(ckv, kv_a_gain) @ w_ukv).reshape(B, S, MLA_HEADS, MLA_NOPE_DIM + MLA_V_DIM)
    k_nope, v = kv[..., :MLA_NOPE_DIM], kv[..., MLA_NOPE_DIM:]
    k_shared = jnp.broadcast_to(k_rope[:, :, None, :], (B, S, MLA_HEADS, MLA_ROPE_DIM))
    k = jnp.concatenate([k_nope, k_shared], axis=-1)
    q = rms_norm(q, q_gain)
    k = rms_norm(k, k_gain)
    q = jnp.concatenate([q[..., :MLA_NOPE_DIM], rope(q[..., MLA_NOPE_DIM:], positions)], axis=-1)
    k = jnp.concatenate([k[..., :MLA_NOPE_DIM], rope(k[..., MLA_NOPE_DIM:], positions)], axis=-1)
    q = q.transpose(0, 2, 1, 3)
    k = k.transpose(0, 2, 1, 3)
    v = v.transpose(0, 2, 1, 3)
    scale = MLA_QK_DIM ** -0.5
    key_idx = jnp.arange(S)

    def block(i):
        start = i * Q_BLOCK
        qb = lax.dynamic_slice_in_dim(q, start, Q_BLOCK, axis=2)
        s = jnp.einsum('bhqd,bhkd->bhqk', qb, k, preferred_element_type=jnp.float32) * scale
        q_idx = start + jnp.arange(Q_BLOCK)
        s = jnp.where(key_idx[None, :] <= q_idx[:, None], s, -jnp.inf)
        p = jax.nn.softmax(s, axis=-1).astype(v.dtype)
        return jnp.einsum('bhqk,bhkd->bqhd', p, v)

    o = lax.map(block, jnp.arange(S // Q_BLOCK))
    return o.transpose(1, 0, 2, 3, 4).reshape(B, S, MLA_WIDTH)


def causal_conv_silu(x, w):
    S = x.shape[1]
    xp = jnp.pad(x, ((0, 0), (CONV_WIDTH - 1, 0), (0, 0)))
    y = xp[:, 0:S, :] * w[0]
    for j in range(1, CONV_WIDTH):
        y = y + xp[:, j:j + S, :] * w[j]
    return jax.nn.silu(y)


def chunked_gated_delta(q, k, v, g, beta):
    B, S, H, Dk = q.shape
    Dv = v.shape[-1]
    N = S // CHUNK

    def to_chunks(t):
        return t.reshape(B, N, CHUNK, H, t.shape[-1]).transpose(0, 3, 1, 2, 4)

    q, k, v = to_chunks(q), to_chunks(k), to_chunks(v)
    g = g.reshape(B, N, CHUNK, H).transpose(0, 3, 1, 2)
    beta = beta.reshape(B, N, CHUNK, H).transpose(0, 3, 1, 2)
    gc = jnp.cumsum(g, axis=-1)
    idx = jnp.arange(CHUNK)
    lower_incl = idx[:, None] >= idx[None, :]
    strict = idx[:, None] > idx[None, :]
    diff = gc[..., :, None] - gc[..., None, :]
    decay = jnp.where(lower_incl, jnp.exp(jnp.where(lower_incl, diff, 0.0)), 0.0)
    k_beta = k * beta[..., None]
    v_beta = v * beta[..., None]
    L = jnp.where(strict, jnp.einsum('bhncd,bhnjd->bhncj', k_beta, k) * decay, 0.0)
    eye = jnp.eye(CHUNK, dtype=jnp.float32)
    rhs = jnp.concatenate([v_beta, k_beta * jnp.exp(gc)[..., None]], axis=-1)
    sol = lax.linalg.triangular_solve(eye + L, rhs, left_side=True, lower=True, unit_diagonal=True)
    u, w = sol[..., :Dv], sol[..., Dv:]
    attn_intra = jnp.einsum('bhncd,bhnjd->bhncj', q, k) * decay
    q_dec = q * jnp.exp(gc)[..., None]
    k_dec = k * jnp.exp(gc[..., -1:] - gc)[..., None]
    g_last = jnp.exp(gc[..., -1])

    def mv(t):
        return jnp.moveaxis(t, 2, 0)

    xs = (mv(u), mv(w), mv(q_dec), mv(k_dec), mv(attn_intra), jnp.moveaxis(g_last, 2, 0))

    def step(state, inp):
        u_c, w_c, qd, kd, a_c, gl = inp
        v_new = u_c - jnp.einsum('bhck,bhkv->bhcv', w_c, state)
        o = jnp.einsum('bhck,bhkv->bhcv', qd, state) + jnp.einsum('bhcj,bhjv->bhcv', a_c, v_new)
        state = state * gl[..., None, None] + jnp.einsum('bhck,bhcv->bhkv', kd, v_new)
        return state, o

    state0 = jnp.zeros((B, H, Dk, Dv), jnp.float32)
    _, o = lax.scan(step, state0, xs)
    return o.transpose(1, 0, 3, 2, 4).reshape(B, S, H, Dv)


def gdn_branch(gq, gk, gv, ga, gb, conv_w, a_log, dt_bias, out_gain):
    B, S, _ = gq.shape
    qkv = causal_conv_silu(jnp.concatenate([gq, gk, gv], axis=-1), conv_w)
    hk = GDN_HEADS * GDN_K_DIM
    q = qkv[..., :hk].reshape(B, S, GDN_HEADS, GDN_K_DIM)
    k = qkv[..., hk:2 * hk].reshape(B, S, GDN_HEADS, GDN_K_DIM)
    v = qkv[..., 2 * hk:].reshape(B, S, GDN_HEADS, GDN_V_DIM).astype(jnp.float32)
    q = l2_norm(q) * (GDN_K_DIM ** -0.5)
    k = l2_norm(k)
    g = -jnp.exp(a_log.astype(jnp.float32)) * jax.nn.softplus(ga.astype(jnp.float32) + dt_bias.astype(jnp.float32))
    beta = jax.nn.sigmoid(gb.astype(jnp.float32))
    o = chunked_gated_delta(q, k, v, g, beta)
    o = rms_norm(o, out_gain).astype(gq.dtype)
    return o.reshape(B, S, GDN_WIDTH)


def setup_inputs(seed: int = 0) -> dict:
    key = jax.random.key(seed)
    ks = jax.random.split(key, 16)
    f32 = jnp.float32

    def dense(k, shape, fan_in):
        return jax.random.normal(k, shape, f32) * fan_in ** -0.5

    def gain(k, shape):
        return 1.0 + 0.02 * jax.random.normal(k, shape, f32)

    x = jax.random.normal(ks[0], (BATCH, SEQ, D_MODEL), f32)
    positions = (jax.random.randint(ks[1], (BATCH, 1), 0, 1024, jnp.int32)
                 + jnp.arange(SEQ, dtype=jnp.int32)[None, :])
    dt = jnp.exp(jax.random.uniform(ks[11], (DEPTH, GDN_HEADS), f32,
                                    minval=math.log(1e-3), maxval=math.log(1e-1)))
    dt_bias = dt + jnp.log(-jnp.expm1(-dt))
    a_log = jnp.log(jax.random.uniform(ks[12], (DEPTH, GDN_HEADS), f32, minval=1.0, maxval=16.0))
    return {
        'x': x,
        'positions': positions,
        'norm_gain': gain(ks[2], (DEPTH, D_MODEL)),
        'w_in': dense(ks[3], (DEPTH, D_MODEL, IN_COLS), D_MODEL),
        'mla_q_a_gain': gain(ks[4], (DEPTH, MLA_Q_LORA)),
        'mla_kv_a_gain': gain(ks[5], (DEPTH, MLA_KV_LORA)),
        'w_uq': dense(ks[6], (DEPTH, MLA_Q_LORA, MLA_HEADS * MLA_QK_DIM), MLA_Q_LORA),
        'w_ukv': dense(ks[7], (DEPTH, MLA_KV_LORA, MLA_HEADS * (MLA_NOPE_DIM + MLA_V_DIM)), MLA_KV_LORA),
        'mla_q_norm_gain': gain(ks[8], (DEPTH, MLA_QK_DIM)),
        'mla_k_norm_gain': gain(ks[9], (DEPTH, MLA_QK_DIM)),
        'gdn_conv_w': dense(ks[10], (DEPTH, CONV_WIDTH, GDN_QKV), CONV_WIDTH),
        'gdn_a_log': a_log,
        'gdn_dt_bias': dt_bias,
        'gdn_out_norm_gain': gain(ks[13], (DEPTH, GDN_V_DIM)),
        'w_out': dense(ks[14], (DEPTH, D_MIX, D_MODEL), D_MIX),
    }


def reference(x, positions, norm_gain, w_in, mla_q_a_gain, mla_kv_a_gain, w_uq, w_ukv,
              mla_q_norm_gain, mla_k_norm_gain, gdn_conv_w, gdn_a_log, gdn_dt_bias,
              gdn_out_norm_gain, w_out):
    B, S, _ = x.shape
    split_idx = np.cumsum(SPLIT_SIZES)[:-1].tolist()
    h = x
    for layer in range(DEPTH):
        xn = rms_norm(h, norm_gain[layer])
        proj = xn @ w_in[layer]
        cq, ckv, k_rope, mla_gate, gq, gk, gv, ga, gb, gdn_gate = jnp.split(proj, split_idx, axis=-1)
        o_mla = mla_branch(cq, ckv, k_rope, positions, mla_q_a_gain[layer], mla_kv_a_gain[layer],
                           w_uq[layer], w_ukv[layer], mla_q_norm_gain[layer], mla_k_norm_gain[layer])
        o_mla = o_mla * jax.nn.silu(mla_gate)
        o_gdn = gdn_branch(gq, gk, gv, ga, gb, gdn_conv_w[layer], gdn_a_log[layer],
                           gdn_dt_bias[layer], gdn_out_norm_gain[layer])
        o_gdn = o_gdn * jax.nn.silu(gdn_gate)
        mixed = jnp.concatenate([o_mla, o_gdn], axis=-1)
        h = h + mixed @ w_out[layer]
    return h
```

```python
import os
import numpy as np
from contextlib import ExitStack
import concourse.bass as bass
import concourse.mybir as mybir
from concourse.bass_utils import run_bass_kernel_spmd

F32 = mybir.dt.float32
BF16 = mybir.dt.bfloat16
I32 = mybir.dt.int32
ALU = mybir.AluOpType
AF = mybir.ActivationFunctionType
AX = mybir.AxisListType

D = 2048
NG = 47
EPS = 1e-6
KDMA = 8


class Buf:
    __slots__ = ("name", "w", "r")

    def __init__(self, name=""):
        self.name = name
        self.w = None
        self.r = {}


class Prog:
    def __init__(self, nc, es):
        self.nc = nc
        self.eng = {"pe": nc.tensor, "act": nc.scalar, "dve": nc.vector, "pool": nc.gpsimd, "sp": nc.sync}
        self.esem = {k: es.enter_context(nc.semaphore("s_" + k)) for k in self.eng}
        self.cnt = {k: 0 for k in self.eng}
        self.pend = {k: False for k in self.eng}
        self.seen = {k: {} for k in self.eng}
        self.dsems = {k: [es.enter_context(nc.semaphore("d_%s%d" % (k, i))) for i in range(KDMA)]
                      for k in ("sp", "pool", "act")}
        self.dval = {k: [0] * KDMA for k in self.dsems}
        self.dslot = {k: 0 for k in self.dsems}
        self.nwaits = 0
        self.maxwait = {}

    def op(self, e, fn, r=(), w=(), dma=False, inc=True):
        need = {}

        def add(t):
            if t is not None and need.get(t[0], 0) < t[1]:
                need[t[0]] = t[1]

        for b in r:
            add(b.w)
        for b in w:
            add(b.w)
            for s, v in b.r.items():
                add((s, v))
        if dma:
            slot = self.dslot[e]
            self.dslot[e] = (slot + 1) % KDMA
            sem = self.dsems[e][slot]
            prev = self.dval[e][slot]
            if prev > 0:
                add((sem, prev))
            self.dval[e][slot] = prev + 16
            tok = (sem, prev + 16)
        else:
            if inc:
                self.cnt[e] += 1
                self.pend[e] = False
                tok = (self.esem[e], self.cnt[e])
            else:
                self.pend[e] = True
                tok = (self.esem[e], self.cnt[e] + 1)
        seen = self.seen[e]
        eo = self.eng[e]
        for s, v in need.items():
            if e == "pe" and s is self.esem["pe"]:
                continue
            if seen.get(s, 0) >= v:
                continue
            seen[s] = v
            eo.wait_ge(s, v)
            self.nwaits += 1
            if self.maxwait.get(s, 0) < v:
                self.maxwait[s] = v
        ins = fn(eo)
        if dma:
            ins.then_inc(tok[0], 16)
        elif inc:
            ins.then_inc(tok[0], 1)
        for b in r:
            if b.r.get(tok[0], 0) < tok[1]:
                b.r[tok[0]] = tok[1]
        for b in w:
            b.w = tok
            b.r = {}
        return tok

    def barrier(self):
        for e, eo in self.eng.items():
            assert not self.pend[e]
        for e, eo in self.eng.items():
            seen = self.seen[e]
            for e2 in self.eng:
                s, v = self.esem[e2], self.cnt[e2]
                if v > 0 and seen.get(s, 0) < v and not (e == e2 and e == "pe"):
                    seen[s] = v
                    eo.wait_ge(s, v)
            for q in self.dsems:
                for s, v in zip(self.dsems[q], self.dval[q]):
                    if v > 0 and seen.get(s, 0) < v:
                        seen[s] = v
                        eo.wait_ge(s, v)

    def wait_all(self, e, bufs):
        eo = self.eng[e]
        need = {}
        for b in bufs:
            for t in ([b.w] if b.w else []) + list(b.r.items()):
                if need.get(t[0], 0) < t[1]:
                    need[t[0]] = t[1]
        for s, v in need.items():
            eo.wait_ge(s, v)


def build_nc(T, dbg=None):
    NT = T // 128
    TSB = min(T, 2048)
    NSB = T // TSB
    NTB = TSB // 512
    nc = bass.Bass("TRN2", target_bir_lowering=False)

    def din(name, shape, dt=F32):
        return nc.dram_tensor(name, shape, dt, kind="ExternalInput").ap()

    x = din("x", [T, D])
    win = din("win", [NG, 128, 16, 128])
    wab = din("wab", [128, 16, 16])
    ngain = din("ngain", [128, 16])
    out = nc.dram_tensor("out", [T, D], F32, kind="ExternalOutput").ap()
    projT = nc.dram_tensor("projT", [NG * 128, T], BF16,
                           kind="Internal").ap()
    pos64 = din("pos64", [64, T], I32)
    wuq = din("wuq", [128, 4, 2048])
    wukv = din("wukv", [128, 2, 2048])
    qag = din("qag", [128, 4])
    kvag = din("kvag", [128, 2])
    gvec = din("gvec", [128, 8])
    cw = din("cw", [128, 24, 4])
    alog64 = din("alog64", [64, 8])
    dtb64 = din("dtb64", [64, 8])
    wout = din("wout", [128, 16, 2048])
    okind = "ExternalOutput" if dbg else "Internal"
    QT = nc.dram_tensor("QT", [8, 192, T], BF16, kind="Internal").ap()
    KT = nc.dram_tensor("KT", [8, 192, T], BF16, kind="Internal").ap()
    Vs = nc.dram_tensor("Vs", [T, 1024], BF16, kind="Internal").ap()
    mixT = nc.dram_tensor("mixT", [2048, T], BF16, kind=okind).ap()

    with ExitStack() as es:
        P = Prog(nc, es)

        def sb(name, shape, dt):
            return es.enter_context(nc.sbuf_tensor(name, shape, dt))

        ident = sb("ident", [128, 128], BF16)
        iot = sb("iot", [128, 128], I32)
        gabtm = sb("gabtm", [64, 2 * NT, 16], F32)
        ngs = sb("ngs", [128, 16], F32)
        B_ident, B_iot, B_gab, B_ngs = Buf("ident"), Buf("iot"), Buf("gab"), Buf("ngs")
        P.op("pool", lambda e: e.iota(iot[:], pattern=[[1, 128]], base=0, channel_multiplier=-1), w=[B_iot])
        P.op("dve", lambda e: e.tensor_single_scalar(out=ident[:], in_=iot[:], scalar=0, op=ALU.is_equal),
             r=[B_iot], w=[B_ident])
        P.op("sp", lambda e: e.dma_start(out=ngs[:], in_=ngain[:, :]), w=[B_ngs], dma=True)

        with ExitStack() as ps:
            def sb1(name, shape, dt):
                return ps.enter_context(nc.sbuf_tensor(name, shape, dt))

            def pp1(name, shape, dt):
                return ps.enter_context(nc.psum_tensor(name, shape, dt))

            xt = [sb1("xt%d" % i, [128, D], F32) for i in range(2)]
            xs = [sb1("xs%d" % i, [128, D], BF16) for i in range(2)]
            junk = sb1("junk", [128, D], BF16)
            ss = [sb1("ss%d" % i, [128, 1], F32) for i in range(2)]
            rs = [sb1("rs%d" % i, [128, 1], F32) for i in range(2)]
            xnT = sb1("xnT", [128, 16, TSB], BF16)
            wf = [sb1("wf%d" % i, [128, 16, 128], F32) for i in range(2)]
            wb = [sb1("wb%d" % i, [128, 16, 128], BF16) for i in range(2)]
            ot = [sb1("ot%d" % i, [128, TSB], BF16) for i in range(2)]
            wabf = sb1("wabf", [128, 16, 16], F32)
            wabb = sb1("wabb", [128, 16, 16], BF16)
            ptr = [pp1("ptr%d" % i, [128, 8, 128], BF16) for i in range(2)]
            pacc = [pp1("pacc%d" % i, [128, 512], F32) for i in range(4)]
            pgab = pp1("pgab", [128, 512], F32)
            B_xt = [Buf() for _ in range(2)]
            B_xs = [Buf() for _ in range(2)]
            B_junk = Buf()
            B_ss = [Buf() for _ in range(2)]
            B_rs = [Buf() for _ in range(2)]
            B_xnT = [Buf() for _ in range(TSB // 128)]
            B_wf = [Buf() for _ in range(2)]
            B_wb = [Buf() for _ in range(2)]
            B_ot = [Buf() for _ in range(2)]
            B_wab, B_wabb = Buf(), Buf()
            B_ptr = [Buf() for _ in range(2)]
            B_pacc = [Buf() for _ in range(4)]
            B_pgab = Buf()
            B_proj = Buf("projT")

            P.op("sp", lambda e: e.dma_start(out=wabf[:], in_=wab[:, :, :]), w=[B_wab], dma=True)
            P.op("dve", lambda e: e.tensor_copy(out=wabb[:], in_=wabf[:]), r=[B_wab], w=[B_wabb])

            nptr = 0
            for sbi in range(NSB):
                for tl in range(TSB // 128):
                    tg = sbi * (TSB // 128) + tl
                    k = tg % 2
                    P.op("sp", lambda e, k=k, tg=tg: e.dma_start(out=xt[k][:], in_=x[tg * 128:(tg + 1) * 128, :]),
                         w=[B_xt[k]], dma=True)
                    P.op("act", lambda e, k=k: e.activation(out=junk[:], in_=xt[k][:], func=AF.Square,
                                                            accum_out=ss[k][:]),
                         r=[B_xt[k]], w=[B_junk, B_ss[k]])
                    P.op("act", lambda e, k=k: e.activation(out=rs[k][:], in_=ss[k][:], func=AF.Ln,
                                                            scale=1.0 / D, bias=EPS),
                         r=[B_ss[k]], w=[B_rs[k]])
                    P.op("act", lambda e, k=k: e.activation(out=rs[k][:], in_=rs[k][:], func=AF.Exp, scale=-0.5),
                         r=[B_rs[k]], w=[B_rs[k]])
                    P.op("act", lambda e, k=k: e.activation(out=xs[k][:], in_=xt[k][:], func=AF.Copy,
                                                            scale=rs[k][:]),
                         r=[B_xt[k], B_rs[k]], w=[B_xs[k]])
                    for half in range(2):
                        pk = nptr % 2
                        nptr += 1
                        for j in range(8):
                            c = half * 8 + j
                            P.op("pe", lambda e, pk=pk, j=j, c=c, k=k: e.transpose(
                                out=ptr[pk][:, j, :], in_=xs[k][:, c * 128:(c + 1) * 128], identity=ident[:]),
                                r=[B_xs[k], B_ident], w=[B_ptr[pk]], inc=(j == 7))
                        P.op("dve", lambda e, pk=pk, half=half, tl=tl: e.tensor_tensor(
                            out=xnT[:, half * 8:(half + 1) * 8, tl * 128:(tl + 1) * 128], in0=ptr[pk][:],
                            in1=ngs[:, half * 8:(half + 1) * 8].unsqueeze(2).broadcast_to([128, 8, 128]),
                            op=ALU.mult),
                            r=[B_ptr[pk], B_ngs], w=[B_xnT[tl]])
                    for hf in range(2):
                        for c in range(16):
                            P.op("pe", lambda e, c=c, tl=tl, hf=hf: e.matmul(
                                pgab[0:64, hf * 16:(hf + 1) * 16],
                                lhsT=xnT[:, c, tl * 128 + hf * 64:tl * 128 + hf * 64 + 64], rhs=wabb[:, c, :],
                                start=(c == 0), stop=(c == 15)),
                                r=[B_xnT[tl], B_wabb], w=[B_pgab], inc=(c == 15))
                    P.op("act", lambda e, tg=tg: e.activation(
                        out=gabtm[:, 2 * tg:2 * tg + 2, :],
                        in_=pgab[0:64, 0:32].rearrange("p (a b) -> p a b", a=2), func=AF.Copy),
                        r=[B_pgab], w=[B_gab])
                for g in range(NG):
                    k = g % 2
                    P.op("sp", lambda e, k=k, g=g: e.dma_start(out=wf[k][:], in_=win[g]), w=[B_wf[k]], dma=True)
                    P.op("dve", lambda e, k=k: e.tensor_copy(out=wb[k][:], in_=wf[k][:]), r=[B_wf[k]], w=[B_wb[k]])
                    if g == 46:
                        P.op("dve", lambda e, k=k: e.tensor_scalar(out=wb[k][:, :, 64:96], in0=wb[k][:, :, 64:96],
                                                                   scalar1=-1.0, scalar2=None, op0=ALU.mult),
                             r=[B_wb[k]], w=[B_wb[k]])
                    for tb in range(NTB):
                        for c in range(16):
                            P.op("pe", lambda e, k=k, tb=tb, c=c: e.matmul(
                                pacc[tb][:], lhsT=wb[k][:, c, :], rhs=xnT[:, c, tb * 512:(tb + 1) * 512],
                                start=(c == 0), stop=(c == 15)),
                                r=[B_wb[k]] + B_xnT[tb * 4:(tb + 1) * 4], w=[B_pacc[tb]], inc=(c == 15))
                        P.op("act", lambda e, k=k, tb=tb: e.activation(
                            out=ot[k][:, tb * 512:(tb + 1) * 512], in_=pacc[tb][:], func=AF.Copy),
                            r=[B_pacc[tb]], w=[B_ot[k]])
                    P.op("pool", lambda e, k=k, g=g, sbi=sbi: e.dma_start(
                        out=projT[g * 128:(g + 1) * 128, sbi * TSB:(sbi + 1) * TSB], in_=ot[k][:]),
                        r=[B_ot[k]], w=[B_proj], dma=True)


        P.barrier()
        NB = T // 512
        ones_b = sb("ones_b", [128, 128], BF16)
        gv = sb("gv", [128, 8], F32)
        B_ones, B_gv = Buf(), Buf()
        P.op("pool", lambda e: e.memset(ones_b[:], 1.0), w=[B_ones])
        P.op("sp", lambda e: e.dma_start(out=gv[:], in_=gvec[:, :]), w=[B_gv], dma=True)
        B_QT, B_KT, B_Vs, B_mix = Buf(), Buf(), Buf(), Buf()
        LN192 = float(np.log(192.0))
        PI = float(np.pi)
        C1 = 6.28125
        C2 = float(2 * np.pi - 6.28125)

        def rstd_from_psum(ps_ap, B_ps, dst_ap, B_dst, n, extra_bias=0.0, parts=128):
            P.op("act", lambda e: e.activation(out=dst_ap, in_=ps_ap, func=AF.Ln, scale=1.0 / n, bias=EPS),
                 r=[B_ps], w=[B_dst])
            P.op("act", lambda e: e.activation(out=dst_ap, in_=dst_ap, func=AF.Exp, scale=-0.5, bias=extra_bias),
                 r=[B_dst], w=[B_dst])

        with ExitStack() as ps:
            def sb2(name, shape, dt):
                return ps.enter_context(nc.sbuf_tensor(name, shape, dt))

            def pp2(name, shape, dt=F32):
                return ps.enter_context(nc.psum_tensor(name, shape, dt))

            wuqf = sb2("wuqf", [128, 4, 2048], F32)
            wuqb = sb2("wuqb", [128, 4, 2048], BF16)
            wukvf = sb2("wukvf", [128, 2, 2048], F32)
            wukvb = sb2("wukvb", [128, 2, 2048], BF16)
            qag_s = sb2("qag_s", [128, 4], F32)
            kvag_s = sb2("kvag_s", [128, 2], F32)
            ivi = sb2("ivi", [64, 1], I32)
            invf = sb2("invf", [64, 1], F32)
            B_wuqf, B_wuqb, B_wukvf, B_wukvb, B_qag, B_kvag, B_ivi, B_invf = [Buf() for _ in range(8)]
            P.op("sp", lambda e: e.dma_start(out=wuqf[:], in_=wuq[:, :, :]), w=[B_wuqf], dma=True)
            P.op("sp", lambda e: e.dma_start(out=wukvf[:], in_=wukv[:, :, :]), w=[B_wukvf], dma=True)
            P.op("sp", lambda e: e.dma_start(out=qag_s[:], in_=qag[:, :]), w=[B_qag], dma=True)
            P.op("sp", lambda e: e.dma_start(out=kvag_s[:], in_=kvag[:, :]), w=[B_kvag], dma=True)
            for kc in range(4):
                P.op("dve", lambda e, kc=kc: e.tensor_scalar(out=wuqb[:, kc, :], in0=wuqf[:, kc, :],
                                                             scalar1=qag_s[:, kc:kc + 1], scalar2=None, op0=ALU.mult),
                     r=[B_wuqf, B_qag], w=[B_wuqb])
            for kc in range(2):
                P.op("dve", lambda e, kc=kc: e.tensor_scalar(out=wukvb[:, kc, :], in0=wukvf[:, kc, :],
                                                             scalar1=kvag_s[:, kc:kc + 1], scalar2=None, op0=ALU.mult),
                     r=[B_wukvf, B_kvag], w=[B_wukvb])
            wq4 = wuqb[:].rearrange("p k (h c) -> p k h c", c=256)
            for kc in range(4):
                P.op("dve", lambda e, kc=kc: e.tensor_scalar(out=wq4[:, kc, :, 192:224], in0=wq4[:, kc, :, 192:224],
                                                             scalar1=-1.0, scalar2=None, op0=ALU.mult),
                     r=[B_wuqb], w=[B_wuqb])
            P.op("pool", lambda e: e.iota(ivi[0:32, :], pattern=[[0, 1]], base=0, channel_multiplier=1), w=[B_ivi])
            P.op("pool", lambda e: e.iota(ivi[32:64, :], pattern=[[0, 1]], base=0, channel_multiplier=1), w=[B_ivi])
            P.op("dve", lambda e: e.tensor_copy(out=invf[:], in_=ivi[:]), r=[B_ivi], w=[B_invf])
            P.op("act", lambda e: e.activation(out=invf[:], in_=invf[:], func=AF.Exp,
                                               scale=-float(np.log(10000.0)) / 32.0), r=[B_invf], w=[B_invf])

            cqT = sb2("cqT", [128, 4, 512], BF16)
            sqb = sb2("sqb", [128, 4, 512], BF16)
            cqn = sb2("cqn", [128, 4, 512], BF16)
            ckvT = sb2("ckvT", [128, 2, 512], BF16)
            ckvn = sb2("ckvn", [128, 2, 512], BF16)
            krA = sb2("krA", [64, 512], BF16)
            krB = sb2("krB", [64, 512], BF16)
            rst = sb2("rst", [128, 512], F32)
            rsth = sb2("rsth", [128, 512], F32)
            posi = sb2("posi", [64, 512], I32)
            ang = sb2("ang", [64, 512], F32)
            nfi = sb2("nfi", [64, 512], I32)
            nff = sb2("nff", [64, 512], F32)
            th = sb2("th", [64, 512], F32)
            thc = sb2("thc", [64, 512], F32)
            msk = sb2("msk", [64, 512], F32)
            cosT = sb2("cosT", [64, 512], F32)
            sinT = sb2("sinT", [64, 512], F32)
            ra = sb2("ra", [64, 512], F32)
            rb_ = sb2("rb_", [64, 512], F32)
            kr = sb2("kr", [64, 512], F32)
            sqn = sb2("sqn", [128, 512], BF16)
            sqr = sb2("sqr", [64, 512], BF16)
            sqkr = sb2("sqkr", [64, 512], BF16)
            on_ = [sb2("on%d" % i, [128, 512], BF16) for i in range(2)]
            or_ = [sb2("or%d" % i, [64, 512], BF16) for i in range(2)]
            vo = [sb2("vo%d" % i, [128, 1024], BF16) for i in range(2)]
            (B_cqT, B_sqb, B_cqn, B_ckvT, B_ckvn, B_krA, B_krB, B_rst, B_rsth, B_posi, B_ang, B_nfi, B_nff, B_th,
             B_thc, B_msk, B_cos, B_sin, B_ra, B_rb, B_kr, B_sqn, B_sqr, B_sqkr) = [Buf() for _ in range(24)]
            B_on = [Buf(), Buf()]
            B_or = [Buf(), Buf()]
            B_vo = [Buf(), Buf()]
            p_bc = pp2("p_bc", [128, 512])
            p_n = pp2("p_n", [128, 512])
            p_a = pp2("p_a", [128, 512])
            p_b = pp2("p_b", [128, 512])
            p_s = pp2("p_s", [128, 512])
            p_v = [pp2("p_v%d" % i, [128, 512]) for i in range(2)]
            B_pbc, B_pn, B_pa, B_pb, B_ps = [Buf() for _ in range(5)]
            B_pv = [Buf(), Buf()]
            nout = 0
            nvo = 0
            for blk in range(NB):
                c0 = blk * 512
                cs = slice(c0, c0 + 512)
                P.op("sp", lambda e, cs=cs: e.dma_start(
                    out=cqT[:], in_=projT[0:512, cs].rearrange("(k p) t -> p k t", p=128)),
                    r=[B_proj], w=[B_cqT], dma=True)
                P.op("sp", lambda e, cs=cs: e.dma_start(
                    out=ckvT[:], in_=projT[512:768, cs].rearrange("(k p) t -> p k t", p=128)),
                    r=[B_proj], w=[B_ckvT], dma=True)
                P.op("sp", lambda e, cs=cs: e.dma_start(out=krA[:], in_=projT[46 * 128:46 * 128 + 64, cs]),
                     r=[B_proj], w=[B_krA], dma=True)
                P.op("sp", lambda e, cs=cs: e.dma_start(out=krB[:], in_=projT[46 * 128 + 64:47 * 128, cs]),
                     r=[B_proj], w=[B_krB], dma=True)
                P.op("sp", lambda e, cs=cs: e.dma_start(out=posi[:], in_=pos64[:, cs]), w=[B_posi], dma=True)
                P.op("dve", lambda e: e.tensor_copy(out=ang[:], in_=posi[:]), r=[B_posi], w=[B_ang])
                P.op("dve", lambda e: e.tensor_scalar(out=ang[:], in0=ang[:], scalar1=invf[:, 0:1], scalar2=None,
                                                      op0=ALU.mult), r=[B_ang, B_invf], w=[B_ang])
                P.op("dve", lambda e: e.tensor_scalar(out=nfi[:], in0=ang[:], scalar1=1.0 / (2 * PI), scalar2=None,
                                                      op0=ALU.mult), r=[B_ang], w=[B_nfi])
                P.op("dve", lambda e: e.tensor_copy(out=nff[:], in_=nfi[:]), r=[B_nfi], w=[B_nff])
                P.op("dve", lambda e: e.scalar_tensor_tensor(out=th[:], in0=nff[:], scalar=-C1, in1=ang[:],
                                                             op0=ALU.mult, op1=ALU.add), r=[B_nff, B_ang], w=[B_th])
                P.op("dve", lambda e: e.scalar_tensor_tensor(out=th[:], in0=nff[:], scalar=-C2, in1=th[:],
                                                             op0=ALU.mult, op1=ALU.add), r=[B_nff, B_th], w=[B_th])

                def wrap(t_ap, B_t):
                    P.op("dve", lambda e: e.tensor_single_scalar(out=msk[:], in_=t_ap, scalar=PI, op=ALU.is_gt),
                         r=[B_t], w=[B_msk])
                    P.op("dve", lambda e: e.scalar_tensor_tensor(out=t_ap, in0=msk[:], scalar=-2 * PI, in1=t_ap,
                                                                 op0=ALU.mult, op1=ALU.add), r=[B_msk, B_t], w=[B_t])
                    P.op("dve", lambda e: e.tensor_single_scalar(out=msk[:], in_=t_ap, scalar=-PI, op=ALU.is_lt),
                         r=[B_t], w=[B_msk])
                    P.op("dve", lambda e: e.scalar_tensor_tensor(out=t_ap, in0=msk[:], scalar=2 * PI, in1=t_ap,
                                                                 op0=ALU.mult, op1=ALU.add), r=[B_msk, B_t], w=[B_t])
                    P.op("dve", lambda e: e.tensor_scalar(out=t_ap, in0=t_ap, scalar1=-PI, scalar2=PI,
                                                          op0=ALU.max, op1=ALU.min), r=[B_t], w=[B_t])

                wrap(th[:], B_th)
                P.op("dve", lambda e: e.tensor_scalar(out=thc[:], in0=th[:], scalar1=PI / 2, scalar2=None,
                                                      op0=ALU.add), r=[B_th], w=[B_thc])
                wrap(thc[:], B_thc)
                P.op("act", lambda e: e.activation(out=sinT[:], in_=th[:], func=AF.Sin), r=[B_th], w=[B_sin])
                P.op("act", lambda e: e.activation(out=cosT[:], in_=thc[:], func=AF.Sin), r=[B_thc], w=[B_cos])

                def lat_norm(src, B_src, dst, B_dst, nk):
                    P.op("act", lambda e: e.activation(out=sqb[:, 0:nk, :], in_=src[:], func=AF.Square),
                         r=[B_src], w=[B_sqb])
                    for kc in range(nk):
                        P.op("pe", lambda e, kc=kc: e.matmul(p_bc[:], lhsT=ones_b[:], rhs=sqb[:, kc, :],
                                                             start=(kc == 0), stop=(kc == nk - 1)),
                             r=[B_ones, B_sqb], w=[B_pbc], inc=(kc == nk - 1))
                    rstd_from_psum(p_bc[:], B_pbc, rst[:], B_rst, 128.0 * nk)
                    P.op("dve", lambda e: e.tensor_tensor(out=dst[:], in0=src[:],
                                                          in1=rst[:].unsqueeze(1).broadcast_to([128, nk, 512]),
                                                          op=ALU.mult), r=[B_src, B_rst], w=[B_dst])

                lat_norm(cqT, B_cqT, cqn, B_cqn, 4)
                lat_norm(ckvT, B_ckvT, ckvn, B_ckvn, 2)

                P.op("dve", lambda e: e.scalar_tensor_tensor(out=ra[:], in0=krA[:], scalar=gv[0:64, 4:5], in1=cosT[:],
                                                             op0=ALU.mult, op1=ALU.mult),
                     r=[B_krA, B_gv, B_cos], w=[B_ra])
                P.op("dve", lambda e: e.scalar_tensor_tensor(out=rb_[:], in0=krB[:], scalar=gv[0:64, 5:6], in1=sinT[:],
                                                             op0=ALU.mult, op1=ALU.mult),
                     r=[B_krB, B_gv, B_sin], w=[B_rb])
                P.op("dve", lambda e: e.tensor_tensor(out=kr[:], in0=ra[:], in1=rb_[:], op=ALU.add),
                     r=[B_ra, B_rb], w=[B_kr])
                P.op("act", lambda e: e.activation(out=sqkr[:], in_=kr[:], func=AF.Square), r=[B_kr], w=[B_sqkr])

                for h in range(8):
                    for kc in range(4):
                        P.op("pe", lambda e, kc=kc, h=h: e.matmul(p_n[:], lhsT=wuqb[:, kc, h * 256:h * 256 + 128],
                                                                  rhs=cqn[:, kc, :], start=(kc == 0), stop=(kc == 3)),
                             r=[B_wuqb, B_cqn], w=[B_pn], inc=(kc == 3))
                    for kc in range(4):
                        P.op("pe", lambda e, kc=kc, h=h: e.matmul(p_a[0:64, :], lhsT=wuqb[:, kc, h * 256 + 128:h * 256 + 192],
                                                                  rhs=cqn[:, kc, :], start=(kc == 0), stop=(kc == 3)),
                             r=[B_wuqb, B_cqn], w=[B_pa], inc=(kc == 3))
                    for kc in range(4):
                        P.op("pe", lambda e, kc=kc, h=h: e.matmul(p_b[0:64, :], lhsT=wuqb[:, kc, h * 256 + 192:h * 256 + 256],
                                                                  rhs=cqn[:, kc, :], start=(kc == 0), stop=(kc == 3)),
                             r=[B_wuqb, B_cqn], w=[B_pb], inc=(kc == 3))
                    P.op("dve", lambda e: e.scalar_tensor_tensor(out=ra[:], in0=p_a[0:64, :], scalar=gv[0:64, 1:2],
                                                                 in1=cosT[:], op0=ALU.mult, op1=ALU.mult),
                         r=[B_pa, B_gv, B_cos], w=[B_ra])
                    P.op("dve", lambda e: e.scalar_tensor_tensor(out=rb_[:], in0=p_b[0:64, :], scalar=gv[0:64, 2:3],
                                                                 in1=sinT[:], op0=ALU.mult, op1=ALU.mult),
                         r=[B_pb, B_gv, B_sin], w=[B_rb])
                    P.op("dve", lambda e: e.tensor_tensor(out=ra[:], in0=ra[:], in1=rb_[:], op=ALU.add),
                         r=[B_ra, B_rb], w=[B_ra])
                    P.op("act", lambda e: e.activation(out=sqn[:], in_=p_n[:], func=AF.Square, scale=gv[:, 0:1]),
                         r=[B_pn, B_gv], w=[B_sqn])
                    P.op("act", lambda e: e.activation(out=sqr[:], in_=ra[:], func=AF.Square), r=[B_ra], w=[B_sqr])
                    P.op("pe", lambda e: e.matmul(p_s[:], lhsT=ones_b[:], rhs=sqn[:], start=True, stop=False),
                         r=[B_ones, B_sqn], w=[B_ps], inc=False)
                    P.op("pe", lambda e: e.matmul(p_s[:], lhsT=ones_b[0:64, :], rhs=sqr[:], start=False, stop=True),
                         r=[B_ones, B_sqr], w=[B_ps])
                    rstd_from_psum(p_s[:], B_ps, rsth[:], B_rsth, 192.0, extra_bias=-0.5 * LN192)
                    k = nout % 2
                    nout += 1
                    P.op("dve", lambda e, k=k: e.scalar_tensor_tensor(out=on_[k][:], in0=p_n[:], scalar=gv[:, 0:1],
                                                                      in1=rsth[:], op0=ALU.mult, op1=ALU.mult),
                         r=[B_pn, B_gv, B_rsth], w=[B_on[k]])
                    P.op("dve", lambda e, k=k: e.tensor_tensor(out=or_[k][:], in0=ra[:], in1=rsth[0:64, :], op=ALU.mult),
                         r=[B_ra, B_rsth], w=[B_or[k]])
                    P.op("pool", lambda e, k=k, h=h, cs=cs: e.dma_start(out=QT[h, 0:128, cs], in_=on_[k][:]),
                         r=[B_on[k]], w=[B_QT], dma=True)
                    P.op("pool", lambda e, k=k, h=h, cs=cs: e.dma_start(out=QT[h, 128:192, cs], in_=or_[k][:]),
                         r=[B_or[k]], w=[B_QT], dma=True)
                    for kc in range(2):
                        P.op("pe", lambda e, kc=kc, h=h: e.matmul(p_n[:], lhsT=wukvb[:, kc, h * 128:(h + 1) * 128],
                                                                  rhs=ckvn[:, kc, :], start=(kc == 0), stop=(kc == 1)),
                             r=[B_wukvb, B_ckvn], w=[B_pn], inc=(kc == 1))
                    P.op("act", lambda e: e.activation(out=sqn[:], in_=p_n[:], func=AF.Square, scale=gv[:, 3:4]),
                         r=[B_pn, B_gv], w=[B_sqn])
                    P.op("pe", lambda e: e.matmul(p_s[:], lhsT=ones_b[:], rhs=sqn[:], start=True, stop=False),
                         r=[B_ones, B_sqn], w=[B_ps], inc=False)
                    P.op("pe", lambda e: e.matmul(p_s[:], lhsT=ones_b[0:64, :], rhs=sqkr[:], start=False, stop=True),
                         r=[B_ones, B_sqkr], w=[B_ps])
                    rstd_from_psum(p_s[:], B_ps, rsth[:], B_rsth, 192.0)
                    k = nout % 2
                    nout += 1
                    P.op("dve", lambda e, k=k: e.scalar_tensor_tensor(out=on_[k][:], in0=p_n[:], scalar=gv[:, 3:4],
                                                                      in1=rsth[:], op0=ALU.mult, op1=ALU.mult),
                         r=[B_pn, B_gv, B_rsth], w=[B_on[k]])
                    P.op("dve", lambda e, k=k: e.tensor_tensor(out=or_[k][:], in0=kr[:], in1=rsth[0:64, :], op=ALU.mult),
                         r=[B_kr, B_rsth], w=[B_or[k]])
                    P.op("pool", lambda e, k=k, h=h, cs=cs: e.dma_start(out=KT[h, 0:128, cs], in_=on_[k][:]),
                         r=[B_on[k]], w=[B_KT], dma=True)
                    P.op("pool", lambda e, k=k, h=h, cs=cs: e.dma_start(out=KT[h, 128:192, cs], in_=or_[k][:]),
                         r=[B_or[k]], w=[B_KT], dma=True)
                for tt in range(4):
                    k = nvo % 2
                    nvo += 1
                    for hh in range(2):
                        for kc in range(2):
                            P.op("pe", lambda e, kc=kc, hh=hh, tt=tt: e.matmul(
                                p_v[hh][:], lhsT=ckvn[:, kc, tt * 128:(tt + 1) * 128],
                                rhs=wukvb[:, kc, 1024 + hh * 512:1024 + (hh + 1) * 512],
                                start=(kc == 0), stop=(kc == 1)),
                                r=[B_ckvn, B_wukvb], w=[B_pv[hh]], inc=(kc == 1))
                        P.op("act", lambda e, k=k, hh=hh: e.activation(out=vo[k][:, hh * 512:(hh + 1) * 512],
                                                                       in_=p_v[hh][:], func=AF.Copy),
                             r=[B_pv[hh]], w=[B_vo[k]])
                    P.op("pool", lambda e, k=k, tt=tt, c0=c0: e.dma_start(
                        out=Vs[c0 + tt * 128:c0 + (tt + 1) * 128, :], in_=vo[k][:]),
                        r=[B_vo[k]], w=[B_Vs], dma=True)

        P.barrier()
        with ExitStack() as ps:
            def sb3(name, shape, dt):
                return ps.enter_context(nc.sbuf_tensor(name, shape, dt))

            def pp3(name, shape, dt=F32):
                return ps.enter_context(nc.psum_tensor(name, shape, dt))

            ktn = [sb3("ktn%d" % i, [128, T], BF16) for i in range(2)]
            ktr = [sb3("ktr%d" % i, [64, T], BF16) for i in range(2)]
            vt = [sb3("vt%d" % i, [128, NT, 128], BF16) for i in range(2)]
            qn = [sb3("qn%d" % i, [128, 512], BF16) for i in range(2)]
            qr = [sb3("qr%d" % i, [64, 512], BF16) for i in range(2)]
            gt = [sb3("gt%d" % i, [128, 512], BF16) for i in range(2)]
            sg = sb3("sg", [128, 512], F32)
            pt = [sb3("pt%d" % i, [128, 512], BF16) for i in range(4)]
            mski = sb3("mski", [128, 512], I32)
            cmask = sb3("cmask", [128, 4, 512], BF16)
            rl = sb3("rl", [128, 512], F32)
            of = sb3("of", [128, 512], F32)
            ob = [sb3("ob%d" % i, [128, 512], BF16) for i in range(2)]
            B_kt = [Buf(), Buf()]
            B_vt = [Buf(), Buf()]
            B_q = [Buf(), Buf()]
            B_gt = [Buf(), Buf()]
            B_sg, B_mski, B_cmask, B_rl, B_of = [Buf() for _ in range(5)]
            B_pt = [Buf() for _ in range(4)]
            B_ob = [Buf(), Buf()]
            pS = [pp3("pS%d" % i, [128, 512]) for i in range(4)]
            pO = [pp3("pO%d" % i, [128, 512]) for i in range(2)]
            pL = [pp3("pL%d" % i, [128, 512]) for i in range(2)]
            B_pS = [Buf() for _ in range(4)]
            B_pO = [Buf(), Buf()]
            B_pL = [Buf(), Buf()]
            for dd in range(4):
                P.op("pool", lambda e, dd=dd: e.iota(mski[:], pattern=[[1, 512]], base=-128 * dd, channel_multiplier=-1),
                     w=[B_mski])
                P.op("dve", lambda e, dd=dd: e.tensor_single_scalar(out=cmask[:, dd, :], in_=mski[:], scalar=0,
                                                                    op=ALU.is_ge), r=[B_mski], w=[B_cmask])
            lacc = [sb3("lacc%d" % i, [128, 512], F32) for i in range(2)]
            onesf3 = sb3("onesf3", [128, 128], F32)
            B_lacc = [Buf(), Buf()]
            B_onesf3 = Buf()
            P.op("pool", lambda e: e.memset(onesf3[:], 1.0), w=[B_onesf3])
            pairs = []
            blocks = []
            for h in range(8):
                for j in range(NB):
                    bidx = len(blocks)
                    blocks.append((h, j))
                    for i in range(4 * j + 4):
                        pairs.append((bidx, i))

            def emit_S(idx):
                bidx, i = pairs[idx]
                h, j = blocks[bidx]
                hk = h % 2
                qk = bidx % 2
                sk = idx % 4
                cs = slice(j * 512, (j + 1) * 512)
                if i == 0:
                    if j == 0:
                        P.op("sp", lambda e: e.dma_start(out=ktn[hk][:], in_=KT[h, 0:128, :]),
                             r=[B_KT], w=[B_kt[hk]], dma=True)
                        P.op("sp", lambda e: e.dma_start(out=ktr[hk][:], in_=KT[h, 128:192, :]),
                             r=[B_KT], w=[B_kt[hk]], dma=True)
                        P.op("sp", lambda e: e.dma_start(
                            out=vt[hk][:], in_=Vs[:, h * 128:(h + 1) * 128].rearrange("(n p) d -> p n d", p=128)),
                            r=[B_Vs], w=[B_vt[hk]], dma=True)
                    P.op("sp", lambda e: e.dma_start(out=qn[qk][:], in_=QT[h, 0:128, cs]),
                         r=[B_QT], w=[B_q[qk]], dma=True)
                    P.op("sp", lambda e: e.dma_start(out=qr[qk][:], in_=QT[h, 128:192, cs]),
                         r=[B_QT], w=[B_q[qk]], dma=True)
                    P.op("sp", lambda e: e.dma_start(out=gt[qk][:], in_=projT[(6 + h) * 128:(7 + h) * 128, cs]),
                         r=[B_proj], w=[B_gt[qk]], dma=True)
                P.op("pe", lambda e: e.matmul(pS[sk][:], lhsT=ktn[hk][:, i * 128:(i + 1) * 128], rhs=qn[qk][:],
                                              start=True, stop=False),
                     r=[B_kt[hk], B_q[qk]], w=[B_pS[sk]], inc=False)
                P.op("pe", lambda e: e.matmul(pS[sk][:], lhsT=ktr[hk][:, i * 128:(i + 1) * 128], rhs=qr[qk][:],
                                              start=False, stop=True),
                     r=[B_kt[hk], B_q[qk]], w=[B_pS[sk]])

            def emit_exp(idx):
                bidx, i = pairs[idx]
                h, j = blocks[bidx]
                sk = idx % 4
                qk = bidx % 2
                P.op("act", lambda e: e.activation(out=pt[sk][:], in_=pS[sk][:], func=AF.Exp),
                     r=[B_pS[sk]], w=[B_pt[sk]])
                if i >= 4 * j:
                    P.op("dve", lambda e: e.tensor_tensor(out=pt[sk][:], in0=pt[sk][:], in1=cmask[:, i - 4 * j, :],
                                                          op=ALU.mult), r=[B_pt[sk], B_cmask], w=[B_pt[sk]])

            def emit_PV(idx):
                bidx, i = pairs[idx]
                h, j = blocks[bidx]
                hk = h % 2
                qk = bidx % 2
                sk = idx % 4
                nk = 4 * j + 4
                cs = slice(j * 512, (j + 1) * 512)
                P.op("pe", lambda e: e.matmul(pO[qk][:], lhsT=vt[hk][:, i, :], rhs=pt[sk][:], start=(i == 0),
                                              stop=(i == nk - 1)),
                     r=[B_vt[hk], B_pt[sk]], w=[B_pO[qk]], inc=False)
                P.op("pe", lambda e: e.matmul(pL[qk][:], lhsT=ones_b[:], rhs=pt[sk][:], start=(i == 0),
                                              stop=(i == nk - 1)),
                     r=[B_ones, B_pt[sk]], w=[B_pL[qk]])
                if i == nk - 1:
                    P.op("dve", lambda e: e.reciprocal(out=rl[:], in_=pL[qk][:]), r=[B_pL[qk]], w=[B_rl])
                    P.op("act", lambda e: e.activation(out=sg[:], in_=gt[qk][:], func=AF.Silu),
                         r=[B_gt[qk]], w=[B_sg])
                    P.op("dve", lambda e: e.tensor_tensor(out=of[:], in0=pO[qk][:], in1=rl[:], op=ALU.mult),
                         r=[B_pO[qk], B_rl], w=[B_of])
                    P.op("dve", lambda e: e.tensor_tensor(out=ob[qk][:], in0=of[:], in1=sg[:], op=ALU.mult),
                         r=[B_of, B_sg], w=[B_ob[qk]])
                    P.op("pool", lambda e: e.dma_start(out=mixT[h * 128:(h + 1) * 128, cs], in_=ob[qk][:]),
                         r=[B_ob[qk]], w=[B_mix], dma=True)

            NP_ = len(pairs)
            for q0 in range(min(3, NP_)):
                emit_S(q0)
            for idx in range(NP_):
                emit_exp(idx)
                if idx + 3 < NP_:
                    emit_S(idx + 3)
                emit_PV(idx)
        P.barrier()
        NC = T // 64
        with ExitStack() as ps:
            def sb4(name, shape, dt):
                return ps.enter_context(nc.sbuf_tensor(name, shape, dt))

            def pp4(name, shape, dt=F32):
                return ps.enter_context(nc.psum_tensor(name, shape, dt))

            def bc3(ap2, n):
                return ap2.unsqueeze(2).broadcast_to([ap2.shape[0], ap2.shape[1], n])

            def bch(ap2, nh=8):
                return ap2.unsqueeze(1).broadcast_to([ap2.shape[0], nh, ap2.shape[1]])

            cw_s = sb4("cw_s", [128, 24, 4], F32)
            al_s = sb4("al_s", [64, 8], F32)
            dt_s = sb4("dt_s", [64, 8], F32)
            g_all = sb4("g_all", [64, NC, 8], F32)
            be_all = sb4("be_all", [64, NC, 8], F32)
            ioj = sb4("ioj", [64, 64], I32)
            iof = sb4("iof", [64, 64], F32)
            tri = sb4("tri", [64, 64], F32)
            id64 = sb4("id64", [64, 64], F32)
            ng_su = sb4("ng_su", [64, 64], F32)
            ng_iu = sb4("ng_iu", [64, 64], F32)
            ng_sl = sb4("ng_sl", [64, 64], F32)
            ones_f = sb4("ones_f", [64, 128], F32)
            S = sb4("S", [128, 8, 128], F32)
            S_bf = sb4("S_bf", [128, 8, 128], BF16)
            dg = sb4("dg", [128, 96, 128], BF16)
            B_dg = Buf()
            ts_ = ExitStack()
            t1 = ts_.enter_context(nc.sbuf_tensor("t1", [64, NC, 8], F32))
            t2 = ts_.enter_context(nc.sbuf_tensor("t2", [64, NC, 8], F32))
            (B_cw, B_al, B_dt, B_g, B_be, B_t1, B_t2, B_ioj, B_iof, B_tri, B_id64, B_ngsu, B_ngiu, B_ngsl, B_onesf,
             B_S, B_Sbf) = [Buf() for _ in range(17)]
            P.op("sp", lambda e: e.dma_start(out=cw_s[:], in_=cw[:, :, :]), w=[B_cw], dma=True)
            P.op("sp", lambda e: e.dma_start(out=al_s[:], in_=alog64[:, :]), w=[B_al], dma=True)
            P.op("sp", lambda e: e.dma_start(out=dt_s[:], in_=dtb64[:, :]), w=[B_dt], dma=True)
            P.op("dve", lambda e: e.tensor_tensor(out=t1[:], in0=gabtm[:, :, 0:8],
                                                  in1=dt_s[:].unsqueeze(1).broadcast_to([64, NC, 8]), op=ALU.add),
                 r=[B_gab, B_dt], w=[B_t1])
            P.op("act", lambda e: e.activation(out=t2[:], in_=t1[:], func=AF.Abs), r=[B_t1], w=[B_t2])
            P.op("act", lambda e: e.activation(out=t2[:], in_=t2[:], func=AF.Exp, scale=-1.0), r=[B_t2], w=[B_t2])
            P.op("act", lambda e: e.activation(out=t2[:], in_=t2[:], func=AF.Ln, bias=1.0), r=[B_t2], w=[B_t2])
            P.op("dve", lambda e: e.tensor_scalar(out=t1[:], in0=t1[:], scalar1=0.0, scalar2=None, op0=ALU.max),
                 r=[B_t1], w=[B_t1])
            P.op("dve", lambda e: e.tensor_tensor(out=t1[:], in0=t1[:], in1=t2[:], op=ALU.add),
                 r=[B_t1, B_t2], w=[B_t1])
            P.op("act", lambda e: e.activation(out=al_s[:], in_=al_s[:], func=AF.Exp), r=[B_al], w=[B_al])
            P.op("dve", lambda e: e.tensor_scalar(out=al_s[:], in0=al_s[:], scalar1=-1.0, scalar2=None, op0=ALU.mult),
                 r=[B_al], w=[B_al])
            P.op("dve", lambda e: e.tensor_tensor(out=g_all[:], in0=t1[:],
                                                  in1=al_s[:].unsqueeze(1).broadcast_to([64, NC, 8]), op=ALU.mult),
                 r=[B_t1, B_al], w=[B_g])
            P.op("act", lambda e: e.activation(out=t2[:], in_=gabtm[:, :, 8:16], func=AF.Exp, scale=-1.0),
                 r=[B_gab], w=[B_t2])
            P.op("dve", lambda e: e.tensor_scalar(out=t2[:], in0=t2[:], scalar1=1.0, scalar2=None, op0=ALU.add),
                 r=[B_t2], w=[B_t2])
            P.op("dve", lambda e: e.reciprocal(out=be_all[:], in_=t2[:]), r=[B_t2], w=[B_be])
            P.op("pool", lambda e: e.iota(ioj[:], pattern=[[1, 64]], base=0, channel_multiplier=-1), w=[B_ioj])
            P.op("dve", lambda e: e.tensor_copy(out=iof[:], in_=ioj[:]), r=[B_ioj], w=[B_iof])
            P.op("dve", lambda e: e.tensor_single_scalar(out=tri[:], in_=iof[:], scalar=0.0, op=ALU.is_ge),
                 r=[B_iof], w=[B_tri])
            P.op("dve", lambda e: e.tensor_single_scalar(out=id64[:], in_=iof[:], scalar=0.0, op=ALU.is_equal),
                 r=[B_iof], w=[B_id64])
            for dst, Bd, thr, cmp in ((ng_su, B_ngsu, 1.0, ALU.is_ge), (ng_iu, B_ngiu, 0.0, ALU.is_ge),
                                      (ng_sl, B_ngsl, -1.0, ALU.is_le)):
                P.op("dve", lambda e, dst=dst, thr=thr, cmp=cmp: e.tensor_single_scalar(
                    out=dst[:], in_=iof[:], scalar=thr, op=cmp), r=[B_iof], w=[Bd])
                P.op("dve", lambda e, dst=dst: e.tensor_scalar(out=dst[:], in0=dst[:], scalar1=-1.0, scalar2=30000.0,
                                                               op0=ALU.add, op1=ALU.mult), r=[Bd], w=[Bd])
            P.op("pool", lambda e: e.memset(ones_f[:], 1.0), w=[B_onesf])
            P.op("pool", lambda e: e.memset(S[:], 0.0), w=[B_S])
            P.op("pool", lambda e: e.memset(S_bf[:], 0.0), w=[B_Sbf])
            for gi in range(24):
                for j in range(4):
                    P.op("dve", lambda e, gi=gi, j=j: e.tensor_scalar(out=dg[:, gi * 4 + j, :], in0=ident[:],
                                                                      scalar1=cw_s[:, gi, j:j + 1], scalar2=None,
                                                                      op0=ALU.mult), r=[B_ident, B_cw], w=[B_dg])
            ts_.close()
            P.barrier()


            GB = 256
            NGB = T // GB
            CPB = GB // 64
            xin = [sb4("xin%d" % i, [128, GB + 3], BF16) for i in range(8)]
            ysl16 = sb4("ysl16", [128, 16, GB], BF16)
            B_ysl16 = Buf()
            sq4 = sb4("sq4", [128, GB], BF16)
            rs4 = sb4("rs4", [128, GB], F32)
            vsT = sb4("vsT", [128, 8, GB], BF16)
            qnT = [sb4("qnT%d" % i, [128, 8, GB], BF16) for i in range(2)]
            knT = [sb4("knT%d" % i, [128, 8, GB], BF16) for i in range(2)]
            ktm = [sb4("ktm%d" % i, [64, CPB, 8, 128], BF16) for i in range(2)]
            vtm = [sb4("vtm%d" % i, [64, CPB, 8, 128], BF16) for i in range(2)]
            oTb = [sb4("oTb%d" % i, [128, 8, GB], BF16) for i in range(2)]
            B_xin = [Buf() for _ in range(8)]
            B_sq4, B_rs4, B_vsT = Buf(), Buf(), Buf()
            B_qnT = [Buf(), Buf()]
            B_knT = [Buf(), Buf()]
            B_ktm = [Buf(), Buf()]
            B_vtm = [Buf(), Buf()]
            B_oTb = [Buf(), Buf()]

            def two(name, shape, dt, n=2):
                return [sb4("%s%d" % (name, i), shape, dt) for i in range(n)], [Buf() for _ in range(n)]

            gTri, B_gTri = two("gTri", [64, 8, 64], F32)
            gcb, B_gcb = two("gcb", [128, 8, 64], F32)
            gct, B_gct = two("gct", [64, 8], F32)
            dA, B_dA = two("dA", [64, 8, 64], F32)
            dD, B_dD = two("dD", [64, 8, 64], F32)
            AmA, B_AmA = two("AmA", [64, 8, 64], BF16)
            AmB, B_AmB = two("AmB", [64, 8, 64], BF16)
            AtA, B_AtA = two("AtA", [64, 8, 64], BF16)
            AtB, B_AtB = two("AtB", [64, 8, 64], BF16)
            TtA, B_TtA = two("TtA", [64, 8, 64], BF16)
            TtB, B_TtB = two("TtB", [64, 8, 64], BF16)
            vb, B_vb = two("vb", [64, 8, 128], BF16)
            kbg, B_kbg = two("kbg", [64, 8, 128], BF16)
            s8a, B_s8a = two("s8a", [64, 8], F32)
            s8b, B_s8b = two("s8b", [64, 8], F32)
            egc, B_egc = two("egc", [128, 8, 64], F32, 3)
            attnT, B_attnT = two("attnT", [64, 8, 64], BF16, 3)
            kd, B_kd = two("kd", [64, 8, 128], BF16, 3)
            u_sb, B_u = two("u_sb", [64, 8, 128], F32, 3)
            wT_sb, B_wT = two("wT_sb", [128, 8, 64], BF16, 3)
            qdT, B_qdT = two("qdT", [128, 8, 64], BF16, 3)
            vnew = sb4("vnew", [64, 8, 128], BF16)
            B_vnew = Buf()
            gg = sb4("gg", [128, 8, GB], BF16)
            sgg = sb4("sgg", [128, 8, GB], BF16)
            o1 = sb4("o1", [128, GB], F32)
            o2 = [sb4("o2%d" % i, [128, GB], BF16) for i in range(2)]
            B_gg, B_sgg, B_o1 = Buf(), Buf(), Buf()
            sq4e = sb4("sq4e", [128, GB], BF16)
            rs4e = sb4("rs4e", [128, GB], F32)
            B_sq4e, B_rs4e = Buf(), Buf()
            B_o2 = [Buf(), Buf()]
            bk = [pp4("bk%d" % i, [128, 512]) for i in range(7)]
            B_bk = [Buf() for _ in range(7)]
            B_half = [B_bk[6], B_bk[6]]
            ptb = pp4("ptb", [128, 8, 128], BF16)
            B_ptb = Buf()
            print("GDN sbuf bytes remaining", nc.sbuf_bytes_remaining)
            pro_done = [False] * NGB
            epi_done = [False] * NGB
            prep_done = [False] * NC
            scan_done = [False] * NC
            cnt4 = {"x": 0, "o2": 0}

            def gen_pro(blk):
                bp = blk % 2
                c0 = blk * GB
                for kind in range(3):
                    for h in range(8):
                        row0 = (14 + kind * 8 + h) * 128
                        if blk == 0:
                            P.op("pool", lambda e, h=h: e.memset(xin[h][:, 0:3], 0.0), w=[B_xin[h]])
                            P.op("sp", lambda e, h=h, row0=row0: e.dma_start(out=xin[h][:, 3:GB + 3],
                                                                             in_=projT[row0:row0 + 128, 0:GB]),
                                 r=[B_proj], w=[B_xin[h]], dma=True)
                        else:
                            P.op("sp", lambda e, h=h, row0=row0: e.dma_start(out=xin[h][:],
                                                                             in_=projT[row0:row0 + 128, c0 - 3:c0 + GB]),
                                 r=[B_proj], w=[B_xin[h]], dma=True)
                    yield
                    for h in range(8):
                        gi = kind * 8 + h
                        hf = h % 2
                        cvh = bk[6][:, hf * 256:hf * 256 + GB]
                        for j in range(4):
                            P.op("pe", lambda e, j=j: e.matmul(cvh, lhsT=dg[:, gi * 4 + j, :], rhs=xin[h][:, j:j + GB],
                                                               start=(j == 0), stop=(j == 3)),
                                 r=[B_dg, B_xin[h]], w=[B_half[hf]], inc=(j == 3))
                        if kind == 2:
                            P.op("act", lambda e: e.activation(out=vsT[:, h, :], in_=cvh, func=AF.Silu),
                                 r=[B_half[hf]], w=[B_vsT])
                        else:
                            P.op("act", lambda e: e.activation(out=ysl16[:, gi, :], in_=cvh, func=AF.Silu),
                                 r=[B_half[hf]], w=[B_ysl16])
                    yield
                for kind in range(2):
                    for h in range(8):
                        gi = kind * 8 + h
                        P.op("act", lambda e: e.activation(out=sq4[:], in_=ysl16[:, gi, :], func=AF.Square),
                             r=[B_ysl16], w=[B_sq4])
                        yield
                        P.op("pe", lambda e: e.matmul(bk[6][:, 0:GB], lhsT=ones_b[:], rhs=sq4[:], start=True, stop=True),
                             r=[B_ones, B_sq4], w=[B_half[0]])
                        P.op("act", lambda e: e.activation(out=rs4[:], in_=bk[6][:, 0:GB], func=AF.Ln, bias=EPS),
                             r=[B_half[0]], w=[B_rs4])
                        P.op("act", lambda e: e.activation(
                            out=rs4[:], in_=rs4[:], func=AF.Exp, scale=-0.5,
                            bias=(-0.5 * float(np.log(128.0)) if kind == 0 else 0.0)), r=[B_rs4], w=[B_rs4])
                        dstT, Bd = (qnT[bp], B_qnT[bp]) if kind == 0 else (knT[bp], B_knT[bp])
                        P.op("dve", lambda e: e.tensor_tensor(out=dstT[:, h, :], in0=ysl16[:, gi, :], in1=rs4[:], op=ALU.mult),
                             r=[B_ysl16, B_rs4], w=[Bd])
                        yield
                for srcT, Bs, dstm, Bdm in ((knT[bp], B_knT[bp], ktm[bp], B_ktm[bp]), (vsT, B_vsT, vtm[bp], B_vtm[bp])):
                    for ci in range(CPB):
                        for h in range(8):
                            P.op("pe", lambda e, h=h: e.transpose(
                                out=ptb[0:64, h, :], in_=srcT[:, h, ci * 64:(ci + 1) * 64], identity=ident[:]),
                                r=[Bs, B_ident], w=[B_ptb], inc=(h == 7))
                        P.op("act", lambda e: e.activation(out=dstm[:, ci, :, :], in_=ptb[0:64, :, :], func=AF.Copy),
                             r=[B_ptb], w=[Bdm])
                        yield
                pro_done[blk] = True

            def gen_prep(n):
                blk, ci = divmod(n, CPB)
                bp = blk % 2
                p = n % 2
                q4 = n % 3
                X0, X1, X2 = (bk[0], bk[1], bk[2]) if p == 0 else (bk[3], bk[4], bk[5])
                BX0, BX1, BX2 = (B_bk[0], B_bk[1], B_bk[2]) if p == 0 else (B_bk[3], B_bk[4], B_bk[5])
                cs = slice(ci * 64, ci * 64 + 64)
                gsl = g_all[:, n, :]
                bsl = be_all[:, n, :]
                KN, QN = knT[bp], qnT[bp]
                P.op("dve", lambda e: e.tensor_tensor(out=gTri[p][:], in0=bch(tri[:]), in1=bc3(gsl, 64), op=ALU.mult),
                     r=[B_tri, B_g], w=[B_gTri[p]])
                yield
                P.op("pe", lambda e: e.matmul(X0[:], lhsT=ones_f[:], rhs=gTri[p][:].rearrange("p a b -> p (a b)"),
                                              start=True, stop=True), r=[B_onesf, B_gTri[p]], w=[BX0])
                P.op("pe", lambda e: e.matmul(X1[0:64, 0:8], lhsT=tri[:], rhs=gsl, start=True, stop=True),
                     r=[B_tri, B_g], w=[BX1])
                for h in range(8):
                    P.op("pe", lambda e, h=h: e.matmul(X2[0:64, h * 64:(h + 1) * 64], lhsT=KN[:, h, cs], rhs=KN[:, h, cs],
                                                       start=True, stop=True), r=[B_knT[bp]], w=[BX2], inc=(h == 7))
                yield
                P.op("act", lambda e: e.activation(out=gcb[p][:].rearrange("p a b -> p (a b)"), in_=X0[:], func=AF.Copy),
                     r=[BX0], w=[B_gcb[p]])
                P.op("act", lambda e: e.activation(out=egc[q4][:].rearrange("p a b -> p (a b)"), in_=X0[:], func=AF.Exp),
                     r=[BX0], w=[B_egc[q4]])
                P.op("dve", lambda e: e.tensor_copy(out=gct[p][:], in_=X1[0:64, 0:8]), r=[BX1], w=[B_gct[p]])
                yield
                for h in range(8):
                    P.op("pe", lambda e, h=h: e.matmul(X0[0:64, h * 64:(h + 1) * 64], lhsT=KN[:, h, cs], rhs=QN[:, h, cs],
                                                       start=True, stop=True), r=[B_knT[bp], B_qnT[bp]], w=[BX0], inc=(h == 7))
                P.op("dve", lambda e: e.tensor_tensor(out=dA[p][:], in0=bc3(gct[p][:], 64), in1=gcb[p][0:64, :, :],
                                                      op=ALU.subtract), r=[B_gct[p], B_gcb[p]], w=[B_dA[p]])
                P.op("dve", lambda e: e.scalar_tensor_tensor(out=dA[p][:], in0=dA[p][:], scalar=0.0, in1=bch(ng_sl[:]),
                                                             op0=ALU.min, op1=ALU.add), r=[B_dA[p], B_ngsl], w=[B_dA[p]])
                P.op("dve", lambda e: e.tensor_tensor(out=dD[p][:], in0=gcb[p][0:64, :, :], in1=bc3(gct[p][:], 64),
                                                      op=ALU.subtract), r=[B_gct[p], B_gcb[p]], w=[B_dD[p]])
                P.op("dve", lambda e: e.scalar_tensor_tensor(out=dD[p][:], in0=dD[p][:], scalar=0.0, in1=bch(ng_iu[:]),
                                                             op0=ALU.min, op1=ALU.add), r=[B_dD[p], B_ngiu], w=[B_dD[p]])
                yield
                P.op("act", lambda e: e.activation(out=dA[p][:], in_=dA[p][:], func=AF.Exp), r=[B_dA[p]], w=[B_dA[p]])
                P.op("act", lambda e: e.activation(out=dD[p][:], in_=dD[p][:], func=AF.Exp), r=[B_dD[p]], w=[B_dD[p]])
                P.op("act", lambda e: e.activation(out=s8a[p][:], in_=gct[p][:], func=AF.Exp), r=[B_gct[p]], w=[B_s8a[p]])
                P.op("dve", lambda e: e.tensor_tensor(out=s8b[p][:], in0=gcb[p][0:64, :, 63], in1=gct[p][:], op=ALU.subtract),
                     r=[B_gcb[p], B_gct[p]], w=[B_s8b[p]])
                P.op("act", lambda e: e.activation(out=s8b[p][:], in_=s8b[p][:], func=AF.Exp), r=[B_s8b[p]], w=[B_s8b[p]])
                yield
                P.op("dve", lambda e: e.tensor_tensor(out=dA[p][:], in0=dA[p][:], in1=bc3(bsl, 64), op=ALU.mult),
                     r=[B_dA[p], B_be], w=[B_dA[p]])
                P.op("dve", lambda e: e.tensor_tensor(out=AmA[p][:], in0=X2[0:64, :].rearrange("p (a b) -> p a b", a=8),
                                                      in1=dA[p][:], op=ALU.mult), r=[BX2, B_dA[p]], w=[B_AmA[p]])
                P.op("dve", lambda e: e.tensor_tensor(out=attnT[q4][:], in0=X0[0:64, :].rearrange("p (a b) -> p a b", a=8),
                                                      in1=dD[p][:], op=ALU.mult), r=[BX0, B_dD[p]], w=[B_attnT[q4]])
                yield
                for h in range(8):
                    P.op("pe", lambda e, h=h: e.transpose(out=ptb[0:64, h, 0:64], in_=AmA[p][:, h, :],
                                                          identity=ident[0:64, 0:64]),
                         r=[B_AmA[p], B_ident], w=[B_ptb], inc=(h == 7))
                S8 = os.environ.get("GDN_S8", "atvs")
                if "a" in S8:
                    P.op("act", lambda e: e.activation(out=AtA[p][:], in_=ptb[0:64, :, 0:64], func=AF.Copy),
                         r=[B_ptb], w=[B_AtA[p]])
                if "t" in S8:
                    P.op("dve", lambda e: e.tensor_tensor(out=TtA[p][:], in0=bch(id64[:]), in1=AtA[p][:],
                                                          op=ALU.subtract), r=[B_id64, B_AtA[p]], w=[B_TtA[p]])
                if "v" in S8:
                    P.op("dve", lambda e: e.tensor_tensor(out=vb[p][:], in0=vtm[bp][:, ci, :, :], in1=bc3(bsl, 128), op=ALU.mult),
                         r=[B_vtm[bp], B_be], w=[B_vb[p]])
                if "s" in S8:
                    P.op("dve", lambda e: e.tensor_tensor(out=s8a[p][:], in0=s8a[p][:], in1=bsl, op=ALU.mult),
                         r=[B_s8a[p], B_be], w=[B_s8a[p]])
                yield
                Am = [(AmA[p], B_AmA[p]), (AmB[p], B_AmB[p])]
                At = [(AtA[p], B_AtA[p]), (AtB[p], B_AtB[p])]
                Tt = [(TtA[p], B_TtA[p]), (TtB[p], B_TtB[p])]
                cur = 0
                tcur = 0
                for lvl in range(5):
                    nxt = 1 - cur
                    (Ac, BAc), (Atc, BAtc) = Am[cur], At[cur]
                    (An, BAn), (Atn, BAtn) = Am[nxt], At[nxt]
                    (Tc, BTc), (Tn, BTn) = Tt[tcur], Tt[1 - tcur]
                    for h in range(8):
                        P.op("pe", lambda e, h=h: e.matmul(X0[0:64, h * 64:(h + 1) * 64], lhsT=Atc[:, h, :], rhs=Ac[:, h, :],
                                                           start=True, stop=True), r=[BAtc, BAc], w=[BX0], inc=(h == 7))
                    if lvl < 4:
                        for h in range(8):
                            P.op("pe", lambda e, h=h: e.matmul(X1[0:64, h * 64:(h + 1) * 64], lhsT=Ac[:, h, :], rhs=Atc[:, h, :],
                                                               start=True, stop=True), r=[BAtc, BAc], w=[BX1], inc=(h == 7))
                    yield
                    P.op("act", lambda e: e.activation(out=An[:].rearrange("p a b -> p (a b)"), in_=X0[0:64, :], func=AF.Copy),
                         r=[BX0], w=[BAn])
                    if lvl < 4:
                        P.op("dve", lambda e: e.tensor_copy(out=Atn[:].rearrange("p a b -> p (a b)"), in_=X1[0:64, :]),
                             r=[BX1], w=[BAtn])
                    if lvl == 0:
                        P.op("dve", lambda e: e.tensor_tensor(out=kbg[p][:], in0=ktm[bp][:, ci, :, :], in1=bc3(s8a[p][:], 128),
                                                              op=ALU.mult), r=[B_ktm[bp], B_s8a[p]], w=[B_kbg[p]])
                    if lvl == 1:
                        P.op("dve", lambda e: e.tensor_tensor(out=kd[q4][:], in0=ktm[bp][:, ci, :, :], in1=bc3(s8b[p][:], 128),
                                                              op=ALU.mult), r=[B_ktm[bp], B_s8b[p]], w=[B_kd[q4]])
                    if lvl == 2:
                        P.op("dve", lambda e: e.tensor_tensor(out=qdT[q4][:], in0=QN[:, :, cs], in1=egc[q4][:], op=ALU.mult),
                             r=[B_qnT[bp], B_egc[q4]], w=[B_qdT[q4]])
                    yield
                    for h in range(8):
                        P.op("pe", lambda e, h=h: e.matmul(X2[0:64, h * 64:(h + 1) * 64], lhsT=An[:, h, :], rhs=Tc[:, h, :],
                                                           start=True, stop=True), r=[BAn, BTc], w=[BX2], inc=(h == 7))
                    yield
                    P.op("dve", lambda e: e.tensor_tensor(out=Tn[:].rearrange("p a b -> p (a b)"), in0=X2[0:64, :],
                                                          in1=Tc[:].rearrange("p a b -> p (a b)"), op=ALU.add),
                         r=[BX2, BTc], w=[BTn])
                    yield
                    cur = nxt
                    tcur = 1 - tcur
                TT, B_TT = Tt[tcur]
                for h in range(8):
                    Xu, BXu = (X0, BX0) if h < 4 else (X1, BX1)
                    P.op("pe", lambda e, h=h, Xu=Xu: e.matmul(Xu[0:64, (h % 4) * 128:(h % 4 + 1) * 128], lhsT=TT[:, h, :],
                                                              rhs=vb[p][:, h, :], start=True, stop=True),
                         r=[B_TT, B_vb[p]], w=[BXu], inc=(h % 4 == 3))
                for h in range(8):
                    P.op("pe", lambda e, h=h: e.matmul(X2[:, h * 64:(h + 1) * 64], lhsT=kbg[p][:, h, :], rhs=TT[:, h, :],
                                                       start=True, stop=True), r=[B_kbg[p], B_TT], w=[BX2], inc=(h == 7))
                yield
                for hb, (Xu, BXu) in enumerate(((X0, BX0), (X1, BX1))):
                    P.op("act", lambda e, hb=hb, Xu=Xu: e.activation(
                        out=u_sb[q4][:, hb * 4:(hb + 1) * 4, :].rearrange("p a b -> p (a b)"), in_=Xu[0:64, :],
                        func=AF.Copy), r=[BXu], w=[B_u[q4]])
                P.op("act", lambda e: e.activation(out=wT_sb[q4][:].rearrange("p a b -> p (a b)"), in_=X2[:], func=AF.Copy),
                     r=[BX2], w=[B_wT[q4]])
                yield
                prep_done[n] = True

            def gen_scan(n):
                blk, ci = divmod(n, CPB)
                bp = blk % 2
                q4 = n % 3
                cs = slice(ci * 64, ci * 64 + 64)
                X = bk[6]
                BXs = [B_bk[6]]
                for hb in range(2):
                    hs = range(hb * 4, hb * 4 + 4)
                    for h in hs:
                        P.op("pe", lambda e, h=h: e.matmul(X[0:64, (h % 4) * 128:(h % 4 + 1) * 128], lhsT=wT_sb[q4][:, h, :],
                                                           rhs=S_bf[:, h, :], start=True, stop=True),
                             r=[B_wT[q4], B_Sbf], w=BXs, inc=(h % 4 == 3))
                    P.op("dve", lambda e: e.tensor_tensor(
                        out=vnew[:, hb * 4:(hb + 1) * 4, :].rearrange("p a b -> p (a b)"),
                        in0=u_sb[q4][:, hb * 4:(hb + 1) * 4, :].rearrange("p a b -> p (a b)"), in1=X[0:64, :],
                        op=ALU.subtract), r=[B_u[q4]] + BXs, w=[B_vnew])
                    yield
                    for h in hs:
                        P.op("pe", lambda e, h=h: e.matmul(X[:, (h % 4) * 64:(h % 4 + 1) * 64], lhsT=S_bf[:, h, :],
                                                           rhs=qdT[q4][:, h, :], start=True, stop=False),
                             r=[B_Sbf, B_qdT[q4]], w=BXs, inc=False)
                        P.op("pe", lambda e, h=h: e.matmul(X[:, (h % 4) * 64:(h % 4 + 1) * 64], lhsT=vnew[:, h, :],
                                                           rhs=attnT[q4][:, h, :], start=False, stop=True),
                             r=[B_vnew, B_attnT[q4]], w=BXs, inc=(h % 4 == 3))
                    P.op("act", lambda e: e.activation(out=oTb[bp][:, hb * 4:(hb + 1) * 4, cs],
                                                       in_=X[:, 0:256].rearrange("p (a b) -> p a b", a=4), func=AF.Copy),
                         r=BXs, w=[B_oTb[bp]])
                    yield
                    for h in hs:
                        P.op("pe", lambda e, h=h: e.matmul(X[:, (h % 4) * 128:(h % 4 + 1) * 128], lhsT=kd[q4][:, h, :],
                                                           rhs=vnew[:, h, :], start=True, stop=True),
                             r=[B_kd[q4], B_vnew], w=BXs, inc=(h % 4 == 3))
                    P.op("dve", lambda e: e.tensor_tensor(
                        out=S[:, hb * 4:(hb + 1) * 4, :], in0=S[:, hb * 4:(hb + 1) * 4, :],
                        in1=egc[q4][:, hb * 4:(hb + 1) * 4, 63:64].broadcast_to([128, 4, 128]), op=ALU.mult),
                        r=[B_S, B_egc[q4]], w=[B_S])
                    P.op("dve", lambda e: e.tensor_tensor(
                        out=S[:, hb * 4:(hb + 1) * 4, :].rearrange("p a b -> p (a b)"),
                        in0=S[:, hb * 4:(hb + 1) * 4, :].rearrange("p a b -> p (a b)"), in1=X[:], op=ALU.add),
                        r=[B_S] + BXs, w=[B_S])
                    yield
                    P.op("act", lambda e: e.activation(out=S_bf[:, hb * 4:(hb + 1) * 4, :], in_=S[:, hb * 4:(hb + 1) * 4, :],
                                                       func=AF.Copy), r=[B_S], w=[B_Sbf])
                    yield
                scan_done[n] = True

            def gen_epi(blk):
                bp = blk % 2
                c0 = blk * GB
                P.op("sp", lambda e: e.dma_start(
                    out=gg[:], in_=projT[38 * 128:46 * 128, c0:c0 + GB].rearrange("(h p) t -> p h t", p=128)),
                    r=[B_proj], w=[B_gg], dma=True)
                yield
                P.op("act", lambda e: e.activation(out=sgg[:], in_=gg[:], func=AF.Silu), r=[B_gg], w=[B_sgg])
                yield
                for h in range(8):
                    P.op("act", lambda e: e.activation(out=sq4e[:], in_=oTb[bp][:, h, :], func=AF.Square),
                         r=[B_oTb[bp]], w=[B_sq4e])
                    yield
                    P.op("pe", lambda e: e.matmul(bk[6][:, 256:256 + GB], lhsT=ones_b[:], rhs=sq4e[:], start=True, stop=True),
                         r=[B_ones, B_sq4e], w=[B_half[1]])
                    P.op("act", lambda e: e.activation(out=rs4e[:], in_=bk[6][:, 256:256 + GB], func=AF.Ln, scale=1.0 / 128,
                                                       bias=EPS), r=[B_half[1]], w=[B_rs4e])
                    P.op("act", lambda e: e.activation(out=rs4e[:], in_=rs4e[:], func=AF.Exp, scale=-0.5),
                         r=[B_rs4e], w=[B_rs4e])
                    yield
                    P.op("dve", lambda e: e.scalar_tensor_tensor(out=o1[:], in0=oTb[bp][:, h, :], scalar=gv[:, 6:7], in1=rs4e[:],
                                                                 op0=ALU.mult, op1=ALU.mult),
                         r=[B_oTb[bp], B_gv, B_rs4e], w=[B_o1])
                    ok = cnt4["o2"] % 2
                    cnt4["o2"] += 1
                    P.op("dve", lambda e: e.tensor_tensor(out=o2[ok][:], in0=o1[:], in1=sgg[:, h, :], op=ALU.mult),
                         r=[B_o1, B_sgg], w=[B_o2[ok]])
                    P.op("pool", lambda e: e.dma_start(out=mixT[1024 + h * 128:1024 + (h + 1) * 128, c0:c0 + GB], in_=o2[ok][:]),
                         r=[B_o2[ok]], w=[B_mix], dma=True)
                    yield
                epi_done[blk] = True

            def chunks_of(b):
                return range(b * CPB, (b + 1) * CPB)

            def stream_prep(par):
                for n in range(par, NC, 2):
                    while not pro_done[n // CPB] or (n >= 3 and not scan_done[n - 3]):
                        yield "blocked"
                    yield from gen_prep(n)

            def stream_scan():
                for n in range(NC):
                    b = n // CPB
                    while not prep_done[n] or (b >= 2 and not epi_done[b - 2]):
                        yield "blocked"
                    yield from gen_scan(n)

            def stream_pro():
                for b in range(NGB):
                    while b >= 2 and not all(prep_done[c] for c in chunks_of(b - 2)):
                        yield "blocked"
                    yield from gen_pro(b)

            def stream_epi():
                for b in range(NGB):
                    while not scan_done[b * CPB + CPB - 1]:
                        yield "blocked"
                    yield from gen_epi(b)

            if os.environ.get("GDN_SEQ"):
                for b in range(NGB):
                    for _ in gen_pro(b):
                        pass
                    for n in chunks_of(b):
                        if not os.environ.get("GDN_NOPREP"):
                            kmax = int(os.environ.get("GDN_PREP_STEPS", "1000"))
                            for k_, _ in enumerate(gen_prep(n)):
                                if k_ + 1 >= kmax:
                                    break
                        if not os.environ.get("GDN_NOSCAN"):
                            for _ in gen_scan(n):
                                pass
                    if not os.environ.get("GDN_NOEPI"):
                        for _ in gen_epi(b):
                            pass
                streams = []
            else:
                streams = [stream_pro(), stream_prep(0), stream_prep(1), stream_scan(), stream_epi()]
            while streams:
                progressed = False
                for st in list(streams):
                    try:
                        r_ = next(st)
                        if r_ != "blocked":
                            progressed = True
                    except StopIteration:
                        streams.remove(st)
                        progressed = True
                assert progressed, "GDN scheduler deadlock"

        P.barrier()
        with ExitStack() as ps:
            def sb5(name, shape, dt):
                return ps.enter_context(nc.sbuf_tensor(name, shape, dt))

            wob = sb5("wob", [128, 16, 2048], BF16)
            wof = [sb5("wof%d" % i, [128, 2, 2048], F32) for i in range(2)]
            mt = [sb5("mt%d" % i, [128, 16, 128], BF16) for i in range(2)]
            xr = [sb5("xr%d" % i, [128, 2048], F32) for i in range(2)]
            yo = [sb5("yo%d" % i, [128, 2048], F32) for i in range(2)]
            B_wob = Buf()
            B_wof = [Buf(), Buf()]
            B_mt = [Buf(), Buf()]
            B_xr = [Buf(), Buf()]
            B_yo = [Buf(), Buf()]
            B_out = Buf()
            po = [ps.enter_context(nc.psum_tensor("po%d" % i, [128, 512], F32)) for i in range(4)]
            B_po = [Buf() for _ in range(4)]
            for q in range(8):
                k = q % 2
                P.op("sp", lambda e, k=k, q=q: e.dma_start(out=wof[k][:], in_=wout[:, 2 * q:2 * q + 2, :]),
                     w=[B_wof[k]], dma=True)
                P.op("dve", lambda e, k=k, q=q: e.tensor_copy(out=wob[:, 2 * q:2 * q + 2, :], in_=wof[k][:]),
                     r=[B_wof[k]], w=[B_wob])
            for tg in range(NT):
                k = tg % 2
                ts_ = slice(tg * 128, (tg + 1) * 128)
                P.op("sp", lambda e, k=k, ts_=ts_: e.dma_start(
                    out=mt[k][:], in_=mixT[:, ts_].rearrange("(c p) t -> p c t", p=128)),
                    r=[B_mix], w=[B_mt[k]], dma=True)
                P.op("sp", lambda e, k=k, ts_=ts_: e.dma_start(out=xr[k][:], in_=x[ts_, :]), w=[B_xr[k]], dma=True)
                for nq in range(4):
                    for c in range(16):
                        P.op("pe", lambda e, k=k, nq=nq, c=c: e.matmul(
                            po[nq][:], lhsT=mt[k][:, c, :], rhs=wob[:, c, nq * 512:(nq + 1) * 512],
                            start=(c == 0), stop=(c == 15)), r=[B_mt[k], B_wob], w=[B_po[nq]], inc=(c == 15))
                    P.op("dve", lambda e, k=k, nq=nq: e.tensor_tensor(
                        out=yo[k][:, nq * 512:(nq + 1) * 512], in0=po[nq][:], in1=xr[k][:, nq * 512:(nq + 1) * 512],
                        op=ALU.add), r=[B_po[nq], B_xr[k]], w=[B_yo[k]])
                P.op("pool", lambda e, k=k, ts_=ts_: e.dma_start(out=out[ts_, :], in_=yo[k][:]),
                     r=[B_yo[k]], w=[B_out], dma=True)
            P.wait_all("pool", [B_out, B_mix])
        final = {P.esem[k]: P.cnt[k] for k in P.eng}
        for q in P.dsems:
            for s_, v_ in zip(P.dsems[q], P.dval[q]):
                final[s_] = v_
        for s_, v_ in P.maxwait.items():
            assert final.get(s_, 0) >= v_, ("unreachable wait", s_, v_, final.get(s_))
        print("ops:", P.cnt, "waits:", P.nwaits, "pend:", P.pend)
    return nc


COL_SLICES = {
    "cq": (0, 512), "ckv": (512, 768), "krope": (768, 832), "mgate": (832, 1856), "gq": (1856, 2880),
    "gk": (2880, 3904), "gv": (3904, 4928), "ga": (4928, 4936), "gb": (4936, 4944), "ggate": (4944, 5968),
}


def layout_inputs(T, x_b, pos_b, inputs):
    w_in = inputs["w_in"][0]
    cs = COL_SLICES

    def cols(name):
        a, b = cs[name]
        return w_in[:, a:b]

    kr = cols("krope")
    kr_perm = np.concatenate([kr[:, 32:64], kr[:, 0:32]], axis=1)
    wr = np.concatenate([cols("cq"), cols("ckv"), cols("mgate"), cols("gq"), cols("gk"), cols("gv"),
                         cols("ggate"), kr, kr_perm], axis=1)
    win = np.ascontiguousarray(wr.reshape(16, 128, NG, 128).transpose(2, 1, 0, 3))
    wab = np.concatenate([cols("ga"), cols("gb")], axis=1)
    wab = np.ascontiguousarray(wab.reshape(16, 128, 16).transpose(1, 0, 2))
    ngain = np.ascontiguousarray(inputs["norm_gain"][0].reshape(16, 128).T)
    mla_q = inputs["mla_q_norm_gain"][0]
    mla_k = inputs["mla_k_norm_gain"][0]
    wuq_o = inputs["w_uq"][0]
    parts = []
    for h in range(8):
        blk = wuq_o[:, h * 192:(h + 1) * 192]
        rp = blk[:, 128:192]
        parts += [blk[:, 0:128], rp, np.concatenate([rp[:, 32:64], rp[:, 0:32]], axis=1)]
    wuq = np.concatenate(parts, axis=1)
    wuq = np.ascontiguousarray(wuq.reshape(4, 128, 2048).transpose(1, 0, 2))
    wukv_o = inputs["w_ukv"][0]
    kn = [wukv_o[:, h * 256:h * 256 + 128] for h in range(8)]
    vv = [wukv_o[:, h * 256 + 128:(h + 1) * 256] for h in range(8)]
    wukv = np.concatenate(kn + vv, axis=1)
    wukv = np.ascontiguousarray(wukv.reshape(2, 128, 2048).transpose(1, 0, 2))
    qag = np.ascontiguousarray(inputs["mla_q_a_gain"][0].reshape(4, 128).T)
    kvag = np.ascontiguousarray(inputs["mla_kv_a_gain"][0].reshape(2, 128).T)
    gvec = np.zeros((128, 8), np.float32)

    def perm(v):
        return np.concatenate([v[32:64], v[0:32]])

    gvec[:, 0] = mla_q[0:128]
    gvec[0:64, 1] = mla_q[128:192]
    gvec[0:64, 2] = perm(mla_q[128:192])
    gvec[:, 3] = mla_k[0:128]
    gvec[0:64, 4] = mla_k[128:192]
    gvec[0:64, 5] = perm(mla_k[128:192])
    gvec[:, 6] = inputs["gdn_out_norm_gain"][0]
    cwo = inputs["gdn_conv_w"][0]
    cw = np.ascontiguousarray(cwo.reshape(4, 24, 128).transpose(2, 1, 0))
    m = {"x": np.ascontiguousarray(x_b[:T]), "win": win, "wab": wab, "ngain": ngain,
         "pos64": np.ascontiguousarray(np.broadcast_to(pos_b[None, :T], (64, T))),
         "wuq": wuq, "wukv": wukv, "qag": qag, "kvag": kvag, "gvec": gvec, "cw": cw,
         "alog64": np.ascontiguousarray(np.broadcast_to(inputs["gdn_a_log"][0][None, :], (64, 8))),
         "dtb64": np.ascontiguousarray(np.broadcast_to(inputs["gdn_dt_bias"][0][None, :], (64, 8))),
         "wout": np.ascontiguousarray(inputs["w_out"][0].reshape(16, 128, 2048).transpose(1, 0, 2))}
    return m


_NC_CACHE = {}


def kernel(**inputs):
    T = inputs["x"].shape[1]
    B = inputs["x"].shape[0]
    if T not in _NC_CACHE:
        _NC_CACHE[T] = build_nc(T)
    nc = _NC_CACHE[T]
    inputs = {k: np.asarray(v) for k, v in inputs.items()}
    maps = [layout_inputs(T, inputs["x"][c % B], inputs["positions"][c % B], inputs) for c in range(8)]
    res = run_bass_kernel_spmd(nc, maps, core_ids=list(range(8)))
    return np.stack([np.asarray(res.results[b]["out"]) for b in range(B)], axis=0).astype(np.float32)
```

```python
import os
import numpy as np
from contextlib import ExitStack
import concourse.bass as bass
import concourse.mybir as mybir
from concourse.bass_utils import run_bass_kernel_spmd

F32 = mybir.dt.float32
BF16 = mybir.dt.bfloat16
I32 = mybir.dt.int32
ALU = mybir.AluOpType
AF = mybir.ActivationFunctionType
AX = mybir.AxisListType

D = 2048
NG = 47
EPS = 1e-6
KDMA = 8


class Buf:
    __slots__ = ("name", "w", "r")

    def __init__(self, name=""):
        self.name = name
        self.w = None
        self.r = {}


class Prog:
    def __init__(self, nc, es):
        self.nc = nc
        self.eng = {"pe": nc.tensor, "act": nc.scalar, "dve": nc.vector, "pool": nc.gpsimd, "sp": nc.sync}
        self.esem = {k: es.enter_context(nc.semaphore("s_" + k)) for k in self.eng}
        self.cnt = {k: 0 for k in self.eng}
        self.pend = {k: False for k in self.eng}
        self.seen = {k: {} for k in self.eng}
        self.dsems = {k: [es.enter_context(nc.semaphore("d_%s%d" % (k, i))) for i in range(KDMA)]
                      for k in ("sp", "pool", "act")}
        self.dval = {k: [0] * KDMA for k in self.dsems}
        self.dslot = {k: 0 for k in self.dsems}
        self.nwaits = 0
        self.maxwait = {}

    def op(self, e, fn, r=(), w=(), dma=False, inc=True):
        need = {}

        def add(t):
            if t is not None and need.get(t[0], 0) < t[1]:
                need[t[0]] = t[1]

        for b in r:
            add(b.w)
        for b in w:
            add(b.w)
            for s, v in b.r.items():
                add((s, v))
        if dma:
            slot = self.dslot[e]
            self.dslot[e] = (slot + 1) % KDMA
            sem = self.dsems[e][slot]
            prev = self.dval[e][slot]
            if prev > 0:
                add((sem, prev))
            self.dval[e][slot] = prev + 16
            tok = (sem, prev + 16)
        else:
            if inc:
                self.cnt[e] += 1
                self.pend[e] = False
                tok = (self.esem[e], self.cnt[e])
            else:
                self.pend[e] = True
                tok = (self.esem[e], self.cnt[e] + 1)
        seen = self.seen[e]
        eo = self.eng[e]
        for s, v in need.items():
            if e == "pe" and s is self.esem["pe"]:
                continue
            if seen.get(s, 0) >= v:
                continue
            seen[s] = v
            eo.wait_ge(s, v)
            self.nwaits += 1
            if self.maxwait.get(s, 0) < v:
                self.maxwait[s] = v
        ins = fn(eo)
        if dma:
            ins.then_inc(tok[0], 16)
        elif inc:
            ins.then_inc(tok[0], 1)
        for b in r:
            if b.r.get(tok[0], 0) < tok[1]:
                b.r[tok[0]] = tok[1]
        for b in w:
            b.w = tok
            b.r = {}
        return tok

    def barrier(self):
        for e, eo in self.eng.items():
            assert not self.pend[e]
        for e, eo in self.eng.items():
            seen = self.seen[e]
            for e2 in self.eng:
                s, v = self.esem[e2], self.cnt[e2]
                if v > 0 and seen.get(s, 0) < v and not (e == e2 and e == "pe"):
                    seen[s] = v
                    eo.wait_ge(s, v)
            for q in self.dsems:
                for s, v in zip(self.dsems[q], self.dval[q]):
                    if v > 0 and seen.get(s, 0) < v:
                        seen[s] = v
                        eo.wait_ge(s, v)

    def wait_all(self, e, bufs):
        eo = self.eng[e]
        need = {}
        for b in bufs:
            for t in ([b.w] if b.w else []) + list(b.r.items()):
                if need.get(t[0], 0) < t[1]:
                    need[t[0]] = t[1]
        for s, v in need.items():
            eo.wait_ge(s, v)


def build_nc(T, dbg=None):
    NT = T // 128
    TSB = min(T, 2048)
    NSB = T // TSB
    NTB = TSB // 512
    nc = bass.Bass("TRN2", target_bir_lowering=False)

    def din(name, shape, dt=F32):
        return nc.dram_tensor(name, shape, dt, kind="ExternalInput").ap()

    x = din("x", [T, D])
    win = din("win", [NG, 128, 16, 128])
    wab = din("wab", [128, 16, 16])
    ngain = din("ngain", [128, 16])
    out = nc.dram_tensor("out", [T, D], F32, kind="ExternalOutput").ap()
    projT = nc.dram_tensor("projT", [NG * 128, T], BF16,
                           kind="Internal").ap()
    pos64 = din("pos64", [64, T], I32)
    wuq = din("wuq", [128, 4, 2048])
    wukv = din("wukv", [128, 2, 2048])
    qag = din("qag", [128, 4])
    kvag = din("kvag", [128, 2])
    gvec = din("gvec", [128, 8])
    cw = din("cw", [128, 24, 4])
    alog64 = din("alog64", [64, 8])
    dtb64 = din("dtb64", [64, 8])
    wout = din("wout", [128, 16, 2048])
    okind = "ExternalOutput" if dbg else "Internal"
    QT = nc.dram_tensor("QT", [8, 192, T], BF16, kind="Internal").ap()
    KT = nc.dram_tensor("KT", [8, 192, T], BF16, kind="Internal").ap()
    Vs = nc.dram_tensor("Vs", [T, 1024], BF16, kind="Internal").ap()
    mixT = nc.dram_tensor("mixT", [2048, T], BF16, kind=okind).ap()

    with ExitStack() as es:
        P = Prog(nc, es)

        def sb(name, shape, dt):
            return es.enter_context(nc.sbuf_tensor(name, shape, dt))

        ident = sb("ident", [128, 128], BF16)
        iot = sb("iot", [128, 128], I32)
        gabtm = sb("gabtm", [64, 2 * NT, 16], F32)
        ngs = sb("ngs", [128, 16], F32)
        B_ident, B_iot, B_gab, B_ngs = Buf("ident"), Buf("iot"), Buf("gab"), Buf("ngs")
        P.op("pool", lambda e: e.iota(iot[:], pattern=[[1, 128]], base=0, channel_multiplier=-1), w=[B_iot])
        P.op("dve", lambda e: e.tensor_single_scalar(out=ident[:], in_=iot[:], scalar=0, op=ALU.is_equal),
             r=[B_iot], w=[B_ident])
        P.op("sp", lambda e: e.dma_start(out=ngs[:], in_=ngain[:, :]), w=[B_ngs], dma=True)

        with ExitStack() as ps:
            def sb1(name, shape, dt):
                return ps.enter_context(nc.sbuf_tensor(name, shape, dt))

            def pp1(name, shape, dt):
                return ps.enter_context(nc.psum_tensor(name, shape, dt))

            xt = [sb1("xt%d" % i, [128, D], F32) for i in range(2)]
            xs = [sb1("xs%d" % i, [128, D], BF16) for i in range(2)]
            junk = sb1("junk", [128, D], BF16)
            ss = [sb1("ss%d" % i, [128, 1], F32) for i in range(2)]
            rs = [sb1("rs%d" % i, [128, 1], F32) for i in range(2)]
            xnT = sb1("xnT", [128, 16, TSB], BF16)
            wf = [sb1("wf%d" % i, [128, 16, 128], F32) for i in range(2)]
            wb = [sb1("wb%d" % i, [128, 16, 128], BF16) for i in range(2)]
            ot = [sb1("ot%d" % i, [128, TSB], BF16) for i in range(2)]
            wabf = sb1("wabf", [128, 16, 16], F32)
            wabb = sb1("wabb", [128, 16, 16], BF16)
            ptr = [pp1("ptr%d" % i, [128, 8, 128], BF16) for i in range(2)]
            pacc = [pp1("pacc%d" % i, [128, 512], F32) for i in range(4)]
            pgab = pp1("pgab", [128, 512], F32)
            B_xt = [Buf() for _ in range(2)]
            B_xs = [Buf() for _ in range(2)]
            B_junk = Buf()
            B_ss = [Buf() for _ in range(2)]
            B_rs = [Buf() for _ in range(2)]
            B_xnT = [Buf() for _ in range(TSB // 128)]
            B_wf = [Buf() for _ in range(2)]
            B_wb = [Buf() for _ in range(2)]
            B_ot = [Buf() for _ in range(2)]
            B_wab, B_wabb = Buf(), Buf()
            B_ptr = [Buf() for _ in range(2)]
            B_pacc = [Buf() for _ in range(4)]
            B_pgab = Buf()
            B_proj = Buf("projT")

            P.op("sp", lambda e: e.dma_start(out=wabf[:], in_=wab[:, :, :]), w=[B_wab], dma=True)
            P.op("dve", lambda e: e.tensor_copy(out=wabb[:], in_=wabf[:]), r=[B_wab], w=[B_wabb])

            nptr = 0
            for sbi in range(NSB):
                for tl in range(TSB // 128):
                    tg = sbi * (TSB // 128) + tl
                    k = tg % 2
                    P.op("sp", lambda e, k=k, tg=tg: e.dma_start(out=xt[k][:], in_=x[tg * 128:(tg + 1) * 128, :]),
                         w=[B_xt[k]], dma=True)
                    P.op("act", lambda e, k=k: e.activation(out=junk[:], in_=xt[k][:], func=AF.Square,
                                                            accum_out=ss[k][:]),
                         r=[B_xt[k]], w=[B_junk, B_ss[k]])
                    P.op("act", lambda e, k=k: e.activation(out=rs[k][:], in_=ss[k][:], func=AF.Ln,
                                                            scale=1.0 / D, bias=EPS),
                         r=[B_ss[k]], w=[B_rs[k]])
                    P.op("act", lambda e, k=k: e.activation(out=rs[k][:], in_=rs[k][:], func=AF.Exp, scale=-0.5),
                         r=[B_rs[k]], w=[B_rs[k]])
                    P.op("act", lambda e, k=k: e.activation(out=xs[k][:], in_=xt[k][:], func=AF.Copy,
                                                            scale=rs[k][:]),
                         r=[B_xt[k], B_rs[k]], w=[B_xs[k]])
                    for half in range(2):
                        pk = nptr % 2
                        nptr += 1
                        for j in range(8):
                            c = half * 8 + j
                            P.op("pe", lambda e, pk=pk, j=j, c=c, k=k: e.transpose(
                                out=ptr[pk][:, j, :], in_=xs[k][:, c * 128:(c + 1) * 128], identity=ident[:]),
                                r=[B_xs[k], B_ident], w=[B_ptr[pk]], inc=(j == 7))
                        P.op("dve", lambda e, pk=pk, half=half, tl=tl: e.tensor_tensor(
                            out=xnT[:, half * 8:(half + 1) * 8, tl * 128:(tl + 1) * 128], in0=ptr[pk][:],
                            in1=ngs[:, half * 8:(half + 1) * 8].unsqueeze(2).broadcast_to([128, 8, 128]),
                            op=ALU.mult),
                            r=[B_ptr[pk], B_ngs], w=[B_xnT[tl]])
                    for hf in range(2):
                        for c in range(16):
                            P.op("pe", lambda e, c=c, tl=tl, hf=hf: e.matmul(
                                pgab[0:64, hf * 16:(hf + 1) * 16],
                                lhsT=xnT[:, c, tl * 128 + hf * 64:tl * 128 + hf * 64 + 64], rhs=wabb[:, c, :],
                                start=(c == 0), stop=(c == 15)),
                                r=[B_xnT[tl], B_wabb], w=[B_pgab], inc=(c == 15))
                    P.op("act", lambda e, tg=tg: e.activation(
                        out=gabtm[:, 2 * tg:2 * tg + 2, :],
                        in_=pgab[0:64, 0:32].rearrange("p (a b) -> p a b", a=2), func=AF.Copy),
                        r=[B_pgab], w=[B_gab])
                for g in range(NG):
                    k = g % 2
                    P.op("sp", lambda e, k=k, g=g: e.dma_start(out=wf[k][:], in_=win[g]), w=[B_wf[k]], dma=True)
                    P.op("dve", lambda e, k=k: e.tensor_copy(out=wb[k][:], in_=wf[k][:]), r=[B_wf[k]], w=[B_wb[k]])
                    if g == 46:
                        P.op("dve", lambda e, k=k: e.tensor_scalar(out=wb[k][:, :, 64:96], in0=wb[k][:, :, 64:96],
                                                                   scalar1=-1.0, scalar2=None, op0=ALU.mult),
                             r=[B_wb[k]], w=[B_wb[k]])
                    for tb in range(NTB):
                        for c in range(16):
                            P.op("pe", lambda e, k=k, tb=tb, c=c: e.matmul(
                                pacc[tb][:], lhsT=wb[k][:, c, :], rhs=xnT[:, c, tb * 512:(tb + 1) * 512],
                                start=(c == 0), stop=(c == 15)),
                                r=[B_wb[k]] + B_xnT[tb * 4:(tb + 1) * 4], w=[B_pacc[tb]], inc=(c == 15))
                        P.op("act", lambda e, k=k, tb=tb: e.activation(
                            out=ot[k][:, tb * 512:(tb + 1) * 512], in_=pacc[tb][:], func=AF.Copy),
                            r=[B_pacc[tb]], w=[B_ot[k]])
                    P.op("pool", lambda e, k=k, g=g, sbi=sbi: e.dma_start(
                        out=projT[g * 128:(g + 1) * 128, sbi * TSB:(sbi + 1) * TSB], in_=ot[k][:]),
                        r=[B_ot[k]], w=[B_proj], dma=True)


        P.barrier()
        NB = T // 512
        ones_b = sb("ones_b", [128, 128], BF16)
        gv = sb("gv", [128, 8], F32)
        B_ones, B_gv = Buf(), Buf()
        P.op("pool", lambda e: e.memset(ones_b[:], 1.0), w=[B_ones])
        P.op("sp", lambda e: e.dma_start(out=gv[:], in_=gvec[:, :]), w=[B_gv], dma=True)
        B_QT, B_KT, B_Vs, B_mix = Buf(), Buf(), Buf(), Buf()
        LN192 = float(np.log(192.0))
        PI = float(np.pi)
        C1 = 6.28125
        C2 = float(2 * np.pi - 6.28125)

        def rstd_from_psum(ps_ap, B_ps, dst_ap, B_dst, n, extra_bias=0.0, parts=128):
            P.op("act", lambda e: e.activation(out=dst_ap, in_=ps_ap, func=AF.Ln, scale=1.0 / n, bias=EPS),
                 r=[B_ps], w=[B_dst])
            P.op("act", lambda e: e.activation(out=dst_ap, in_=dst_ap, func=AF.Exp, scale=-0.5, bias=extra_bias),
                 r=[B_dst], w=[B_dst])

        with ExitStack() as ps:
            def sb2(name, shape, dt):
                return ps.enter_context(nc.sbuf_tensor(name, shape, dt))

            def pp2(name, shape, dt=F32):
                return ps.enter_context(nc.psum_tensor(name, shape, dt))

            wuqf = sb2("wuqf", [128, 4, 2048], F32)
            wuqb = sb2("wuqb", [128, 4, 2048], BF16)
            wukvf = sb2("wukvf", [128, 2, 2048], F32)
            wukvb = sb2("wukvb", [128, 2, 2048], BF16)
            qag_s = sb2("qag_s", [128, 4], F32)
            kvag_s = sb2("kvag_s", [128, 2], F32)
            ivi = sb2("ivi", [64, 1], I32)
            invf = sb2("invf", [64, 1], F32)
            B_wuqf, B_wuqb, B_wukvf, B_wukvb, B_qag, B_kvag, B_ivi, B_invf = [Buf() for _ in range(8)]
            P.op("sp", lambda e: e.dma_start(out=wuqf[:], in_=wuq[:, :, :]), w=[B_wuqf], dma=True)
            P.op("sp", lambda e: e.dma_start(out=wukvf[:], in_=wukv[:, :, :]), w=[B_wukvf], dma=True)
            P.op("sp", lambda e: e.dma_start(out=qag_s[:], in_=qag[:, :]), w=[B_qag], dma=True)
            P.op("sp", lambda e: e.dma_start(out=kvag_s[:], in_=kvag[:, :]), w=[B_kvag], dma=True)
            for kc in range(4):
                P.op("dve", lambda e, kc=kc: e.tensor_scalar(out=wuqb[:, kc, :], in0=wuqf[:, kc, :],
                                                             scalar1=qag_s[:, kc:kc + 1], scalar2=None, op0=ALU.mult),
                     r=[B_wuqf, B_qag], w=[B_wuqb])
            for kc in range(2):
                P.op("dve", lambda e, kc=kc: e.tensor_scalar(out=wukvb[:, kc, :], in0=wukvf[:, kc, :],
                                                             scalar1=kvag_s[:, kc:kc + 1], scalar2=None, op0=ALU.mult),
                     r=[B_wukvf, B_kvag], w=[B_wukvb])
            wq4 = wuqb[:].rearrange("p k (h c) -> p k h c", c=256)
            for kc in range(4):
                P.op("dve", lambda e, kc=kc: e.tensor_scalar(out=wq4[:, kc, :, 192:224], in0=wq4[:, kc, :, 192:224],
                                                             scalar1=-1.0, scalar2=None, op0=ALU.mult),
                     r=[B_wuqb], w=[B_wuqb])
            P.op("pool", lambda e: e.iota(ivi[0:32, :], pattern=[[0, 1]], base=0, channel_multiplier=1), w=[B_ivi])
            P.op("pool", lambda e: e.iota(ivi[32:64, :], pattern=[[0, 1]], base=0, channel_multiplier=1), w=[B_ivi])
            P.op("dve", lambda e: e.tensor_copy(out=invf[:], in_=ivi[:]), r=[B_ivi], w=[B_invf])
            P.op("act", lambda e: e.activation(out=invf[:], in_=invf[:], func=AF.Exp,
                                               scale=-float(np.log(10000.0)) / 32.0), r=[B_invf], w=[B_invf])

            cqT = sb2("cqT", [128, 4, 512], BF16)
            sqb = sb2("sqb", [128, 4, 512], BF16)
            cqn = sb2("cqn", [128, 4, 512], BF16)
            ckvT = sb2("ckvT", [128, 2, 512], BF16)
            ckvn = sb2("ckvn", [128, 2, 512], BF16)
            krA = sb2("krA", [64, 512], BF16)
            krB = sb2("krB", [64, 512], BF16)
            rst = sb2("rst", [128, 512], F32)
            rsth = sb2("rsth", [128, 512], F32)
            posi = sb2("posi", [64, 512], I32)
            ang = sb2("ang", [64, 512], F32)
            nfi = sb2("nfi", [64, 512], I32)
            nff = sb2("nff", [64, 512], F32)
            th = sb2("th", [64, 512], F32)
            thc = sb2("thc", [64, 512], F32)
            msk = sb2("msk", [64, 512], F32)
            cosT = sb2("cosT", [64, 512], F32)
            sinT = sb2("sinT", [64, 512], F32)
            ra = sb2("ra", [64, 512], F32)
            rb_ = sb2("rb_", [64, 512], F32)
            kr = sb2("kr", [64, 512], F32)
            sqn = sb2("sqn", [128, 512], BF16)
            sqr = sb2("sqr", [64, 512], BF16)
            sqkr = sb2("sqkr", [64, 512], BF16)
            on_ = [sb2("on%d" % i, [128, 512], BF16) for i in range(4)]
            or_ = [sb2("or%d" % i, [64, 512], BF16) for i in range(4)]
            sqnk = sb2("sqnk", [128, 512], BF16)
            rsthk = sb2("rsthk", [128, 512], F32)
            B_sqnk, B_rsthk = Buf(), Buf()
            vo = [sb2("vo%d" % i, [128, 1024], BF16) for i in range(2)]
            (B_cqT, B_sqb, B_cqn, B_ckvT, B_ckvn, B_krA, B_krB, B_rst, B_rsth, B_posi, B_ang, B_nfi, B_nff, B_th,
             B_thc, B_msk, B_cos, B_sin, B_ra, B_rb, B_kr, B_sqn, B_sqr, B_sqkr) = [Buf() for _ in range(24)]
            B_on = [Buf() for _ in range(4)]
            B_or = [Buf() for _ in range(4)]
            B_vo = [Buf(), Buf()]
            p_bc = pp2("p_bc", [128, 512])
            p_n = pp2("p_n", [128, 512])
            p_a = pp2("p_a", [128, 512])
            p_b = pp2("p_b", [128, 512])
            p_s = pp2("p_s", [128, 512])
            p_v = [pp2("p_v%d" % i, [128, 512]) for i in range(2)]
            B_pbc, B_pn, B_pa, B_pb, B_ps = [Buf() for _ in range(5)]
            B_pv = [Buf(), Buf()]
            nout = 0
            nvo = 0
            for blk in range(NB):
                c0 = blk * 512
                cs = slice(c0, c0 + 512)
                P.op("sp", lambda e, cs=cs: e.dma_start(
                    out=cqT[:], in_=projT[0:512, cs].rearrange("(k p) t -> p k t", p=128)),
                    r=[B_proj], w=[B_cqT], dma=True)
                P.op("sp", lambda e, cs=cs: e.dma_start(
                    out=ckvT[:], in_=projT[512:768, cs].rearrange("(k p) t -> p k t", p=128)),
                    r=[B_proj], w=[B_ckvT], dma=True)
                P.op("sp", lambda e, cs=cs: e.dma_start(out=krA[:], in_=projT[46 * 128:46 * 128 + 64, cs]),
                     r=[B_proj], w=[B_krA], dma=True)
                P.op("sp", lambda e, cs=cs: e.dma_start(out=krB[:], in_=projT[46 * 128 + 64:47 * 128, cs]),
                     r=[B_proj], w=[B_krB], dma=True)
                P.op("sp", lambda e, cs=cs: e.dma_start(out=posi[:], in_=pos64[:, cs]), w=[B_posi], dma=True)
                P.op("dve", lambda e: e.tensor_copy(out=ang[:], in_=posi[:]), r=[B_posi], w=[B_ang])
                P.op("dve", lambda e: e.tensor_scalar(out=ang[:], in0=ang[:], scalar1=invf[:, 0:1], scalar2=None,
                                                      op0=ALU.mult), r=[B_ang, B_invf], w=[B_ang])
                P.op("dve", lambda e: e.tensor_scalar(out=nfi[:], in0=ang[:], scalar1=1.0 / (2 * PI), scalar2=None,
                                                      op0=ALU.mult), r=[B_ang], w=[B_nfi])
                P.op("dve", lambda e: e.tensor_copy(out=nff[:], in_=nfi[:]), r=[B_nfi], w=[B_nff])
                P.op("dve", lambda e: e.scalar_tensor_tensor(out=th[:], in0=nff[:], scalar=-C1, in1=ang[:],
                                                             op0=ALU.mult, op1=ALU.add), r=[B_nff, B_ang], w=[B_th])
                P.op("dve", lambda e: e.scalar_tensor_tensor(out=th[:], in0=nff[:], scalar=-C2, in1=th[:],
                                                             op0=ALU.mult, op1=ALU.add), r=[B_nff, B_th], w=[B_th])

                def wrap(t_ap, B_t):
                    P.op("dve", lambda e: e.tensor_single_scalar(out=msk[:], in_=t_ap, scalar=PI, op=ALU.is_gt),
                         r=[B_t], w=[B_msk])
                    P.op("dve", lambda e: e.scalar_tensor_tensor(out=t_ap, in0=msk[:], scalar=-2 * PI, in1=t_ap,
                                                                 op0=ALU.mult, op1=ALU.add), r=[B_msk, B_t], w=[B_t])
                    P.op("dve", lambda e: e.tensor_single_scalar(out=msk[:], in_=t_ap, scalar=-PI, op=ALU.is_lt),
                         r=[B_t], w=[B_msk])
                    P.op("dve", lambda e: e.scalar_tensor_tensor(out=t_ap, in0=msk[:], scalar=2 * PI, in1=t_ap,
                                                                 op0=ALU.mult, op1=ALU.add), r=[B_msk, B_t], w=[B_t])
                    P.op("dve", lambda e: e.tensor_scalar(out=t_ap, in0=t_ap, scalar1=-PI, scalar2=PI,
                                                          op0=ALU.max, op1=ALU.min), r=[B_t], w=[B_t])

                wrap(th[:], B_th)
                P.op("dve", lambda e: e.tensor_scalar(out=thc[:], in0=th[:], scalar1=PI / 2, scalar2=None,
                                                      op0=ALU.add), r=[B_th], w=[B_thc])
                wrap(thc[:], B_thc)
                P.op("act", lambda e: e.activation(out=sinT[:], in_=th[:], func=AF.Sin), r=[B_th], w=[B_sin])
                P.op("act", lambda e: e.activation(out=cosT[:], in_=thc[:], func=AF.Sin), r=[B_thc], w=[B_cos])

                def lat_norm(src, B_src, dst, B_dst, nk):
                    P.op("act", lambda e: e.activation(out=sqb[:, 0:nk, :], in_=src[:], func=AF.Square),
                         r=[B_src], w=[B_sqb])
                    for kc in range(nk):
                        P.op("pe", lambda e, kc=kc: e.matmul(p_bc[:], lhsT=ones_b[:], rhs=sqb[:, kc, :],
                                                             start=(kc == 0), stop=(kc == nk - 1)),
                             r=[B_ones, B_sqb], w=[B_pbc], inc=(kc == nk - 1))
                    rstd_from_psum(p_bc[:], B_pbc, rst[:], B_rst, 128.0 * nk)
                    P.op("dve", lambda e: e.tensor_tensor(out=dst[:], in0=src[:],
                                                          in1=rst[:].unsqueeze(1).broadcast_to([128, nk, 512]),
                                                          op=ALU.mult), r=[B_src, B_rst], w=[B_dst])

                lat_norm(cqT, B_cqT, cqn, B_cqn, 4)
                lat_norm(ckvT, B_ckvT, ckvn, B_ckvn, 2)

                P.op("dve", lambda e: e.scalar_tensor_tensor(out=ra[:], in0=krA[:], scalar=gv[0:64, 4:5], in1=cosT[:],
                                                             op0=ALU.mult, op1=ALU.mult),
                     r=[B_krA, B_gv, B_cos], w=[B_ra])
                P.op("dve", lambda e: e.scalar_tensor_tensor(out=rb_[:], in0=krB[:], scalar=gv[0:64, 5:6], in1=sinT[:],
                                                             op0=ALU.mult, op1=ALU.mult),
                     r=[B_krB, B_gv, B_sin], w=[B_rb])
                P.op("dve", lambda e: e.tensor_tensor(out=kr[:], in0=ra[:], in1=rb_[:], op=ALU.add),
                     r=[B_ra, B_rb], w=[B_kr])
                P.op("act", lambda e: e.activation(out=sqkr[:], in_=kr[:], func=AF.Square), r=[B_kr], w=[B_sqkr])

                def gen_q():
                    nonlocal nout
                    for h in range(8):
                        for kc in range(4):
                            P.op("pe", lambda e, kc=kc: e.matmul(p_n[:], lhsT=wuqb[:, kc, h * 256:h * 256 + 128],
                                                                 rhs=cqn[:, kc, :], start=(kc == 0), stop=(kc == 3)),
                                 r=[B_wuqb, B_cqn], w=[B_pn], inc=(kc == 3))
                        for kc in range(4):
                            P.op("pe", lambda e, kc=kc: e.matmul(p_a[0:64, :], lhsT=wuqb[:, kc, h * 256 + 128:h * 256 + 192],
                                                                 rhs=cqn[:, kc, :], start=(kc == 0), stop=(kc == 3)),
                                 r=[B_wuqb, B_cqn], w=[B_pa], inc=(kc == 3))
                        for kc in range(4):
                            P.op("pe", lambda e, kc=kc: e.matmul(p_b[0:64, :], lhsT=wuqb[:, kc, h * 256 + 192:h * 256 + 256],
                                                                 rhs=cqn[:, kc, :], start=(kc == 0), stop=(kc == 3)),
                                 r=[B_wuqb, B_cqn], w=[B_pb], inc=(kc == 3))
                        yield
                        P.op("dve", lambda e: e.scalar_tensor_tensor(out=ra[:], in0=p_a[0:64, :], scalar=gv[0:64, 1:2],
                                                                     in1=cosT[:], op0=ALU.mult, op1=ALU.mult),
                             r=[B_pa, B_gv, B_cos], w=[B_ra])
                        P.op("dve", lambda e: e.scalar_tensor_tensor(out=rb_[:], in0=p_b[0:64, :], scalar=gv[0:64, 2:3],
                                                                     in1=sinT[:], op0=ALU.mult, op1=ALU.mult),
                             r=[B_pb, B_gv, B_sin], w=[B_rb])
                        P.op("act", lambda e: e.activation(out=sqn[:], in_=p_n[:], func=AF.Square, scale=gv[:, 0:1]),
                             r=[B_pn, B_gv], w=[B_sqn])
                        P.op("dve", lambda e: e.tensor_tensor(out=ra[:], in0=ra[:], in1=rb_[:], op=ALU.add),
                             r=[B_ra, B_rb], w=[B_ra])
                        yield
                        P.op("act", lambda e: e.activation(out=sqr[:], in_=ra[:], func=AF.Square), r=[B_ra], w=[B_sqr])
                        P.op("pe", lambda e: e.matmul(p_s[:], lhsT=ones_b[:], rhs=sqn[:], start=True, stop=False),
                             r=[B_ones, B_sqn], w=[B_ps], inc=False)
                        P.op("pe", lambda e: e.matmul(p_s[:], lhsT=ones_b[0:64, :], rhs=sqr[:], start=False, stop=True),
                             r=[B_ones, B_sqr], w=[B_ps])
                        yield
                        rstd_from_psum(p_s[:], B_ps, rsth[:], B_rsth, 192.0, extra_bias=-0.5 * LN192)
                        yield
                        k = nout % 4
                        nout += 1
                        P.op("dve", lambda e: e.scalar_tensor_tensor(out=on_[k][:], in0=p_n[:], scalar=gv[:, 0:1],
                                                                     in1=rsth[:], op0=ALU.mult, op1=ALU.mult),
                             r=[B_pn, B_gv, B_rsth], w=[B_on[k]])
                        P.op("dve", lambda e: e.tensor_tensor(out=or_[k][:], in0=ra[:], in1=rsth[0:64, :], op=ALU.mult),
                             r=[B_ra, B_rsth], w=[B_or[k]])
                        P.op("pool", lambda e: e.dma_start(out=QT[h, 0:128, cs], in_=on_[k][:]),
                             r=[B_on[k]], w=[B_QT], dma=True)
                        P.op("pool", lambda e: e.dma_start(out=QT[h, 128:192, cs], in_=or_[k][:]),
                             r=[B_or[k]], w=[B_QT], dma=True)
                        yield

                def gen_k():
                    nonlocal nout
                    for h in range(8):
                        for kc in range(2):
                            P.op("pe", lambda e, kc=kc: e.matmul(p_bc[:], lhsT=wukvb[:, kc, h * 128:(h + 1) * 128],
                                                                 rhs=ckvn[:, kc, :], start=(kc == 0), stop=(kc == 1)),
                                 r=[B_wukvb, B_ckvn], w=[B_pbc], inc=(kc == 1))
                        yield
                        P.op("act", lambda e: e.activation(out=sqnk[:], in_=p_bc[:], func=AF.Square, scale=gv[:, 3:4]),
                             r=[B_pbc, B_gv], w=[B_sqnk])
                        yield
                        P.op("pe", lambda e: e.matmul(p_v[1][:], lhsT=ones_b[:], rhs=sqnk[:], start=True, stop=False),
                             r=[B_ones, B_sqnk], w=[B_pv[1]], inc=False)
                        P.op("pe", lambda e: e.matmul(p_v[1][:], lhsT=ones_b[0:64, :], rhs=sqkr[:], start=False, stop=True),
                             r=[B_ones, B_sqkr], w=[B_pv[1]])
                        yield
                        rstd_from_psum(p_v[1][:], B_pv[1], rsthk[:], B_rsthk, 192.0)
                        yield
                        k = nout % 4
                        nout += 1
                        P.op("dve", lambda e: e.scalar_tensor_tensor(out=on_[k][:], in0=p_bc[:], scalar=gv[:, 3:4],
                                                                     in1=rsthk[:], op0=ALU.mult, op1=ALU.mult),
                             r=[B_pbc, B_gv, B_rsthk], w=[B_on[k]])
                        P.op("dve", lambda e: e.tensor_tensor(out=or_[k][:], in0=kr[:], in1=rsthk[0:64, :], op=ALU.mult),
                             r=[B_kr, B_rsthk], w=[B_or[k]])
                        P.op("pool", lambda e: e.dma_start(out=KT[h, 0:128, cs], in_=on_[k][:]),
                             r=[B_on[k]], w=[B_KT], dma=True)
                        P.op("pool", lambda e: e.dma_start(out=KT[h, 128:192, cs], in_=or_[k][:]),
                             r=[B_or[k]], w=[B_KT], dma=True)
                        yield

                def gen_v():
                    nonlocal nvo
                    for tt in range(4):
                        k = nvo % 2
                        nvo += 1
                        for hh in range(2):
                            for kc in range(2):
                                P.op("pe", lambda e, kc=kc, hh=hh: e.matmul(
                                    p_v[0][:], lhsT=ckvn[:, kc, tt * 128:(tt + 1) * 128],
                                    rhs=wukvb[:, kc, 1024 + hh * 512:1024 + (hh + 1) * 512],
                                    start=(kc == 0), stop=(kc == 1)),
                                    r=[B_ckvn, B_wukvb], w=[B_pv[0]], inc=(kc == 1))
                            P.op("act", lambda e, hh=hh: e.activation(out=vo[k][:, hh * 512:(hh + 1) * 512],
                                                                      in_=p_v[0][:], func=AF.Copy),
                                 r=[B_pv[0]], w=[B_vo[k]])
                            yield
                        P.op("pool", lambda e: e.dma_start(out=Vs[c0 + tt * 128:c0 + (tt + 1) * 128, :], in_=vo[k][:]),
                             r=[B_vo[k]], w=[B_Vs], dma=True)
                        yield

                sts = [gen_q(), gen_k(), gen_v()]
                while sts:
                    for st in list(sts):
                        try:
                            next(st)
                        except StopIteration:
                            sts.remove(st)

        P.barrier()
        with ExitStack() as ps:
            def sb3(name, shape, dt):
                return ps.enter_context(nc.sbuf_tensor(name, shape, dt))

            def pp3(name, shape, dt=F32):
                return ps.enter_context(nc.psum_tensor(name, shape, dt))

            ktn = [sb3("ktn%d" % i, [128, T], BF16) for i in range(2)]
            ktr = [sb3("ktr%d" % i, [64, T], BF16) for i in range(2)]
            vt = [sb3("vt%d" % i, [128, NT, 128], BF16) for i in range(2)]
            qn = [sb3("qn%d" % i, [128, 512], BF16) for i in range(2)]
            qr = [sb3("qr%d" % i, [64, 512], BF16) for i in range(2)]
            gt = [sb3("gt%d" % i, [128, 512], BF16) for i in range(2)]
            sg = sb3("sg", [128, 512], F32)
            pt = [sb3("pt%d" % i, [128, 512], BF16) for i in range(4)]
            mski = sb3("mski", [128, 512], I32)
            cmask = sb3("cmask", [128, 4, 512], BF16)
            rl = sb3("rl", [128, 512], F32)
            of = sb3("of", [128, 512], F32)
            ob = [sb3("ob%d" % i, [128, 512], BF16) for i in range(2)]
            B_kt = [Buf(), Buf()]
            B_vt = [Buf(), Buf()]
            B_q = [Buf(), Buf()]
            B_gt = [Buf(), Buf()]
            B_sg, B_mski, B_cmask, B_rl, B_of = [Buf() for _ in range(5)]
            B_pt = [Buf() for _ in range(4)]
            B_ob = [Buf(), Buf()]
            pS = [pp3("pS%d" % i, [128, 512]) for i in range(4)]
            pO = [pp3("pO%d" % i, [128, 512]) for i in range(2)]
            pL = [pp3("pL%d" % i, [128, 512]) for i in range(2)]
            B_pS = [Buf() for _ in range(4)]
            B_pO = [Buf(), Buf()]
            B_pL = [Buf(), Buf()]
            for dd in range(4):
                P.op("pool", lambda e, dd=dd: e.iota(mski[:], pattern=[[1, 512]], base=-128 * dd, channel_multiplier=-1),
                     w=[B_mski])
                P.op("dve", lambda e, dd=dd: e.tensor_single_scalar(out=cmask[:, dd, :], in_=mski[:], scalar=0,
                                                                    op=ALU.is_ge), r=[B_mski], w=[B_cmask])
            lacc = [sb3("lacc%d" % i, [128, 512], F32) for i in range(2)]
            onesf3 = sb3("onesf3", [128, 128], F32)
            B_lacc = [Buf(), Buf()]
            B_onesf3 = Buf()
            P.op("pool", lambda e: e.memset(onesf3[:], 1.0), w=[B_onesf3])
            pairs = []
            blocks = []
            for h in range(8):
                for j in range(NB):
                    bidx = len(blocks)
                    blocks.append((h, j))
                    for i in range(4 * j + 4):
                        pairs.append((bidx, i))

            def emit_S(idx):
                bidx, i = pairs[idx]
                h, j = blocks[bidx]
                hk = h % 2
                qk = bidx % 2
                sk = idx % 4
                cs = slice(j * 512, (j + 1) * 512)
                if i == 0:
                    if j == 0:
                        P.op("sp", lambda e: e.dma_start(out=ktn[hk][:], in_=KT[h, 0:128, :]),
                             r=[B_KT], w=[B_kt[hk]], dma=True)
                        P.op("sp", lambda e: e.dma_start(out=ktr[hk][:], in_=KT[h, 128:192, :]),
                             r=[B_KT], w=[B_kt[hk]], dma=True)
                        P.op("sp", lambda e: e.dma_start(
                            out=vt[hk][:], in_=Vs[:, h * 128:(h + 1) * 128].rearrange("(n p) d -> p n d", p=128)),
                            r=[B_Vs], w=[B_vt[hk]], dma=True)
                    P.op("sp", lambda e: e.dma_start(out=qn[qk][:], in_=QT[h, 0:128, cs]),
                         r=[B_QT], w=[B_q[qk]], dma=True)
                    P.op("sp", lambda e: e.dma_start(out=qr[qk][:], in_=QT[h, 128:192, cs]),
                         r=[B_QT], w=[B_q[qk]], dma=True)
                    P.op("sp", lambda e: e.dma_start(out=gt[qk][:], in_=projT[(6 + h) * 128:(7 + h) * 128, cs]),
                         r=[B_proj], w=[B_gt[qk]], dma=True)
                P.op("pe", lambda e: e.matmul(pS[sk][:], lhsT=ktn[hk][:, i * 128:(i + 1) * 128], rhs=qn[qk][:],
                                              start=True, stop=False),
                     r=[B_kt[hk], B_q[qk]], w=[B_pS[sk]], inc=False)
                P.op("pe", lambda e: e.matmul(pS[sk][:], lhsT=ktr[hk][:, i * 128:(i + 1) * 128], rhs=qr[qk][:],
                                              start=False, stop=True),
                     r=[B_kt[hk], B_q[qk]], w=[B_pS[sk]])

            def emit_exp(idx):
                bidx, i = pairs[idx]
                h, j = blocks[bidx]
                sk = idx % 4
                qk = bidx % 2
                P.op("act", lambda e: e.activation(out=pt[sk][:], in_=pS[sk][:], func=AF.Exp),
                     r=[B_pS[sk]], w=[B_pt[sk]])
                if i >= 4 * j:
                    P.op("dve", lambda e: e.tensor_tensor(out=pt[sk][:], in0=pt[sk][:], in1=cmask[:, i - 4 * j, :],
                                                          op=ALU.mult), r=[B_pt[sk], B_cmask], w=[B_pt[sk]])

            def emit_PV(idx):
                bidx, i = pairs[idx]
                h, j = blocks[bidx]
                hk = h % 2
                qk = bidx % 2
                sk = idx % 4
                nk = 4 * j + 4
                cs = slice(j * 512, (j + 1) * 512)
                P.op("pe", lambda e: e.matmul(pO[qk][:], lhsT=vt[hk][:, i, :], rhs=pt[sk][:], start=(i == 0),
                                              stop=(i == nk - 1)),
                     r=[B_vt[hk], B_pt[sk]], w=[B_pO[qk]], inc=False)
                P.op("pe", lambda e: e.matmul(pL[qk][:], lhsT=ones_b[:], rhs=pt[sk][:], start=(i == 0),
                                              stop=(i == nk - 1)),
                     r=[B_ones, B_pt[sk]], w=[B_pL[qk]])
                if i == nk - 1:
                    P.op("dve", lambda e: e.reciprocal(out=rl[:], in_=pL[qk][:]), r=[B_pL[qk]], w=[B_rl])
                    P.op("act", lambda e: e.activation(out=sg[:], in_=gt[qk][:], func=AF.Silu),
                         r=[B_gt[qk]], w=[B_sg])
                    P.op("dve", lambda e: e.tensor_tensor(out=of[:], in0=pO[qk][:], in1=rl[:], op=ALU.mult),
                         r=[B_pO[qk], B_rl], w=[B_of])
                    P.op("dve", lambda e: e.tensor_tensor(out=ob[qk][:], in0=of[:], in1=sg[:], op=ALU.mult),
                         r=[B_of, B_sg], w=[B_ob[qk]])
                    P.op("pool", lambda e: e.dma_start(out=mixT[h * 128:(h + 1) * 128, cs], in_=ob[qk][:]),
                         r=[B_ob[qk]], w=[B_mix], dma=True)

            NP_ = len(pairs)
            for q0 in range(min(3, NP_)):
                emit_S(q0)
            for idx in range(NP_):
                emit_exp(idx)
                if idx + 3 < NP_:
                    emit_S(idx + 3)
                emit_PV(idx)
        P.barrier()
        NC = T // 64
        with ExitStack() as ps:
            def sb4(name, shape, dt):
                return ps.enter_context(nc.sbuf_tensor(name, shape, dt))

            def pp4(name, shape, dt=F32):
                return ps.enter_context(nc.psum_tensor(name, shape, dt))

            def bc3(ap2, n):
                return ap2.unsqueeze(2).broadcast_to([ap2.shape[0], ap2.shape[1], n])

            def bch(ap2, nh=8):
                return ap2.unsqueeze(1).broadcast_to([ap2.shape[0], nh, ap2.shape[1]])

            cw_s = sb4("cw_s", [128, 24, 4], F32)
            al_s = sb4("al_s", [64, 8], F32)
            dt_s = sb4("dt_s", [64, 8], F32)
            g_all = sb4("g_all", [64, NC, 8], F32)
            be_all = sb4("be_all", [64, NC, 8], F32)
            ioj = sb4("ioj", [64, 64], I32)
            iof = sb4("iof", [64, 64], F32)
            tri = sb4("tri", [64, 64], F32)
            id64 = sb4("id64", [64, 64], F32)
            ng_su = sb4("ng_su", [64, 64], F32)
            ng_iu = sb4("ng_iu", [64, 64], F32)
            ng_sl = sb4("ng_sl", [64, 64], F32)
            ones_f = sb4("ones_f", [64, 128], F32)
            S = sb4("S", [128, 8, 128], F32)
            S_bf = sb4("S_bf", [128, 8, 128], BF16)
            dg = sb4("dg", [128, 96, 128], BF16)
            B_dg = Buf()
            ts_ = ExitStack()
            t1 = ts_.enter_context(nc.sbuf_tensor("t1", [64, NC, 8], F32))
            t2 = ts_.enter_context(nc.sbuf_tensor("t2", [64, NC, 8], F32))
            (B_cw, B_al, B_dt, B_g, B_be, B_t1, B_t2, B_ioj, B_iof, B_tri, B_id64, B_ngsu, B_ngiu, B_ngsl, B_onesf,
             B_S, B_Sbf) = [Buf() for _ in range(17)]
            P.op("sp", lambda e: e.dma_start(out=cw_s[:], in_=cw[:, :, :]), w=[B_cw], dma=True)
            P.op("sp", lambda e: e.dma_start(out=al_s[:], in_=alog64[:, :]), w=[B_al], dma=True)
            P.op("sp", lambda e: e.dma_start(out=dt_s[:], in_=dtb64[:, :]), w=[B_dt], dma=True)
            P.op("dve", lambda e: e.tensor_tensor(out=t1[:], in0=gabtm[:, :, 0:8],
                                                  in1=dt_s[:].unsqueeze(1).broadcast_to([64, NC, 8]), op=ALU.add),
                 r=[B_gab, B_dt], w=[B_t1])
            P.op("act", lambda e: e.activation(out=t2[:], in_=t1[:], func=AF.Abs), r=[B_t1], w=[B_t2])
            P.op("act", lambda e: e.activation(out=t2[:], in_=t2[:], func=AF.Exp, scale=-1.0), r=[B_t2], w=[B_t2])
            P.op("act", lambda e: e.activation(out=t2[:], in_=t2[:], func=AF.Ln, bias=1.0), r=[B_t2], w=[B_t2])
            P.op("dve", lambda e: e.tensor_scalar(out=t1[:], in0=t1[:], scalar1=0.0, scalar2=None, op0=ALU.max),
                 r=[B_t1], w=[B_t1])
            P.op("dve", lambda e: e.tensor_tensor(out=t1[:], in0=t1[:], in1=t2[:], op=ALU.add),
                 r=[B_t1, B_t2], w=[B_t1])
            P.op("act", lambda e: e.activation(out=al_s[:], in_=al_s[:], func=AF.Exp), r=[B_al], w=[B_al])
            P.op("dve", lambda e: e.tensor_scalar(out=al_s[:], in0=al_s[:], scalar1=-1.0, scalar2=None, op0=ALU.mult),
                 r=[B_al], w=[B_al])
            P.op("dve", lambda e: e.tensor_tensor(out=g_all[:], in0=t1[:],
                                                  in1=al_s[:].unsqueeze(1).broadcast_to([64, NC, 8]), op=ALU.mult),
                 r=[B_t1, B_al], w=[B_g])
            P.op("act", lambda e: e.activation(out=t2[:], in_=gabtm[:, :, 8:16], func=AF.Exp, scale=-1.0),
                 r=[B_gab], w=[B_t2])
            P.op("dve", lambda e: e.tensor_scalar(out=t2[:], in0=t2[:], scalar1=1.0, scalar2=None, op0=ALU.add),
                 r=[B_t2], w=[B_t2])
            P.op("dve", lambda e: e.reciprocal(out=be_all[:], in_=t2[:]), r=[B_t2], w=[B_be])
            P.op("pool", lambda e: e.iota(ioj[:], pattern=[[1, 64]], base=0, channel_multiplier=-1), w=[B_ioj])
            P.op("dve", lambda e: e.tensor_copy(out=iof[:], in_=ioj[:]), r=[B_ioj], w=[B_iof])
            P.op("dve", lambda e: e.tensor_single_scalar(out=tri[:], in_=iof[:], scalar=0.0, op=ALU.is_ge),
                 r=[B_iof], w=[B_tri])
            P.op("dve", lambda e: e.tensor_single_scalar(out=id64[:], in_=iof[:], scalar=0.0, op=ALU.is_equal),
                 r=[B_iof], w=[B_id64])
            for dst, Bd, thr, cmp in ((ng_su, B_ngsu, 1.0, ALU.is_ge), (ng_iu, B_ngiu, 0.0, ALU.is_ge),
                                      (ng_sl, B_ngsl, -1.0, ALU.is_le)):
                P.op("dve", lambda e, dst=dst, thr=thr, cmp=cmp: e.tensor_single_scalar(
                    out=dst[:], in_=iof[:], scalar=thr, op=cmp), r=[B_iof], w=[Bd])
                P.op("dve", lambda e, dst=dst: e.tensor_scalar(out=dst[:], in0=dst[:], scalar1=-1.0, scalar2=30000.0,
                                                               op0=ALU.add, op1=ALU.mult), r=[Bd], w=[Bd])
            P.op("pool", lambda e: e.memset(ones_f[:], 1.0), w=[B_onesf])
            P.op("pool", lambda e: e.memset(S[:], 0.0), w=[B_S])
            P.op("pool", lambda e: e.memset(S_bf[:], 0.0), w=[B_Sbf])
            for gi in range(24):
                for j in range(4):
                    P.op("dve", lambda e, gi=gi, j=j: e.tensor_scalar(out=dg[:, gi * 4 + j, :], in0=ident[:],
                                                                      scalar1=cw_s[:, gi, j:j + 1], scalar2=None,
                                                                      op0=ALU.mult), r=[B_ident, B_cw], w=[B_dg])
            ts_.close()
            P.barrier()


            GB = 256
            NGB = T // GB
            CPB = GB // 64
            xin = [sb4("xin%d" % i, [128, GB + 3], BF16) for i in range(8)]
            ysl16 = sb4("ysl16", [128, 16, GB], BF16)
            B_ysl16 = Buf()
            sq4 = sb4("sq4", [128, GB], BF16)
            rs4 = sb4("rs4", [128, GB], F32)
            vsT = sb4("vsT", [128, 8, GB], BF16)
            qnT = [sb4("qnT%d" % i, [128, 8, GB], BF16) for i in range(2)]
            knT = [sb4("knT%d" % i, [128, 8, GB], BF16) for i in range(2)]
            ktm = [sb4("ktm%d" % i, [64, CPB, 8, 128], BF16) for i in range(2)]
            vtm = [sb4("vtm%d" % i, [64, CPB, 8, 128], BF16) for i in range(2)]
            oTb = [sb4("oTb%d" % i, [128, 8, GB], BF16) for i in range(2)]
            B_xin = [Buf() for _ in range(8)]
            B_sq4, B_rs4, B_vsT = Buf(), Buf(), Buf()
            B_qnT = [Buf(), Buf()]
            B_knT = [Buf(), Buf()]
            B_ktm = [Buf(), Buf()]
            B_vtm = [Buf(), Buf()]
            B_oTb = [Buf(), Buf()]

            def two(name, shape, dt, n=2):
                return [sb4("%s%d" % (name, i), shape, dt) for i in range(n)], [Buf() for _ in range(n)]

            gTri, B_gTri = two("gTri", [64, 8, 64], F32)
            gcb, B_gcb = two("gcb", [128, 8, 64], F32)
            gct, B_gct = two("gct", [64, 8], F32)
            dA, B_dA = two("dA", [64, 8, 64], F32)
            dD, B_dD = two("dD", [64, 8, 64], F32)
            AmA, B_AmA = two("AmA", [64, 8, 64], BF16)
            AmB, B_AmB = two("AmB", [64, 8, 64], BF16)
            AtA, B_AtA = two("AtA", [64, 8, 64], BF16)
            AtB, B_AtB = two("AtB", [64, 8, 64], BF16)
            TtA, B_TtA = two("TtA", [64, 8, 64], BF16)
            TtB, B_TtB = two("TtB", [64, 8, 64], BF16)
            vb, B_vb = two("vb", [64, 8, 128], BF16)
            kbg, B_kbg = two("kbg", [64, 8, 128], BF16)
            s8a, B_s8a = two("s8a", [64, 8], F32)
            s8b, B_s8b = two("s8b", [64, 8], F32)
            egc, B_egc = two("egc", [128, 8, 64], F32, 3)
            attnT, B_attnT = two("attnT", [64, 8, 64], BF16, 3)
            kd, B_kd = two("kd", [64, 8, 128], BF16, 3)
            u_sb, B_u = two("u_sb", [64, 8, 128], F32, 3)
            wT_sb, B_wT = two("wT_sb", [128, 8, 64], BF16, 3)
            qdT, B_qdT = two("qdT", [128, 8, 64], BF16, 3)
            vnew = sb4("vnew", [64, 8, 128], BF16)
            B_vnew = Buf()
            gg = sb4("gg", [128, 8, GB], BF16)
            sgg = sb4("sgg", [128, 8, GB], BF16)
            o1 = sb4("o1", [128, GB], F32)
            o2 = [sb4("o2%d" % i, [128, GB], BF16) for i in range(2)]
            B_gg, B_sgg, B_o1 = Buf(), Buf(), Buf()
            sq4e = sb4("sq4e", [128, GB], BF16)
            rs4e = sb4("rs4e", [128, GB], F32)
            B_sq4e, B_rs4e = Buf(), Buf()
            B_o2 = [Buf(), Buf()]
            bk = [pp4("bk%d" % i, [128, 512]) for i in range(7)]
            B_bk = [Buf() for _ in range(7)]
            B_half = [B_bk[6], B_bk[6]]
            ptb = pp4("ptb", [128, 8, 128], BF16)
            B_ptb = Buf()
            print("GDN sbuf bytes remaining", nc.sbuf_bytes_remaining)
            pro_done = [False] * NGB
            epi_done = [False] * NGB
            prep_done = [False] * NC
            scan_done = [False] * NC
            cnt4 = {"x": 0, "o2": 0}

            def gen_pro(blk):
                bp = blk % 2
                c0 = blk * GB
                for kind in range(3):
                    for h in range(8):
                        row0 = (14 + kind * 8 + h) * 128
                        if blk == 0:
                            P.op("pool", lambda e, h=h: e.memset(xin[h][:, 0:3], 0.0), w=[B_xin[h]])
                            P.op("sp", lambda e, h=h, row0=row0: e.dma_start(out=xin[h][:, 3:GB + 3],
                                                                             in_=projT[row0:row0 + 128, 0:GB]),
                                 r=[B_proj], w=[B_xin[h]], dma=True)
                        else:
                            P.op("sp", lambda e, h=h, row0=row0: e.dma_start(out=xin[h][:],
                                                                             in_=projT[row0:row0 + 128, c0 - 3:c0 + GB]),
                                 r=[B_proj], w=[B_xin[h]], dma=True)
                    yield
                    for h in range(8):
                        gi = kind * 8 + h
                        hf = h % 2
                        cvh = bk[6][:, hf * 256:hf * 256 + GB]
                        for j in range(4):
                            P.op("pe", lambda e, j=j: e.matmul(cvh, lhsT=dg[:, gi * 4 + j, :], rhs=xin[h][:, j:j + GB],
                                                               start=(j == 0), stop=(j == 3)),
                                 r=[B_dg, B_xin[h]], w=[B_half[hf]], inc=(j == 3))
                        if kind == 2:
                            P.op("act", lambda e: e.activation(out=vsT[:, h, :], in_=cvh, func=AF.Silu),
                                 r=[B_half[hf]], w=[B_vsT])
                        else:
                            P.op("act", lambda e: e.activation(out=ysl16[:, gi, :], in_=cvh, func=AF.Silu),
                                 r=[B_half[hf]], w=[B_ysl16])
                    yield
                for kind in range(2):
                    for h in range(8):
                        gi = kind * 8 + h
                        P.op("act", lambda e: e.activation(out=sq4[:], in_=ysl16[:, gi, :], func=AF.Square),
                             r=[B_ysl16], w=[B_sq4])
                        yield
                        P.op("pe", lambda e: e.matmul(bk[6][:, 0:GB], lhsT=ones_b[:], rhs=sq4[:], start=True, stop=True),
                             r=[B_ones, B_sq4], w=[B_half[0]])
                        P.op("act", lambda e: e.activation(out=rs4[:], in_=bk[6][:, 0:GB], func=AF.Ln, bias=EPS),
                             r=[B_half[0]], w=[B_rs4])
                        P.op("act", lambda e: e.activation(
                            out=rs4[:], in_=rs4[:], func=AF.Exp, scale=-0.5,
                            bias=(-0.5 * float(np.log(128.0)) if kind == 0 else 0.0)), r=[B_rs4], w=[B_rs4])
                        dstT, Bd = (qnT[bp], B_qnT[bp]) if kind == 0 else (knT[bp], B_knT[bp])
                        P.op("dve", lambda e: e.tensor_tensor(out=dstT[:, h, :], in0=ysl16[:, gi, :], in1=rs4[:], op=ALU.mult),
                             r=[B_ysl16, B_rs4], w=[Bd])
                        yield
                for srcT, Bs, dstm, Bdm in ((knT[bp], B_knT[bp], ktm[bp], B_ktm[bp]), (vsT, B_vsT, vtm[bp], B_vtm[bp])):
                    for ci in range(CPB):
                        for h in range(8):
                            P.op("pe", lambda e, h=h: e.transpose(
                                out=ptb[0:64, h, :], in_=srcT[:, h, ci * 64:(ci + 1) * 64], identity=ident[:]),
                                r=[Bs, B_ident], w=[B_ptb], inc=(h == 7))
                        P.op("act", lambda e: e.activation(out=dstm[:, ci, :, :], in_=ptb[0:64, :, :], func=AF.Copy),
                             r=[B_ptb], w=[Bdm])
                        yield
                pro_done[blk] = True

            def gen_prep(n):
                blk, ci = divmod(n, CPB)
                bp = blk % 2
                p = n % 2
                q4 = n % 3
                X0, X1, X2 = (bk[0], bk[1], bk[2]) if p == 0 else (bk[3], bk[4], bk[5])
                BX0, BX1, BX2 = (B_bk[0], B_bk[1], B_bk[2]) if p == 0 else (B_bk[3], B_bk[4], B_bk[5])
                cs = slice(ci * 64, ci * 64 + 64)
                gsl = g_all[:, n, :]
                bsl = be_all[:, n, :]
                KN, QN = knT[bp], qnT[bp]
                P.op("dve", lambda e: e.tensor_tensor(out=gTri[p][:], in0=bch(tri[:]), in1=bc3(gsl, 64), op=ALU.mult),
                     r=[B_tri, B_g], w=[B_gTri[p]])
                yield
                P.op("pe", lambda e: e.matmul(X0[:], lhsT=ones_f[:], rhs=gTri[p][:].rearrange("p a b -> p (a b)"),
                                              start=True, stop=True), r=[B_onesf, B_gTri[p]], w=[BX0])
                P.op("pe", lambda e: e.matmul(X1[0:64, 0:8], lhsT=tri[:], rhs=gsl, start=True, stop=True),
                     r=[B_tri, B_g], w=[BX1])
                for h in range(8):
                    P.op("pe", lambda e, h=h: e.matmul(X2[0:64, h * 64:(h + 1) * 64], lhsT=KN[:, h, cs], rhs=KN[:, h, cs],
                                                       start=True, stop=True), r=[B_knT[bp]], w=[BX2], inc=(h == 7))
                yield
                P.op("act", lambda e: e.activation(out=gcb[p][:].rearrange("p a b -> p (a b)"), in_=X0[:], func=AF.Copy),
                     r=[BX0], w=[B_gcb[p]])
                P.op("act", lambda e: e.activation(out=egc[q4][:].rearrange("p a b -> p (a b)"), in_=X0[:], func=AF.Exp),
                     r=[BX0], w=[B_egc[q4]])
                P.op("dve", lambda e: e.tensor_copy(out=gct[p][:], in_=X1[0:64, 0:8]), r=[BX1], w=[B_gct[p]])
                yield
                for h in range(8):
                    P.op("pe", lambda e, h=h: e.matmul(X0[0:64, h * 64:(h + 1) * 64], lhsT=KN[:, h, cs], rhs=QN[:, h, cs],
                                                       start=True, stop=True), r=[B_knT[bp], B_qnT[bp]], w=[BX0], inc=(h == 7))
                P.op("dve", lambda e: e.tensor_tensor(out=dA[p][:], in0=bc3(gct[p][:], 64), in1=gcb[p][0:64, :, :],
                                                      op=ALU.subtract), r=[B_gct[p], B_gcb[p]], w=[B_dA[p]])
                P.op("dve", lambda e: e.scalar_tensor_tensor(out=dA[p][:], in0=dA[p][:], scalar=0.0, in1=bch(ng_sl[:]),
                                                             op0=ALU.min, op1=ALU.add), r=[B_dA[p], B_ngsl], w=[B_dA[p]])
                P.op("dve", lambda e: e.tensor_tensor(out=dD[p][:], in0=gcb[p][0:64, :, :], in1=bc3(gct[p][:], 64),
                                                      op=ALU.subtract), r=[B_gct[p], B_gcb[p]], w=[B_dD[p]])
                P.op("dve", lambda e: e.scalar_tensor_tensor(out=dD[p][:], in0=dD[p][:], scalar=0.0, in1=bch(ng_iu[:]),
                                                             op0=ALU.min, op1=ALU.add), r=[B_dD[p], B_ngiu], w=[B_dD[p]])
                yield
                P.op("act", lambda e: e.activation(out=dA[p][:], in_=dA[p][:], func=AF.Exp), r=[B_dA[p]], w=[B_dA[p]])
                P.op("act", lambda e: e.activation(out=dD[p][:], in_=dD[p][:], func=AF.Exp), r=[B_dD[p]], w=[B_dD[p]])
                P.op("act", lambda e: e.activation(out=s8a[p][:], in_=gct[p][:], func=AF.Exp), r=[B_gct[p]], w=[B_s8a[p]])
                P.op("dve", lambda e: e.tensor_tensor(out=s8b[p][:], in0=gcb[p][0:64, :, 63], in1=gct[p][:], op=ALU.subtract),
                     r=[B_gcb[p], B_gct[p]], w=[B_s8b[p]])
                P.op("act", lambda e: e.activation(out=s8b[p][:], in_=s8b[p][:], func=AF.Exp), r=[B_s8b[p]], w=[B_s8b[p]])
                yield
                P.op("dve", lambda e: e.tensor_tensor(out=dA[p][:], in0=dA[p][:], in1=bc3(bsl, 64), op=ALU.mult),
                     r=[B_dA[p], B_be], w=[B_dA[p]])
                P.op("dve", lambda e: e.tensor_tensor(out=AmA[p][:], in0=X2[0:64, :].rearrange("p (a b) -> p a b", a=8),
                                                      in1=dA[p][:], op=ALU.mult), r=[BX2, B_dA[p]], w=[B_AmA[p]])
                P.op("dve", lambda e: e.tensor_tensor(out=attnT[q4][:], in0=X0[0:64, :].rearrange("p (a b) -> p a b", a=8),
                                                      in1=dD[p][:], op=ALU.mult), r=[BX0, B_dD[p]], w=[B_attnT[q4]])
                yield
                for h in range(8):
                    P.op("pe", lambda e, h=h: e.transpose(out=ptb[0:64, h, 0:64], in_=AmA[p][:, h, :],
                                                          identity=ident[0:64, 0:64]),
                         r=[B_AmA[p], B_ident], w=[B_ptb], inc=(h == 7))
                S8 = os.environ.get("GDN_S8", "atvs")
                if "a" in S8:
                    P.op("act", lambda e: e.activation(out=AtA[p][:], in_=ptb[0:64, :, 0:64], func=AF.Copy),
                         r=[B_ptb], w=[B_AtA[p]])
                if "t" in S8:
                    P.op("dve", lambda e: e.tensor_tensor(out=TtA[p][:], in0=bch(id64[:]), in1=AtA[p][:],
                                                          op=ALU.subtract), r=[B_id64, B_AtA[p]], w=[B_TtA[p]])
                if "v" in S8:
                    P.op("dve", lambda e: e.tensor_tensor(out=vb[p][:], in0=vtm[bp][:, ci, :, :], in1=bc3(bsl, 128), op=ALU.mult),
                         r=[B_vtm[bp], B_be], w=[B_vb[p]])
                if "s" in S8:
                    P.op("dve", lambda e: e.tensor_tensor(out=s8a[p][:], in0=s8a[p][:], in1=bsl, op=ALU.mult),
                         r=[B_s8a[p], B_be], w=[B_s8a[p]])
                yield
                Am = [(AmA[p], B_AmA[p]), (AmB[p], B_AmB[p])]
                At = [(AtA[p], B_AtA[p]), (AtB[p], B_AtB[p])]
                Tt = [(TtA[p], B_TtA[p]), (TtB[p], B_TtB[p])]
                cur = 0
                tcur = 0
                for lvl in range(5):
                    nxt = 1 - cur
                    (Ac, BAc), (Atc, BAtc) = Am[cur], At[cur]
                    (An, BAn), (Atn, BAtn) = Am[nxt], At[nxt]
                    (Tc, BTc), (Tn, BTn) = Tt[tcur], Tt[1 - tcur]
                    for h in range(8):
                        P.op("pe", lambda e, h=h: e.matmul(X0[0:64, h * 64:(h + 1) * 64], lhsT=Atc[:, h, :], rhs=Ac[:, h, :],
                                                           start=True, stop=True), r=[BAtc, BAc], w=[BX0], inc=(h == 7))
                    if lvl < 4:
                        for h in range(8):
                            P.op("pe", lambda e, h=h: e.matmul(X1[0:64, h * 64:(h + 1) * 64], lhsT=Ac[:, h, :], rhs=Atc[:, h, :],
                                                               start=True, stop=True), r=[BAtc, BAc], w=[BX1], inc=(h == 7))
                    yield
                    P.op("act", lambda e: e.activation(out=An[:].rearrange("p a b -> p (a b)"), in_=X0[0:64, :], func=AF.Copy),
                         r=[BX0], w=[BAn])
                    if lvl < 4:
                        P.op("dve", lambda e: e.tensor_copy(out=Atn[:].rearrange("p a b -> p (a b)"), in_=X1[0:64, :]),
                             r=[BX1], w=[BAtn])
                    if lvl == 0:
                        P.op("dve", lambda e: e.tensor_tensor(out=kbg[p][:], in0=ktm[bp][:, ci, :, :], in1=bc3(s8a[p][:], 128),
                                                              op=ALU.mult), r=[B_ktm[bp], B_s8a[p]], w=[B_kbg[p]])
                    if lvl == 1:
                        P.op("dve", lambda e: e.tensor_tensor(out=kd[q4][:], in0=ktm[bp][:, ci, :, :], in1=bc3(s8b[p][:], 128),
                                                              op=ALU.mult), r=[B_ktm[bp], B_s8b[p]], w=[B_kd[q4]])
                    if lvl == 2:
                        P.op("dve", lambda e: e.tensor_tensor(out=qdT[q4][:], in0=QN[:, :, cs], in1=egc[q4][:], op=ALU.mult),
                             r=[B_qnT[bp], B_egc[q4]], w=[B_qdT[q4]])
                    yield
                    for h in range(8):
                        P.op("pe", lambda e, h=h: e.matmul(X2[0:64, h * 64:(h + 1) * 64], lhsT=An[:, h, :], rhs=Tc[:, h, :],
                                                           start=True, stop=True), r=[BAn, BTc], w=[BX2], inc=(h == 7))
                    yield
                    P.op("dve", lambda e: e.tensor_tensor(out=Tn[:].rearrange("p a b -> p (a b)"), in0=X2[0:64, :],
                                                          in1=Tc[:].rearrange("p a b -> p (a b)"), op=ALU.add),
                         r=[BX2, BTc], w=[BTn])
                    yield
                    cur = nxt
                    tcur = 1 - tcur
                TT, B_TT = Tt[tcur]
                for h in range(8):
                    Xu, BXu = (X0, BX0) if h < 4 else (X1, BX1)
                    P.op("pe", lambda e, h=h, Xu=Xu: e.matmul(Xu[0:64, (h % 4) * 128:(h % 4 + 1) * 128], lhsT=TT[:, h, :],
                                                              rhs=vb[p][:, h, :], start=True, stop=True),
                         r=[B_TT, B_vb[p]], w=[BXu], inc=(h % 4 == 3))
                for h in range(8):
                    P.op("pe", lambda e, h=h: e.matmul(X2[:, h * 64:(h + 1) * 64], lhsT=kbg[p][:, h, :], rhs=TT[:, h, :],
                                                       start=True, stop=True), r=[B_kbg[p], B_TT], w=[BX2], inc=(h == 7))
                yield
                for hb, (Xu, BXu) in enumerate(((X0, BX0), (X1, BX1))):
                    P.op("act", lambda e, hb=hb, Xu=Xu: e.activation(
                        out=u_sb[q4][:, hb * 4:(hb + 1) * 4, :].rearrange("p a b -> p (a b)"), in_=Xu[0:64, :],
                        func=AF.Copy), r=[BXu], w=[B_u[q4]])
                P.op("act", lambda e: e.activation(out=wT_sb[q4][:].rearrange("p a b -> p (a b)"), in_=X2[:], func=AF.Copy),
                     r=[BX2], w=[B_wT[q4]])
                yield
                prep_done[n] = True

            def gen_scan(n):
                blk, ci = divmod(n, CPB)
                bp = blk % 2
                q4 = n % 3
                cs = slice(ci * 64, ci * 64 + 64)
                X = bk[6]
                BXs = [B_bk[6]]
                for hb in range(2):
                    hs = range(hb * 4, hb * 4 + 4)
                    for h in hs:
                        P.op("pe", lambda e, h=h: e.matmul(X[0:64, (h % 4) * 128:(h % 4 + 1) * 128], lhsT=wT_sb[q4][:, h, :],
                                                           rhs=S_bf[:, h, :], start=True, stop=True),
                             r=[B_wT[q4], B_Sbf], w=BXs, inc=(h % 4 == 3))
                    P.op("dve", lambda e: e.tensor_tensor(
                        out=vnew[:, hb * 4:(hb + 1) * 4, :].rearrange("p a b -> p (a b)"),
                        in0=u_sb[q4][:, hb * 4:(hb + 1) * 4, :].rearrange("p a b -> p (a b)"), in1=X[0:64, :],
                        op=ALU.subtract), r=[B_u[q4]] + BXs, w=[B_vnew])
                    yield
                    for h in hs:
                        P.op("pe", lambda e, h=h: e.matmul(X[:, (h % 4) * 64:(h % 4 + 1) * 64], lhsT=S_bf[:, h, :],
                                                           rhs=qdT[q4][:, h, :], start=True, stop=False),
                             r=[B_Sbf, B_qdT[q4]], w=BXs, inc=False)
                        P.op("pe", lambda e, h=h: e.matmul(X[:, (h % 4) * 64:(h % 4 + 1) * 64], lhsT=vnew[:, h, :],
                                                           rhs=attnT[q4][:, h, :], start=False, stop=True),
                             r=[B_vnew, B_attnT[q4]], w=BXs, inc=(h % 4 == 3))
                    P.op("act", lambda e: e.activation(out=oTb[bp][:, hb * 4:(hb + 1) * 4, cs],
                                                       in_=X[:, 0:256].rearrange("p (a b) -> p a b", a=4), func=AF.Copy),
                         r=BXs, w=[B_oTb[bp]])
                    yield
                    for h in hs:
                        P.op("pe", lambda e, h=h: e.matmul(X[:, (h % 4) * 128:(h % 4 + 1) * 128], lhsT=kd[q4][:, h, :],
                                                           rhs=vnew[:, h, :], start=True, stop=True),
                             r=[B_kd[q4], B_vnew], w=BXs, inc=(h % 4 == 3))
                    P.op("dve", lambda e: e.tensor_tensor(
                        out=S[:, hb * 4:(hb + 1) * 4, :], in0=S[:, hb * 4:(hb + 1) * 4, :],
                        in1=egc[q4][:, hb * 4:(hb + 1) * 4, 63:64].broadcast_to([128, 4, 128]), op=ALU.mult),
                        r=[B_S, B_egc[q4]], w=[B_S])
                    P.op("dve", lambda e: e.tensor_tensor(
                        out=S[:, hb * 4:(hb + 1) * 4, :].rearrange("p a b -> p (a b)"),
                        in0=S[:, hb * 4:(hb + 1) * 4, :].rearrange("p a b -> p (a b)"), in1=X[:], op=ALU.add),
                        r=[B_S] + BXs, w=[B_S])
                    yield
                    P.op("act", lambda e: e.activation(out=S_bf[:, hb * 4:(hb + 1) * 4, :], in_=S[:, hb * 4:(hb + 1) * 4, :],
                                                       func=AF.Copy), r=[B_S], w=[B_Sbf])
                    yield
                scan_done[n] = True

            def gen_epi(blk):
                bp = blk % 2
                c0 = blk * GB
                P.op("sp", lambda e: e.dma_start(
                    out=gg[:], in_=projT[38 * 128:46 * 128, c0:c0 + GB].rearrange("(h p) t -> p h t", p=128)),
                    r=[B_proj], w=[B_gg], dma=True)
                yield
                P.op("act", lambda e: e.activation(out=sgg[:], in_=gg[:], func=AF.Silu), r=[B_gg], w=[B_sgg])
                yield
                for h in range(8):
                    P.op("act", lambda e: e.activation(out=sq4e[:], in_=oTb[bp][:, h, :], func=AF.Square),
                         r=[B_oTb[bp]], w=[B_sq4e])
                    yield
                    P.op("pe", lambda e: e.matmul(bk[6][:, 256:256 + GB], lhsT=ones_b[:], rhs=sq4e[:], start=True, stop=True),
                         r=[B_ones, B_sq4e], w=[B_half[1]])
                    P.op("act", lambda e: e.activation(out=rs4e[:], in_=bk[6][:, 256:256 + GB], func=AF.Ln, scale=1.0 / 128,
                                                       bias=EPS), r=[B_half[1]], w=[B_rs4e])
                    P.op("act", lambda e: e.activation(out=rs4e[:], in_=rs4e[:], func=AF.Exp, scale=-0.5),
                         r=[B_rs4e], w=[B_rs4e])
                    yield
                    P.op("dve", lambda e: e.scalar_tensor_tensor(out=o1[:], in0=oTb[bp][:, h, :], scalar=gv[:, 6:7], in1=rs4e[:],
                                                                 op0=ALU.mult, op1=ALU.mult),
                         r=[B_oTb[bp], B_gv, B_rs4e], w=[B_o1])
                    ok = cnt4["o2"] % 2
                    cnt4["o2"] += 1
                    P.op("dve", lambda e: e.tensor_tensor(out=o2[ok][:], in0=o1[:], in1=sgg[:, h, :], op=ALU.mult),
                         r=[B_o1, B_sgg], w=[B_o2[ok]])
                    P.op("pool", lambda e: e.dma_start(out=mixT[1024 + h * 128:1024 + (h + 1) * 128, c0:c0 + GB], in_=o2[ok][:]),
                         r=[B_o2[ok]], w=[B_mix], dma=True)
                    yield
                epi_done[blk] = True

            def chunks_of(b):
                return range(b * CPB, (b + 1) * CPB)

            def stream_prep(par):
                for n in range(par, NC, 2):
                    while not pro_done[n // CPB] or (n >= 3 and not scan_done[n - 3]):
                        yield "blocked"
                    yield from gen_prep(n)

            def stream_scan():
                for n in range(NC):
                    b = n // CPB
                    while not prep_done[n] or (b >= 2 and not epi_done[b - 2]):
                        yield "blocked"
                    yield from gen_scan(n)

            def stream_pro():
                for b in range(NGB):
                    while b >= 2 and not all(prep_done[c] for c in chunks_of(b - 2)):
                        yield "blocked"
                    yield from gen_pro(b)

            def stream_epi():
                for b in range(NGB):
                    while not scan_done[b * CPB + CPB - 1]:
                        yield "blocked"
                    yield from gen_epi(b)

            if os.environ.get("GDN_SEQ"):
                for b in range(NGB):
                    for _ in gen_pro(b):
                        pass
                    for n in chunks_of(b):
                        if not os.environ.get("GDN_NOPREP"):
                            kmax = int(os.environ.get("GDN_PREP_STEPS", "1000"))
                            for k_, _ in enumerate(gen_prep(n)):
                                if k_ + 1 >= kmax:
                                    break
                        if not os.environ.get("GDN_NOSCAN"):
                            for _ in gen_scan(n):
                                pass
                    if not os.environ.get("GDN_NOEPI"):
                        for _ in gen_epi(b):
                            pass
                streams = []
            else:
                streams = [stream_pro(), stream_prep(0), stream_prep(1), stream_scan(), stream_epi()]
            while streams:
                progressed = False
                for st in list(streams):
                    try:
                        r_ = next(st)
                        if r_ != "blocked":
                            progressed = True
                    except StopIteration:
                        streams.remove(st)
                        progressed = True
                assert progressed, "GDN scheduler deadlock"

        P.barrier()
        with ExitStack() as ps:
            def sb5(name, shape, dt):
                return ps.enter_context(nc.sbuf_tensor(name, shape, dt))

            wob = sb5("wob", [128, 16, 2048], BF16)
            wof = [sb5("wof%d" % i, [128, 2, 2048], F32) for i in range(2)]
            mt = [sb5("mt%d" % i, [128, 16, 512], BF16) for i in range(2)]
            xr = [sb5("xr%d" % i, [128, 2048], F32) for i in range(2)]
            yo = [sb5("yo%d" % i, [128, 2048], F32) for i in range(2)]
            B_wobq = [Buf() for _ in range(8)]
            B_wof = [Buf(), Buf()]
            B_mt = [Buf(), Buf()]
            B_xr = [Buf(), Buf()]
            B_yo = [Buf(), Buf()]
            B_outs = [Buf() for _ in range(4)]
            po = [ps.enter_context(nc.psum_tensor("po%d" % i, [128, 512], F32)) for i in range(8)]
            B_po = [Buf() for _ in range(8)]
            for q in range(8):
                k = q % 2
                P.op("sp", lambda e, k=k, q=q: e.dma_start(out=wof[k][:], in_=wout[:, 2 * q:2 * q + 2, :]),
                     w=[B_wof[k]], dma=True)
                P.op("act", lambda e, k=k, q=q: e.activation(out=wob[:, 2 * q:2 * q + 2, :], in_=wof[k][:], func=AF.Copy),
                     r=[B_wof[k]], w=[B_wobq[q]])
            npo = 0
            for t4 in range(T // 512):
                mk = t4 % 2
                P.op("sp", lambda e, mk=mk, t4=t4: e.dma_start(
                    out=mt[mk][:], in_=mixT[:, t4 * 512:(t4 + 1) * 512].rearrange("(c p) t -> p c t", p=128)),
                    r=[B_mix], w=[B_mt[mk]], dma=True)
                for tt in range(4):
                    tg = t4 * 4 + tt
                    k = tg % 2
                    ts_ = slice(tg * 128, (tg + 1) * 128)
                    P.op("sp", lambda e, k=k, ts_=ts_: e.dma_start(out=xr[k][:], in_=x[ts_, :]), w=[B_xr[k]], dma=True)
                    for nq in range(4):
                        pk = npo % 8
                        npo += 1
                        for c in range(16):
                            P.op("pe", lambda e, pk=pk, nq=nq, c=c, tt=tt, mk=mk: e.matmul(
                                po[pk][:], lhsT=mt[mk][:, c, tt * 128:(tt + 1) * 128], rhs=wob[:, c, nq * 512:(nq + 1) * 512],
                                start=(c == 0), stop=(c == 15)), r=[B_mt[mk], B_wobq[c // 2]], w=[B_po[pk]], inc=(c == 15))
                        P.op("dve", lambda e, k=k, nq=nq, pk=pk: e.tensor_tensor(
                            out=yo[k][:, nq * 512:(nq + 1) * 512], in0=po[pk][:], in1=xr[k][:, nq * 512:(nq + 1) * 512],
                            op=ALU.add), r=[B_po[pk], B_xr[k]], w=[B_yo[k]])
                    P.op("pool", lambda e, k=k, ts_=ts_: e.dma_start(out=out[ts_, :], in_=yo[k][:]),
                         r=[B_yo[k]], w=[B_outs[tg % 4]], dma=True)
            P.wait_all("pool", B_outs + [B_mix])
        final = {P.esem[k]: P.cnt[k] for k in P.eng}
        for q in P.dsems:
            for s_, v_ in zip(P.dsems[q], P.dval[q]):
                final[s_] = v_
        for s_, v_ in P.maxwait.items():
            assert final.get(s_, 0) >= v_, ("unreachable wait", s_, v_, final.get(s_))
        print("ops:", P.cnt, "waits:", P.nwaits, "pend:", P.pend)
    return nc


COL_SLICES = {
    "cq": (0, 512), "ckv": (512, 768), "krope": (768, 832), "mgate": (832, 1856), "gq": (1856, 2880),
    "gk": (2880, 3904), "gv": (3904, 4928), "ga": (4928, 4936), "gb": (4936, 4944), "ggate": (4944, 5968),
}


def layout_inputs(T, x_b, pos_b, inputs):
    w_in = inputs["w_in"][0]
    cs = COL_SLICES

    def cols(name):
        a, b = cs[name]
        return w_in[:, a:b]

    kr = cols("krope")
    kr_perm = np.concatenate([kr[:, 32:64], kr[:, 0:32]], axis=1)
    wr = np.concatenate([cols("cq"), cols("ckv"), cols("mgate"), cols("gq"), cols("gk"), cols("gv"),
                         cols("ggate"), kr, kr_perm], axis=1)
    win = np.ascontiguousarray(wr.reshape(16, 128, NG, 128).transpose(2, 1, 0, 3))
    wab = np.concatenate([cols("ga"), cols("gb")], axis=1)
    wab = np.ascontiguousarray(wab.reshape(16, 128, 16).transpose(1, 0, 2))
    ngain = np.ascontiguousarray(inputs["norm_gain"][0].reshape(16, 128).T)
    mla_q = inputs["mla_q_norm_gain"][0]
    mla_k = inputs["mla_k_norm_gain"][0]
    wuq_o = inputs["w_uq"][0]
    parts = []
    for h in range(8):
        blk = wuq_o[:, h * 192:(h + 1) * 192]
        rp = blk[:, 128:192]
        parts += [blk[:, 0:128], rp, np.concatenate([rp[:, 32:64], rp[:, 0:32]], axis=1)]
    wuq = np.concatenate(parts, axis=1)
    wuq = np.ascontiguousarray(wuq.reshape(4, 128, 2048).transpose(1, 0, 2))
    wukv_o = inputs["w_ukv"][0]
    kn = [wukv_o[:, h * 256:h * 256 + 128] for h in range(8)]
    vv = [wukv_o[:, h * 256 + 128:(h + 1) * 256] for h in range(8)]
    wukv = np.concatenate(kn + vv, axis=1)
    wukv = np.ascontiguousarray(wukv.reshape(2, 128, 2048).transpose(1, 0, 2))
    qag = np.ascontiguousarray(inputs["mla_q_a_gain"][0].reshape(4, 128).T)
    kvag = np.ascontiguousarray(inputs["mla_kv_a_gain"][0].reshape(2, 128).T)
    gvec = np.zeros((128, 8), np.float32)

    def perm(v):
        return np.concatenate([v[32:64], v[0:32]])

    gvec[:, 0] = mla_q[0:128]
    gvec[0:64, 1] = mla_q[128:192]
    gvec[0:64, 2] = perm(mla_q[128:192])
    gvec[:, 3] = mla_k[0:128]
    gvec[0:64, 4] = mla_k[128:192]
    gvec[0:64, 5] = perm(mla_k[128:192])
    gvec[:, 6] = inputs["gdn_out_norm_gain"][0]
    cwo = inputs["gdn_conv_w"][0]
    cw = np.ascontiguousarray(cwo.reshape(4, 24, 128).transpose(2, 1, 0))
    m = {"x": np.ascontiguousarray(x_b[:T]), "win": win, "wab": wab, "ngain": ngain,
         "pos64": np.ascontiguousarray(np.broadcast_to(pos_b[None, :T], (64, T))),
         "wuq": wuq, "wukv": wukv, "qag": qag, "kvag": kvag, "gvec": gvec, "cw": cw,
         "alog64": np.ascontiguousarray(np.broadcast_to(inputs["gdn_a_log"][0][None, :], (64, 8))),
         "dtb64": np.ascontiguousarray(np.broadcast_to(inputs["gdn_dt_bias"][0][None, :], (64, 8))),
         "wout": np.ascontiguousarray(inputs["w_out"][0].reshape(16, 128, 2048).transpose(1, 0, 2))}
    return m


_NC_CACHE = {}


def kernel(**inputs):
    T = inputs["x"].shape[1]
    B = inputs["x"].shape[0]
    if T not in _NC_CACHE:
        _NC_CACHE[T] = build_nc(T)
    nc = _NC_CACHE[T]
    inputs = {k: np.asarray(v) for k, v in inputs.items()}
    maps = [layout_inputs(T, inputs["x"][c % B], inputs["positions"][c % B], inputs) for c in range(8)]
    res = run_bass_kernel_spmd(nc, maps, core_ids=list(range(8)))
    return np.stack([np.asarray(res.results[b]["out"]) for b in range(B)], axis=0).astype(np.float32)
```

```python
import os
import numpy as np
from contextlib import ExitStack
import concourse.bass as bass
import concourse.mybir as mybir
from concourse.bass_utils import run_bass_kernel_spmd

F32 = mybir.dt.float32
BF16 = mybir.dt.bfloat16
I32 = mybir.dt.int32
ALU = mybir.AluOpType
AF = mybir.ActivationFunctionType
AX = mybir.AxisListType

D = 2048
NG = 47
EPS = 1e-6
KDMA = 8


class Buf:
    __slots__ = ("name", "w", "r")

    def __init__(self, name=""):
        self.name = name
        self.w = None
        self.r = {}


class Prog:
    def __init__(self, nc, es):
        self.nc = nc
        self.eng = {"pe": nc.tensor, "act": nc.scalar, "dve": nc.vector, "pool": nc.gpsimd, "sp": nc.sync}
        self.esem = {k: es.enter_context(nc.semaphore("s_" + k)) for k in self.eng}
        self.cnt = {k: 0 for k in self.eng}
        self.pend = {k: False for k in self.eng}
        self.seen = {k: {} for k in self.eng}
        self.dsems = {k: [es.enter_context(nc.semaphore("d_%s%d" % (k, i))) for i in range(KDMA)]
                      for k in ("sp", "pool", "act")}
        self.dval = {k: [0] * KDMA for k in self.dsems}
        self.dslot = {k: 0 for k in self.dsems}
        self.nwaits = 0
        self.maxwait = {}

    def op(self, e, fn, r=(), w=(), dma=False, inc=True):
        need = {}

        def add(t):
            if t is not None and need.get(t[0], 0) < t[1]:
                need[t[0]] = t[1]

        for b in r:
            add(b.w)
        for b in w:
            add(b.w)
            for s, v in b.r.items():
                add((s, v))
        if dma:
            slot = self.dslot[e]
            self.dslot[e] = (slot + 1) % KDMA
            sem = self.dsems[e][slot]
            prev = self.dval[e][slot]
            if prev > 0:
                add((sem, prev))
            self.dval[e][slot] = prev + 16
            tok = (sem, prev + 16)
        else:
            if inc:
                self.cnt[e] += 1
                self.pend[e] = False
                tok = (self.esem[e], self.cnt[e])
            else:
                self.pend[e] = True
                tok = (self.esem[e], self.cnt[e] + 1)
        seen = self.seen[e]
        eo = self.eng[e]
        for s, v in need.items():
            if e == "pe" and s is self.esem["pe"]:
                continue
            if seen.get(s, 0) >= v:
                continue
            seen[s] = v
            eo.wait_ge(s, v)
            self.nwaits += 1
            if self.maxwait.get(s, 0) < v:
                self.maxwait[s] = v
        ins = fn(eo)
        if dma:
            ins.then_inc(tok[0], 16)
        elif inc:
            ins.then_inc(tok[0], 1)
        for b in r:
            if b.r.get(tok[0], 0) < tok[1]:
                b.r[tok[0]] = tok[1]
        for b in w:
            b.w = tok
            b.r = {}
        return tok

    def barrier(self):
        for e, eo in self.eng.items():
            assert not self.pend[e]
        for e, eo in self.eng.items():
            seen = self.seen[e]
            for e2 in self.eng:
                s, v = self.esem[e2], self.cnt[e2]
                if v > 0 and seen.get(s, 0) < v and not (e == e2 and e == "pe"):
                    seen[s] = v
                    eo.wait_ge(s, v)
            for q in self.dsems:
                for s, v in zip(self.dsems[q], self.dval[q]):
                    if v > 0 and seen.get(s, 0) < v:
                        seen[s] = v
                        eo.wait_ge(s, v)

    def wait_all(self, e, bufs):
        eo = self.eng[e]
        need = {}
        for b in bufs:
            for t in ([b.w] if b.w else []) + list(b.r.items()):
                if need.get(t[0], 0) < t[1]:
                    need[t[0]] = t[1]
        for s, v in need.items():
            eo.wait_ge(s, v)


def build_nc(T, dbg=None):
    NT = T // 128
    TSB = min(T, 2048)
    NSB = T // TSB
    NTB = TSB // 512
    nc = bass.Bass("TRN2", target_bir_lowering=False)

    def din(name, shape, dt=F32):
        return nc.dram_tensor(name, shape, dt, kind="ExternalInput").ap()

    x = din("x", [T, D])
    win = din("win", [NG, 128, 16, 128])
    wab = din("wab", [128, 16, 16])
    ngain = din("ngain", [128, 16])
    out = nc.dram_tensor("out", [T, D], F32, kind="ExternalOutput").ap()
    projT = nc.dram_tensor("projT", [NG * 128, T], BF16,
                           kind="Internal").ap()
    pos64 = din("pos64", [64, T], I32)
    wuq = din("wuq", [128, 4, 2048])
    wukv = din("wukv", [128, 2, 2048])
    qag = din("qag", [128, 4])
    kvag = din("kvag", [128, 2])
    gvec = din("gvec", [128, 8])
    cw = din("cw", [128, 24, 4])
    alog64 = din("alog64", [64, 8])
    dtb64 = din("dtb64", [64, 8])
    wout = din("wout", [128, 16, 2048])
    okind = "ExternalOutput" if dbg else "Internal"
    QT = nc.dram_tensor("QT", [8, 192, T], BF16, kind="Internal").ap()
    KT = nc.dram_tensor("KT", [8, 192, T], BF16, kind="Internal").ap()
    Vs = nc.dram_tensor("Vs", [T, 1024], BF16, kind="Internal").ap()
    mixT = nc.dram_tensor("mixT", [2048, T], BF16, kind=okind).ap()

    with ExitStack() as es:
        P = Prog(nc, es)

        def sb(name, shape, dt):
            return es.enter_context(nc.sbuf_tensor(name, shape, dt))

        ident = sb("ident", [128, 128], BF16)
        iot = sb("iot", [128, 128], I32)
        gabtm = sb("gabtm", [64, 2 * NT, 16], F32)
        ngs = sb("ngs", [128, 16], F32)
        B_ident, B_iot, B_gab, B_ngs = Buf("ident"), Buf("iot"), Buf("gab"), Buf("ngs")
        P.op("pool", lambda e: e.iota(iot[:], pattern=[[1, 128]], base=0, channel_multiplier=-1), w=[B_iot])
        P.op("dve", lambda e: e.tensor_single_scalar(out=ident[:], in_=iot[:], scalar=0, op=ALU.is_equal),
             r=[B_iot], w=[B_ident])
        P.op("sp", lambda e: e.dma_start(out=ngs[:], in_=ngain[:, :]), w=[B_ngs], dma=True)

        with ExitStack() as ps:
            def sb1(name, shape, dt):
                return ps.enter_context(nc.sbuf_tensor(name, shape, dt))

            def pp1(name, shape, dt):
                return ps.enter_context(nc.psum_tensor(name, shape, dt))

            xt = [sb1("xt%d" % i, [128, D], F32) for i in range(2)]
            xs = [sb1("xs%d" % i, [128, D], BF16) for i in range(2)]
            junk = sb1("junk", [128, D], BF16)
            ss = [sb1("ss%d" % i, [128, 1], F32) for i in range(2)]
            rs = [sb1("rs%d" % i, [128, 1], F32) for i in range(2)]
            xnT = sb1("xnT", [128, 16, TSB], BF16)
            wf = [sb1("wf%d" % i, [128, 16, 128], F32) for i in range(2)]
            wb = [sb1("wb%d" % i, [128, 16, 128], BF16) for i in range(2)]
            ot = [sb1("ot%d" % i, [128, TSB], BF16) for i in range(2)]
            wabf = sb1("wabf", [128, 16, 16], F32)
            wabb = sb1("wabb", [128, 16, 16], BF16)
            ptr = [pp1("ptr%d" % i, [128, 8, 128], BF16) for i in range(2)]
            pacc = [pp1("pacc%d" % i, [128, 512], F32) for i in range(4)]
            pgab = pp1("pgab", [128, 512], F32)
            B_xt = [Buf() for _ in range(2)]
            B_xs = [Buf() for _ in range(2)]
            B_junk = Buf()
            B_ss = [Buf() for _ in range(2)]
            B_rs = [Buf() for _ in range(2)]
            B_xnT = [Buf() for _ in range(TSB // 128)]
            B_wf = [Buf() for _ in range(2)]
            B_wb = [Buf() for _ in range(2)]
            B_ot = [Buf() for _ in range(2)]
            B_wab, B_wabb = Buf(), Buf()
            B_ptr = [Buf() for _ in range(2)]
            B_pacc = [Buf() for _ in range(4)]
            B_pgab = Buf()
            B_proj = Buf("projT")

            P.op("sp", lambda e: e.dma_start(out=wabf[:], in_=wab[:, :, :]), w=[B_wab], dma=True)
            P.op("dve", lambda e: e.tensor_copy(out=wabb[:], in_=wabf[:]), r=[B_wab], w=[B_wabb])

            nptr = 0
            for sbi in range(NSB):
                for tl in range(TSB // 128):
                    tg = sbi * (TSB // 128) + tl
                    k = tg % 2
                    P.op("sp", lambda e, k=k, tg=tg: e.dma_start(out=xt[k][:], in_=x[tg * 128:(tg + 1) * 128, :]),
                         w=[B_xt[k]], dma=True)
                    P.op("act", lambda e, k=k: e.activation(out=junk[:], in_=xt[k][:], func=AF.Square,
                                                            accum_out=ss[k][:]),
                         r=[B_xt[k]], w=[B_junk, B_ss[k]])
                    P.op("act", lambda e, k=k: e.activation(out=rs[k][:], in_=ss[k][:], func=AF.Ln,
                                                            scale=1.0 / D, bias=EPS),
                         r=[B_ss[k]], w=[B_rs[k]])
                    P.op("act", lambda e, k=k: e.activation(out=rs[k][:], in_=rs[k][:], func=AF.Exp, scale=-0.5),
                         r=[B_rs[k]], w=[B_rs[k]])
                    P.op("act", lambda e, k=k: e.activation(out=xs[k][:], in_=xt[k][:], func=AF.Copy,
                                                            scale=rs[k][:]),
                         r=[B_xt[k], B_rs[k]], w=[B_xs[k]])
                    for half in range(2):
                        pk = nptr % 2
                        nptr += 1
                        for j in range(8):
                            c = half * 8 + j
                            P.op("pe", lambda e, pk=pk, j=j, c=c, k=k: e.transpose(
                                out=ptr[pk][:, j, :], in_=xs[k][:, c * 128:(c + 1) * 128], identity=ident[:]),
                                r=[B_xs[k], B_ident], w=[B_ptr[pk]], inc=(j == 7))
                        P.op("dve", lambda e, pk=pk, half=half, tl=tl: e.tensor_tensor(
                            out=xnT[:, half * 8:(half + 1) * 8, tl * 128:(tl + 1) * 128], in0=ptr[pk][:],
                            in1=ngs[:, half * 8:(half + 1) * 8].unsqueeze(2).broadcast_to([128, 8, 128]),
                            op=ALU.mult),
                            r=[B_ptr[pk], B_ngs], w=[B_xnT[tl]])
                    for hf in range(2):
                        for c in range(16):
                            P.op("pe", lambda e, c=c, tl=tl, hf=hf: e.matmul(
                                pgab[0:64, hf * 16:(hf + 1) * 16],
                                lhsT=xnT[:, c, tl * 128 + hf * 64:tl * 128 + hf * 64 + 64], rhs=wabb[:, c, :],
                                start=(c == 0), stop=(c == 15)),
                                r=[B_xnT[tl], B_wabb], w=[B_pgab], inc=(c == 15))
                    P.op("act", lambda e, tg=tg: e.activation(
                        out=gabtm[:, 2 * tg:2 * tg + 2, :],
                        in_=pgab[0:64, 0:32].rearrange("p (a b) -> p a b", a=2), func=AF.Copy),
                        r=[B_pgab], w=[B_gab])
                for g in range(NG):
                    k = g % 2
                    P.op("sp", lambda e, k=k, g=g: e.dma_start(out=wf[k][:], in_=win[g]), w=[B_wf[k]], dma=True)
                    P.op("dve", lambda e, k=k: e.tensor_copy(out=wb[k][:], in_=wf[k][:]), r=[B_wf[k]], w=[B_wb[k]])
                    if g == 46:
                        P.op("dve", lambda e, k=k: e.tensor_scalar(out=wb[k][:, :, 64:96], in0=wb[k][:, :, 64:96],
                                                                   scalar1=-1.0, scalar2=None, op0=ALU.mult),
                             r=[B_wb[k]], w=[B_wb[k]])
                    for tb in range(NTB):
                        for c in range(16):
                            P.op("pe", lambda e, k=k, tb=tb, c=c: e.matmul(
                                pacc[tb][:], lhsT=wb[k][:, c, :], rhs=xnT[:, c, tb * 512:(tb + 1) * 512],
                                start=(c == 0), stop=(c == 15)),
                                r=[B_wb[k]] + B_xnT[tb * 4:(tb + 1) * 4], w=[B_pacc[tb]], inc=(c == 15))
                        P.op("act", lambda e, k=k, tb=tb: e.activation(
                            out=ot[k][:, tb * 512:(tb + 1) * 512], in_=pacc[tb][:], func=AF.Copy),
                            r=[B_pacc[tb]], w=[B_ot[k]])
                    P.op("pool", lambda e, k=k, g=g, sbi=sbi: e.dma_start(
                        out=projT[g * 128:(g + 1) * 128, sbi * TSB:(sbi + 1) * TSB], in_=ot[k][:]),
                        r=[B_ot[k]], w=[B_proj], dma=True)


        P.barrier()
        NB = T // 512
        ones_b = sb("ones_b", [128, 128], BF16)
        gv = sb("gv", [128, 8], F32)
        B_ones, B_gv = Buf(), Buf()
        P.op("pool", lambda e: e.memset(ones_b[:], 1.0), w=[B_ones])
        P.op("sp", lambda e: e.dma_start(out=gv[:], in_=gvec[:, :]), w=[B_gv], dma=True)
        B_QT, B_KT, B_Vs, B_mix = Buf(), Buf(), Buf(), Buf()
        LN192 = float(np.log(192.0))
        PI = float(np.pi)
        C1 = 6.28125
        C2 = float(2 * np.pi - 6.28125)

        def rstd_from_psum(ps_ap, B_ps, dst_ap, B_dst, n, extra_bias=0.0, parts=128):
            P.op("act", lambda e: e.activation(out=dst_ap, in_=ps_ap, func=AF.Ln, scale=1.0 / n, bias=EPS),
                 r=[B_ps], w=[B_dst])
            P.op("act", lambda e: e.activation(out=dst_ap, in_=dst_ap, func=AF.Exp, scale=-0.5, bias=extra_bias),
                 r=[B_dst], w=[B_dst])

        with ExitStack() as ps:
            def sb2(name, shape, dt):
                return ps.enter_context(nc.sbuf_tensor(name, shape, dt))

            def pp2(name, shape, dt=F32):
                return ps.enter_context(nc.psum_tensor(name, shape, dt))

            wuqf = sb2("wuqf", [128, 4, 2048], F32)
            wuqb = sb2("wuqb", [128, 4, 2048], BF16)
            wukvf = sb2("wukvf", [128, 2, 2048], F32)
            wukvb = sb2("wukvb", [128, 2, 2048], BF16)
            qag_s = sb2("qag_s", [128, 4], F32)
            kvag_s = sb2("kvag_s", [128, 2], F32)
            ivi = sb2("ivi", [64, 1], I32)
            invf = sb2("invf", [64, 1], F32)
            B_wuqf, B_wuqb, B_wukvf, B_wukvb, B_qag, B_kvag, B_ivi, B_invf = [Buf() for _ in range(8)]
            P.op("sp", lambda e: e.dma_start(out=wuqf[:], in_=wuq[:, :, :]), w=[B_wuqf], dma=True)
            P.op("sp", lambda e: e.dma_start(out=wukvf[:], in_=wukv[:, :, :]), w=[B_wukvf], dma=True)
            P.op("sp", lambda e: e.dma_start(out=qag_s[:], in_=qag[:, :]), w=[B_qag], dma=True)
            P.op("sp", lambda e: e.dma_start(out=kvag_s[:], in_=kvag[:, :]), w=[B_kvag], dma=True)
            for kc in range(4):
                P.op("dve", lambda e, kc=kc: e.tensor_scalar(out=wuqb[:, kc, :], in0=wuqf[:, kc, :],
                                                             scalar1=qag_s[:, kc:kc + 1], scalar2=None, op0=ALU.mult),
                     r=[B_wuqf, B_qag], w=[B_wuqb])
            for kc in range(2):
                P.op("dve", lambda e, kc=kc: e.tensor_scalar(out=wukvb[:, kc, :], in0=wukvf[:, kc, :],
                                                             scalar1=kvag_s[:, kc:kc + 1], scalar2=None, op0=ALU.mult),
                     r=[B_wukvf, B_kvag], w=[B_wukvb])
            wq4 = wuqb[:].rearrange("p k (h c) -> p k h c", c=256)
            for kc in range(4):
                P.op("dve", lambda e, kc=kc: e.tensor_scalar(out=wq4[:, kc, :, 192:224], in0=wq4[:, kc, :, 192:224],
                                                             scalar1=-1.0, scalar2=None, op0=ALU.mult),
                     r=[B_wuqb], w=[B_wuqb])
            P.op("pool", lambda e: e.iota(ivi[0:32, :], pattern=[[0, 1]], base=0, channel_multiplier=1), w=[B_ivi])
            P.op("pool", lambda e: e.iota(ivi[32:64, :], pattern=[[0, 1]], base=0, channel_multiplier=1), w=[B_ivi])
            P.op("dve", lambda e: e.tensor_copy(out=invf[:], in_=ivi[:]), r=[B_ivi], w=[B_invf])
            P.op("act", lambda e: e.activation(out=invf[:], in_=invf[:], func=AF.Exp,
                                               scale=-float(np.log(10000.0)) / 32.0), r=[B_invf], w=[B_invf])

            cqT = sb2("cqT", [128, 4, 512], BF16)
            sqb = sb2("sqb", [128, 4, 512], BF16)
            cqn = sb2("cqn", [128, 4, 512], BF16)
            ckvT = sb2("ckvT", [128, 2, 512], BF16)
            ckvn = sb2("ckvn", [128, 2, 512], BF16)
            krA = sb2("krA", [64, 512], BF16)
            krB = sb2("krB", [64, 512], BF16)
            rst = sb2("rst", [128, 512], F32)
            rsth = sb2("rsth", [128, 512], F32)
            posi = sb2("posi", [64, 512], I32)
            ang = sb2("ang", [64, 512], F32)
            nfi = sb2("nfi", [64, 512], I32)
            nff = sb2("nff", [64, 512], F32)
            th = sb2("th", [64, 512], F32)
            thc = sb2("thc", [64, 512], F32)
            msk = sb2("msk", [64, 512], F32)
            cosT = sb2("cosT", [64, 512], F32)
            sinT = sb2("sinT", [64, 512], F32)
            ra = sb2("ra", [64, 512], F32)
            rb_ = sb2("rb_", [64, 512], F32)
            kr = sb2("kr", [64, 512], F32)
            sqn = sb2("sqn", [128, 512], BF16)
            sqr = sb2("sqr", [64, 512], BF16)
            sqkr = sb2("sqkr", [64, 512], BF16)
            on_ = [sb2("on%d" % i, [128, 512], BF16) for i in range(4)]
            or_ = [sb2("or%d" % i, [64, 512], BF16) for i in range(4)]
            sqnk = sb2("sqnk", [128, 512], BF16)
            rsthk = sb2("rsthk", [128, 512], F32)
            B_sqnk, B_rsthk = Buf(), Buf()
            vo = [sb2("vo%d" % i, [128, 1024], BF16) for i in range(2)]
            (B_cqT, B_sqb, B_cqn, B_ckvT, B_ckvn, B_krA, B_krB, B_rst, B_rsth, B_posi, B_ang, B_nfi, B_nff, B_th,
             B_thc, B_msk, B_cos, B_sin, B_ra, B_rb, B_kr, B_sqn, B_sqr, B_sqkr) = [Buf() for _ in range(24)]
            B_on = [Buf() for _ in range(4)]
            B_or = [Buf() for _ in range(4)]
            B_vo = [Buf(), Buf()]
            p_bc = pp2("p_bc", [128, 512])
            p_n = pp2("p_n", [128, 512])
            p_a = pp2("p_a", [128, 512])
            p_b = pp2("p_b", [128, 512])
            p_s = pp2("p_s", [128, 512])
            p_v = [pp2("p_v%d" % i, [128, 512]) for i in range(2)]
            B_pbc, B_pn, B_pa, B_pb, B_ps = [Buf() for _ in range(5)]
            B_pv = [Buf(), Buf()]
            nout = 0
            nvo = 0
            for blk in range(NB):
                c0 = blk * 512
                cs = slice(c0, c0 + 512)
                P.op("sp", lambda e, cs=cs: e.dma_start(
                    out=cqT[:], in_=projT[0:512, cs].rearrange("(k p) t -> p k t", p=128)),
                    r=[B_proj], w=[B_cqT], dma=True)
                P.op("sp", lambda e, cs=cs: e.dma_start(
                    out=ckvT[:], in_=projT[512:768, cs].rearrange("(k p) t -> p k t", p=128)),
                    r=[B_proj], w=[B_ckvT], dma=True)
                P.op("sp", lambda e, cs=cs: e.dma_start(out=krA[:], in_=projT[46 * 128:46 * 128 + 64, cs]),
                     r=[B_proj], w=[B_krA], dma=True)
                P.op("sp", lambda e, cs=cs: e.dma_start(out=krB[:], in_=projT[46 * 128 + 64:47 * 128, cs]),
                     r=[B_proj], w=[B_krB], dma=True)
                P.op("sp", lambda e, cs=cs: e.dma_start(out=posi[:], in_=pos64[:, cs]), w=[B_posi], dma=True)
                P.op("dve", lambda e: e.tensor_copy(out=ang[:], in_=posi[:]), r=[B_posi], w=[B_ang])
                P.op("dve", lambda e: e.tensor_scalar(out=ang[:], in0=ang[:], scalar1=invf[:, 0:1], scalar2=None,
                                                      op0=ALU.mult), r=[B_ang, B_invf], w=[B_ang])
                P.op("dve", lambda e: e.tensor_scalar(out=nfi[:], in0=ang[:], scalar1=1.0 / (2 * PI), scalar2=None,
                                                      op0=ALU.mult), r=[B_ang], w=[B_nfi])
                P.op("dve", lambda e: e.tensor_copy(out=nff[:], in_=nfi[:]), r=[B_nfi], w=[B_nff])
                P.op("dve", lambda e: e.scalar_tensor_tensor(out=th[:], in0=nff[:], scalar=-C1, in1=ang[:],
                                                             op0=ALU.mult, op1=ALU.add), r=[B_nff, B_ang], w=[B_th])
                P.op("dve", lambda e: e.scalar_tensor_tensor(out=th[:], in0=nff[:], scalar=-C2, in1=th[:],
                                                             op0=ALU.mult, op1=ALU.add), r=[B_nff, B_th], w=[B_th])

                def wrap(t_ap, B_t):
                    P.op("dve", lambda e: e.tensor_single_scalar(out=msk[:], in_=t_ap, scalar=PI, op=ALU.is_gt),
                         r=[B_t], w=[B_msk])
                    P.op("dve", lambda e: e.scalar_tensor_tensor(out=t_ap, in0=msk[:], scalar=-2 * PI, in1=t_ap,
                                                                 op0=ALU.mult, op1=ALU.add), r=[B_msk, B_t], w=[B_t])
                    P.op("dve", lambda e: e.tensor_single_scalar(out=msk[:], in_=t_ap, scalar=-PI, op=ALU.is_lt),
                         r=[B_t], w=[B_msk])
                    P.op("dve", lambda e: e.scalar_tensor_tensor(out=t_ap, in0=msk[:], scalar=2 * PI, in1=t_ap,
                                                                 op0=ALU.mult, op1=ALU.add), r=[B_msk, B_t], w=[B_t])
                    P.op("dve", lambda e: e.tensor_scalar(out=t_ap, in0=t_ap, scalar1=-PI, scalar2=PI,
                                                          op0=ALU.max, op1=ALU.min), r=[B_t], w=[B_t])

                wrap(th[:], B_th)
                P.op("dve", lambda e: e.tensor_scalar(out=thc[:], in0=th[:], scalar1=PI / 2, scalar2=None,
                                                      op0=ALU.add), r=[B_th], w=[B_thc])
                wrap(thc[:], B_thc)
                P.op("act", lambda e: e.activation(out=sinT[:], in_=th[:], func=AF.Sin), r=[B_th], w=[B_sin])
                P.op("act", lambda e: e.activation(out=cosT[:], in_=thc[:], func=AF.Sin), r=[B_thc], w=[B_cos])

                def lat_norm(src, B_src, dst, B_dst, nk):
                    P.op("act", lambda e: e.activation(out=sqb[:, 0:nk, :], in_=src[:], func=AF.Square),
                         r=[B_src], w=[B_sqb])
                    for kc in range(nk):
                        P.op("pe", lambda e, kc=kc: e.matmul(p_bc[:], lhsT=ones_b[:], rhs=sqb[:, kc, :],
                                                             start=(kc == 0), stop=(kc == nk - 1)),
                             r=[B_ones, B_sqb], w=[B_pbc], inc=(kc == nk - 1))
                    rstd_from_psum(p_bc[:], B_pbc, rst[:], B_rst, 128.0 * nk)
                    P.op("dve", lambda e: e.tensor_tensor(out=dst[:], in0=src[:],
                                                          in1=rst[:].unsqueeze(1).broadcast_to([128, nk, 512]),
                                                          op=ALU.mult), r=[B_src, B_rst], w=[B_dst])

                lat_norm(cqT, B_cqT, cqn, B_cqn, 4)
                lat_norm(ckvT, B_ckvT, ckvn, B_ckvn, 2)

                P.op("dve", lambda e: e.scalar_tensor_tensor(out=ra[:], in0=krA[:], scalar=gv[0:64, 4:5], in1=cosT[:],
                                                             op0=ALU.mult, op1=ALU.mult),
                     r=[B_krA, B_gv, B_cos], w=[B_ra])
                P.op("dve", lambda e: e.scalar_tensor_tensor(out=rb_[:], in0=krB[:], scalar=gv[0:64, 5:6], in1=sinT[:],
                                                             op0=ALU.mult, op1=ALU.mult),
                     r=[B_krB, B_gv, B_sin], w=[B_rb])
                P.op("dve", lambda e: e.tensor_tensor(out=kr[:], in0=ra[:], in1=rb_[:], op=ALU.add),
                     r=[B_ra, B_rb], w=[B_kr])
                P.op("act", lambda e: e.activation(out=sqkr[:], in_=kr[:], func=AF.Square), r=[B_kr], w=[B_sqkr])

                def gen_q():
                    nonlocal nout
                    for h in range(8):
                        for kc in range(4):
                            P.op("pe", lambda e, kc=kc: e.matmul(p_n[:], lhsT=wuqb[:, kc, h * 256:h * 256 + 128],
                                                                 rhs=cqn[:, kc, :], start=(kc == 0), stop=(kc == 3)),
                                 r=[B_wuqb, B_cqn], w=[B_pn], inc=(kc == 3))
                        for kc in range(4):
                            P.op("pe", lambda e, kc=kc: e.matmul(p_a[0:64, :], lhsT=wuqb[:, kc, h * 256 + 128:h * 256 + 192],
                                                                 rhs=cqn[:, kc, :], start=(kc == 0), stop=(kc == 3)),
                                 r=[B_wuqb, B_cqn], w=[B_pa], inc=(kc == 3))
                        for kc in range(4):
                            P.op("pe", lambda e, kc=kc: e.matmul(p_b[0:64, :], lhsT=wuqb[:, kc, h * 256 + 192:h * 256 + 256],
                                                                 rhs=cqn[:, kc, :], start=(kc == 0), stop=(kc == 3)),
                                 r=[B_wuqb, B_cqn], w=[B_pb], inc=(kc == 3))
                        yield
                        P.op("dve", lambda e: e.scalar_tensor_tensor(out=ra[:], in0=p_a[0:64, :], scalar=gv[0:64, 1:2],
                                                                     in1=cosT[:], op0=ALU.mult, op1=ALU.mult),
                             r=[B_pa, B_gv, B_cos], w=[B_ra])
                        P.op("dve", lambda e: e.scalar_tensor_tensor(out=rb_[:], in0=p_b[0:64, :], scalar=gv[0:64, 2:3],
                                                                     in1=sinT[:], op0=ALU.mult, op1=ALU.mult),
                             r=[B_pb, B_gv, B_sin], w=[B_rb])
                        P.op("act", lambda e: e.activation(out=sqn[:], in_=p_n[:], func=AF.Square, scale=gv[:, 0:1]),
                             r=[B_pn, B_gv], w=[B_sqn])
                        P.op("dve", lambda e: e.tensor_tensor(out=ra[:], in0=ra[:], in1=rb_[:], op=ALU.add),
                             r=[B_ra, B_rb], w=[B_ra])
                        yield
                        P.op("act", lambda e: e.activation(out=sqr[:], in_=ra[:], func=AF.Square), r=[B_ra], w=[B_sqr])
                        P.op("pe", lambda e: e.matmul(p_s[:], lhsT=ones_b[:], rhs=sqn[:], start=True, stop=False),
                             r=[B_ones, B_sqn], w=[B_ps], inc=False)
                        P.op("pe", lambda e: e.matmul(p_s[:], lhsT=ones_b[0:64, :], rhs=sqr[:], start=False, stop=True),
                             r=[B_ones, B_sqr], w=[B_ps])
                        yield
                        rstd_from_psum(p_s[:], B_ps, rsth[:], B_rsth, 192.0, extra_bias=-0.5 * LN192)
                        yield
                        k = nout % 4
                        nout += 1
                        P.op("dve", lambda e: e.scalar_tensor_tensor(out=on_[k][:], in0=p_n[:], scalar=gv[:, 0:1],
                                                                     in1=rsth[:], op0=ALU.mult, op1=ALU.mult),
                             r=[B_pn, B_gv, B_rsth], w=[B_on[k]])
                        P.op("dve", lambda e: e.tensor_tensor(out=or_[k][:], in0=ra[:], in1=rsth[0:64, :], op=ALU.mult),
                             r=[B_ra, B_rsth], w=[B_or[k]])
                        P.op("pool", lambda e: e.dma_start(out=QT[h, 0:128, cs], in_=on_[k][:]),
                             r=[B_on[k]], w=[B_QT], dma=True)
                        P.op("pool", lambda e: e.dma_start(out=QT[h, 128:192, cs], in_=or_[k][:]),
                             r=[B_or[k]], w=[B_QT], dma=True)
                        yield

                def gen_k():
                    nonlocal nout
                    for h in range(8):
                        for kc in range(2):
                            P.op("pe", lambda e, kc=kc: e.matmul(p_bc[:], lhsT=wukvb[:, kc, h * 128:(h + 1) * 128],
                                                                 rhs=ckvn[:, kc, :], start=(kc == 0), stop=(kc == 1)),
                                 r=[B_wukvb, B_ckvn], w=[B_pbc], inc=(kc == 1))
                        yield
                        P.op("act", lambda e: e.activation(out=sqnk[:], in_=p_bc[:], func=AF.Square, scale=gv[:, 3:4]),
                             r=[B_pbc, B_gv], w=[B_sqnk])
                        yield
                        P.op("pe", lambda e: e.matmul(p_v[1][:], lhsT=ones_b[:], rhs=sqnk[:], start=True, stop=False),
                             r=[B_ones, B_sqnk], w=[B_pv[1]], inc=False)
                        P.op("pe", lambda e: e.matmul(p_v[1][:], lhsT=ones_b[0:64, :], rhs=sqkr[:], start=False, stop=True),
                             r=[B_ones, B_sqkr], w=[B_pv[1]])
                        yield
                        rstd_from_psum(p_v[1][:], B_pv[1], rsthk[:], B_rsthk, 192.0)
                        yield
                        k = nout % 4
                        nout += 1
                        P.op("dve", lambda e: e.scalar_tensor_tensor(out=on_[k][:], in0=p_bc[:], scalar=gv[:, 3:4],
                                                                     in1=rsthk[:], op0=ALU.mult, op1=ALU.mult),
                             r=[B_pbc, B_gv, B_rsthk], w=[B_on[k]])
                        P.op("dve", lambda e: e.tensor_tensor(out=or_[k][:], in0=kr[:], in1=rsthk[0:64, :], op=ALU.mult),
                             r=[B_kr, B_rsthk], w=[B_or[k]])
                        P.op("pool", lambda e: e.dma_start(out=KT[h, 0:128, cs], in_=on_[k][:]),
                             r=[B_on[k]], w=[B_KT], dma=True)
                        P.op("pool", lambda e: e.dma_start(out=KT[h, 128:192, cs], in_=or_[k][:]),
                             r=[B_or[k]], w=[B_KT], dma=True)
                        yield

                def gen_v():
                    nonlocal nvo
                    for tt in range(4):
                        k = nvo % 2
                        nvo += 1
                        for hh in range(2):
                            for kc in range(2):
                                P.op("pe", lambda e, kc=kc, hh=hh: e.matmul(
                                    p_v[0][:], lhsT=ckvn[:, kc, tt * 128:(tt + 1) * 128],
                                    rhs=wukvb[:, kc, 1024 + hh * 512:1024 + (hh + 1) * 512],
                                    start=(kc == 0), stop=(kc == 1)),
                                    r=[B_ckvn, B_wukvb], w=[B_pv[0]], inc=(kc == 1))
                            P.op("act", lambda e, hh=hh: e.activation(out=vo[k][:, hh * 512:(hh + 1) * 512],
                                                                      in_=p_v[0][:], func=AF.Copy),
                                 r=[B_pv[0]], w=[B_vo[k]])
                            yield
                        P.op("pool", lambda e: e.dma_start(out=Vs[c0 + tt * 128:c0 + (tt + 1) * 128, :], in_=vo[k][:]),
                             r=[B_vo[k]], w=[B_Vs], dma=True)
                        yield

                sts = [gen_q(), gen_k(), gen_v()]
                while sts:
                    for st in list(sts):
                        try:
                            next(st)
                        except StopIteration:
                            sts.remove(st)

        P.barrier()
        with ExitStack() as ps:
            def sb3(name, shape, dt):
                return ps.enter_context(nc.sbuf_tensor(name, shape, dt))

            def pp3(name, shape, dt=F32):
                return ps.enter_context(nc.psum_tensor(name, shape, dt))

            ktn = [sb3("ktn%d" % i, [128, T], BF16) for i in range(2)]
            ktr = [sb3("ktr%d" % i, [64, T], BF16) for i in range(2)]
            vt = [sb3("vt%d" % i, [128, NT, 128], BF16) for i in range(2)]
            qn = [sb3("qn%d" % i, [128, 512], BF16) for i in range(2)]
            qr = [sb3("qr%d" % i, [64, 512], BF16) for i in range(2)]
            gt = [sb3("gt%d" % i, [128, 512], BF16) for i in range(2)]
            sg = sb3("sg", [128, 512], F32)
            pt = [sb3("pt%d" % i, [128, 512], BF16) for i in range(4)]
            mski = sb3("mski", [128, 512], I32)
            cmask = sb3("cmask", [128, 4, 512], BF16)
            rl = sb3("rl", [128, 512], F32)
            of = sb3("of", [128, 512], F32)
            ob = [sb3("ob%d" % i, [128, 512], BF16) for i in range(2)]
            B_kt = [Buf(), Buf()]
            B_vt = [Buf(), Buf()]
            B_q = [Buf(), Buf()]
            B_gt = [Buf(), Buf()]
            B_sg, B_mski, B_cmask, B_rl, B_of = [Buf() for _ in range(5)]
            B_pt = [Buf() for _ in range(4)]
            B_ob = [Buf(), Buf()]
            pS = [pp3("pS%d" % i, [128, 512]) for i in range(4)]
            pO = [pp3("pO%d" % i, [128, 512]) for i in range(2)]
            pL = [pp3("pL%d" % i, [128, 512]) for i in range(2)]
            B_pS = [Buf() for _ in range(4)]
            B_pO = [Buf(), Buf()]
            B_pL = [Buf(), Buf()]
            for dd in range(4):
                P.op("pool", lambda e, dd=dd: e.iota(mski[:], pattern=[[1, 512]], base=-128 * dd, channel_multiplier=-1),
                     w=[B_mski])
                P.op("dve", lambda e, dd=dd: e.tensor_single_scalar(out=cmask[:, dd, :], in_=mski[:], scalar=0,
                                                                    op=ALU.is_ge), r=[B_mski], w=[B_cmask])
            lacc = [sb3("lacc%d" % i, [128, 512], F32) for i in range(2)]
            onesf3 = sb3("onesf3", [128, 128], F32)
            B_lacc = [Buf(), Buf()]
            B_onesf3 = Buf()
            P.op("pool", lambda e: e.memset(onesf3[:], 1.0), w=[B_onesf3])
            pairs = []
            blocks = []
            for h in range(8):
                for j in range(NB):
                    bidx = len(blocks)
                    blocks.append((h, j))
                    for i in range(4 * j + 4):
                        pairs.append((bidx, i))

            def emit_S(idx):
                bidx, i = pairs[idx]
                h, j = blocks[bidx]
                hk = h % 2
                qk = bidx % 2
                sk = idx % 4
                cs = slice(j * 512, (j + 1) * 512)
                if i == 0:
                    if j == 0:
                        P.op("sp", lambda e: e.dma_start(out=ktn[hk][:], in_=KT[h, 0:128, :]),
                             r=[B_KT], w=[B_kt[hk]], dma=True)
                        P.op("sp", lambda e: e.dma_start(out=ktr[hk][:], in_=KT[h, 128:192, :]),
                             r=[B_KT], w=[B_kt[hk]], dma=True)
                        P.op("sp", lambda e: e.dma_start(
                            out=vt[hk][:], in_=Vs[:, h * 128:(h + 1) * 128].rearrange("(n p) d -> p n d", p=128)),
                            r=[B_Vs], w=[B_vt[hk]], dma=True)
                    P.op("sp", lambda e: e.dma_start(out=qn[qk][:], in_=QT[h, 0:128, cs]),
                         r=[B_QT], w=[B_q[qk]], dma=True)
                    P.op("sp", lambda e: e.dma_start(out=qr[qk][:], in_=QT[h, 128:192, cs]),
                         r=[B_QT], w=[B_q[qk]], dma=True)
                    P.op("sp", lambda e: e.dma_start(out=gt[qk][:], in_=projT[(6 + h) * 128:(7 + h) * 128, cs]),
                         r=[B_proj], w=[B_gt[qk]], dma=True)
                P.op("pe", lambda e: e.matmul(pS[sk][:], lhsT=ktn[hk][:, i * 128:(i + 1) * 128], rhs=qn[qk][:],
                                              start=True, stop=False),
                     r=[B_kt[hk], B_q[qk]], w=[B_pS[sk]], inc=False)
                P.op("pe", lambda e: e.matmul(pS[sk][:], lhsT=ktr[hk][:, i * 128:(i + 1) * 128], rhs=qr[qk][:],
                                              start=False, stop=True),
                     r=[B_kt[hk], B_q[qk]], w=[B_pS[sk]])

            def emit_exp(idx):
                bidx, i = pairs[idx]
                h, j = blocks[bidx]
                sk = idx % 4
                qk = bidx % 2
                P.op("act", lambda e: e.activation(out=pt[sk][:], in_=pS[sk][:], func=AF.Exp),
                     r=[B_pS[sk]], w=[B_pt[sk]])
                if i >= 4 * j:
                    P.op("dve", lambda e: e.tensor_tensor(out=pt[sk][:], in0=pt[sk][:], in1=cmask[:, i - 4 * j, :],
                                                          op=ALU.mult), r=[B_pt[sk], B_cmask], w=[B_pt[sk]])

            def emit_PV(idx):
                bidx, i = pairs[idx]
                h, j = blocks[bidx]
                hk = h % 2
                qk = bidx % 2
                sk = idx % 4
                nk = 4 * j + 4
                cs = slice(j * 512, (j + 1) * 512)
                P.op("pe", lambda e: e.matmul(pO[qk][:], lhsT=vt[hk][:, i, :], rhs=pt[sk][:], start=(i == 0),
                                              stop=(i == nk - 1)),
                     r=[B_vt[hk], B_pt[sk]], w=[B_pO[qk]], inc=False)
                P.op("pe", lambda e: e.matmul(pL[qk][:], lhsT=ones_b[:], rhs=pt[sk][:], start=(i == 0),
                                              stop=(i == nk - 1)),
                     r=[B_ones, B_pt[sk]], w=[B_pL[qk]])
                if i == nk - 1:
                    P.op("dve", lambda e: e.reciprocal(out=rl[:], in_=pL[qk][:]), r=[B_pL[qk]], w=[B_rl])
                    P.op("act", lambda e: e.activation(out=sg[:], in_=gt[qk][:], func=AF.Silu),
                         r=[B_gt[qk]], w=[B_sg])
                    P.op("dve", lambda e: e.tensor_tensor(out=of[:], in0=pO[qk][:], in1=rl[:], op=ALU.mult),
                         r=[B_pO[qk], B_rl], w=[B_of])
                    P.op("dve", lambda e: e.tensor_tensor(out=ob[qk][:], in0=of[:], in1=sg[:], op=ALU.mult),
                         r=[B_of, B_sg], w=[B_ob[qk]])
                    P.op("pool", lambda e: e.dma_start(out=mixT[h * 128:(h + 1) * 128, cs], in_=ob[qk][:]),
                         r=[B_ob[qk]], w=[B_mix], dma=True)

            NP_ = len(pairs)
            for q0 in range(min(3, NP_)):
                emit_S(q0)
            for idx in range(NP_):
                emit_exp(idx)
                if idx + 3 < NP_:
                    emit_S(idx + 3)
                emit_PV(idx)
        P.barrier()
        NC = T // 64
        with ExitStack() as ps:
            def sb4(name, shape, dt):
                return ps.enter_context(nc.sbuf_tensor(name, shape, dt))

            def pp4(name, shape, dt=F32):
                return ps.enter_context(nc.psum_tensor(name, shape, dt))

            def bc3(ap2, n):
                return ap2.unsqueeze(2).broadcast_to([ap2.shape[0], ap2.shape[1], n])

            def bch(ap2, nh=8):
                return ap2.unsqueeze(1).broadcast_to([ap2.shape[0], nh, ap2.shape[1]])

            cw_s = sb4("cw_s", [128, 24, 4], F32)
            al_s = sb4("al_s", [64, 8], F32)
            dt_s = sb4("dt_s", [64, 8], F32)
            g_all = sb4("g_all", [64, NC, 8], F32)
            be_all = sb4("be_all", [64, NC, 8], F32)
            ioj = sb4("ioj", [64, 64], I32)
            iof = sb4("iof", [64, 64], F32)
            tri = sb4("tri", [64, 64], F32)
            id64 = sb4("id64", [64, 64], F32)
            ng_su = sb4("ng_su", [64, 64], F32)
            ng_iu = sb4("ng_iu", [64, 64], F32)
            ng_sl = sb4("ng_sl", [64, 64], F32)
            ones_f = sb4("ones_f", [64, 128], F32)
            S = sb4("S", [128, 8, 128], F32)
            S_bf = sb4("S_bf", [128, 8, 128], BF16)
            dg = sb4("dg", [128, 96, 128], BF16)
            B_dg = Buf()
            ts_ = ExitStack()
            t1 = ts_.enter_context(nc.sbuf_tensor("t1", [64, NC, 8], F32))
            t2 = ts_.enter_context(nc.sbuf_tensor("t2", [64, NC, 8], F32))
            (B_cw, B_al, B_dt, B_g, B_be, B_t1, B_t2, B_ioj, B_iof, B_tri, B_id64, B_ngsu, B_ngiu, B_ngsl, B_onesf,
             B_S, B_Sbf) = [Buf() for _ in range(17)]
            P.op("sp", lambda e: e.dma_start(out=cw_s[:], in_=cw[:, :, :]), w=[B_cw], dma=True)
            P.op("sp", lambda e: e.dma_start(out=al_s[:], in_=alog64[:, :]), w=[B_al], dma=True)
            P.op("sp", lambda e: e.dma_start(out=dt_s[:], in_=dtb64[:, :]), w=[B_dt], dma=True)
            P.op("dve", lambda e: e.tensor_tensor(out=t1[:], in0=gabtm[:, :, 0:8],
                                                  in1=dt_s[:].unsqueeze(1).broadcast_to([64, NC, 8]), op=ALU.add),
                 r=[B_gab, B_dt], w=[B_t1])
            P.op("act", lambda e: e.activation(out=t2[:], in_=t1[:], func=AF.Abs), r=[B_t1], w=[B_t2])
            P.op("act", lambda e: e.activation(out=t2[:], in_=t2[:], func=AF.Exp, scale=-1.0), r=[B_t2], w=[B_t2])
            P.op("act", lambda e: e.activation(out=t2[:], in_=t2[:], func=AF.Ln, bias=1.0), r=[B_t2], w=[B_t2])
            P.op("dve", lambda e: e.tensor_scalar(out=t1[:], in0=t1[:], scalar1=0.0, scalar2=None, op0=ALU.max),
                 r=[B_t1], w=[B_t1])
            P.op("dve", lambda e: e.tensor_tensor(out=t1[:], in0=t1[:], in1=t2[:], op=ALU.add),
                 r=[B_t1, B_t2], w=[B_t1])
            P.op("act", lambda e: e.activation(out=al_s[:], in_=al_s[:], func=AF.Exp), r=[B_al], w=[B_al])
            P.op("dve", lambda e: e.tensor_scalar(out=al_s[:], in0=al_s[:], scalar1=-1.0, scalar2=None, op0=ALU.mult),
                 r=[B_al], w=[B_al])
            P.op("dve", lambda e: e.tensor_tensor(out=g_all[:], in0=t1[:],
                                                  in1=al_s[:].unsqueeze(1).broadcast_to([64, NC, 8]), op=ALU.mult),
                 r=[B_t1, B_al], w=[B_g])
            P.op("act", lambda e: e.activation(out=t2[:], in_=gabtm[:, :, 8:16], func=AF.Exp, scale=-1.0),
                 r=[B_gab], w=[B_t2])
            P.op("dve", lambda e: e.tensor_scalar(out=t2[:], in0=t2[:], scalar1=1.0, scalar2=None, op0=ALU.add),
                 r=[B_t2], w=[B_t2])
            P.op("dve", lambda e: e.reciprocal(out=be_all[:], in_=t2[:]), r=[B_t2], w=[B_be])
            P.op("pool", lambda e: e.iota(ioj[:], pattern=[[1, 64]], base=0, channel_multiplier=-1), w=[B_ioj])
            P.op("dve", lambda e: e.tensor_copy(out=iof[:], in_=ioj[:]), r=[B_ioj], w=[B_iof])
            P.op("dve", lambda e: e.tensor_single_scalar(out=tri[:], in_=iof[:], scalar=0.0, op=ALU.is_ge),
                 r=[B_iof], w=[B_tri])
            P.op("dve", lambda e: e.tensor_single_scalar(out=id64[:], in_=iof[:], scalar=0.0, op=ALU.is_equal),
                 r=[B_iof], w=[B_id64])
            for dst, Bd, thr, cmp in ((ng_su, B_ngsu, 1.0, ALU.is_ge), (ng_iu, B_ngiu, 0.0, ALU.is_ge),
                                      (ng_sl, B_ngsl, -1.0, ALU.is_le)):
                P.op("dve", lambda e, dst=dst, thr=thr, cmp=cmp: e.tensor_single_scalar(
                    out=dst[:], in_=iof[:], scalar=thr, op=cmp), r=[B_iof], w=[Bd])
                P.op("dve", lambda e, dst=dst: e.tensor_scalar(out=dst[:], in0=dst[:], scalar1=-1.0, scalar2=30000.0,
                                                               op0=ALU.add, op1=ALU.mult), r=[Bd], w=[Bd])
            P.op("pool", lambda e: e.memset(ones_f[:], 1.0), w=[B_onesf])
            P.op("pool", lambda e: e.memset(S[:], 0.0), w=[B_S])
            P.op("pool", lambda e: e.memset(S_bf[:], 0.0), w=[B_Sbf])
            for gi in range(24):
                for j in range(4):
                    P.op("dve", lambda e, gi=gi, j=j: e.tensor_scalar(out=dg[:, gi * 4 + j, :], in0=ident[:],
                                                                      scalar1=cw_s[:, gi, j:j + 1], scalar2=None,
                                                                      op0=ALU.mult), r=[B_ident, B_cw], w=[B_dg])
            ts_.close()
            P.barrier()


            GB = 256
            NGB = T // GB
            CPB = GB // 64
            xin = [sb4("xin%d" % i, [128, GB + 3], BF16) for i in range(8)]
            ysl16 = sb4("ysl16", [128, 16, GB], BF16)
            B_ysl16 = Buf()
            sq4 = sb4("sq4", [128, GB], BF16)
            rs4 = sb4("rs4", [128, GB], F32)
            vsT = sb4("vsT", [128, 8, GB], BF16)
            qnT = [sb4("qnT%d" % i, [128, 8, GB], BF16) for i in range(2)]
            knT = [sb4("knT%d" % i, [128, 8, GB], BF16) for i in range(2)]
            ktm = [sb4("ktm%d" % i, [64, CPB, 8, 128], BF16) for i in range(2)]
            vtm = [sb4("vtm%d" % i, [64, CPB, 8, 128], BF16) for i in range(2)]
            oTb = [sb4("oTb%d" % i, [128, 8, GB], BF16) for i in range(2)]
            B_xin = [Buf() for _ in range(8)]
            B_sq4, B_rs4, B_vsT = Buf(), Buf(), Buf()
            B_qnT = [Buf(), Buf()]
            B_knT = [Buf(), Buf()]
            B_ktm = [Buf(), Buf()]
            B_vtm = [Buf(), Buf()]
            B_oTb = [Buf(), Buf()]

            def two(name, shape, dt, n=2):
                return [sb4("%s%d" % (name, i), shape, dt) for i in range(n)], [Buf() for _ in range(n)]

            gTri, B_gTri = two("gTri", [64, 8, 64], F32)
            gcb, B_gcb = two("gcb", [128, 8, 64], F32)
            gct, B_gct = two("gct", [64, 8], F32)
            dA, B_dA = two("dA", [64, 8, 64], F32)
            dD, B_dD = two("dD", [64, 8, 64], F32)
            AmA, B_AmA = two("AmA", [64, 8, 64], BF16)
            AmB, B_AmB = two("AmB", [64, 8, 64], BF16)
            AtA, B_AtA = two("AtA", [64, 8, 64], BF16)
            AtB, B_AtB = two("AtB", [64, 8, 64], BF16)
            TtA, B_TtA = two("TtA", [64, 8, 64], BF16)
            TtB, B_TtB = two("TtB", [64, 8, 64], BF16)
            vb, B_vb = two("vb", [64, 8, 128], BF16)
            kbg, B_kbg = two("kbg", [64, 8, 128], BF16)
            s8a, B_s8a = two("s8a", [64, 8], F32)
            s8b, B_s8b = two("s8b", [64, 8], F32)
            egc, B_egc = two("egc", [128, 8, 64], F32, 3)
            attnT, B_attnT = two("attnT", [64, 8, 64], BF16, 3)
            kd, B_kd = two("kd", [64, 8, 128], BF16, 3)
            u_sb, B_u = two("u_sb", [64, 8, 128], F32, 3)
            wT_sb, B_wT = two("wT_sb", [128, 8, 64], BF16, 3)
            qdT, B_qdT = two("qdT", [128, 8, 64], BF16, 3)
            vnew = sb4("vnew", [64, 8, 128], BF16)
            B_vnew = Buf()
            gg = sb4("gg", [128, 8, GB], BF16)
            sgg = sb4("sgg", [128, 8, GB], BF16)
            o1 = sb4("o1", [128, GB], F32)
            o2 = [sb4("o2%d" % i, [128, GB], BF16) for i in range(2)]
            B_gg, B_sgg, B_o1 = Buf(), Buf(), Buf()
            sq4e = sb4("sq4e", [128, GB], BF16)
            rs4e = sb4("rs4e", [128, GB], F32)
            B_sq4e, B_rs4e = Buf(), Buf()
            B_o2 = [Buf(), Buf()]
            bk = [pp4("bk%d" % i, [128, 512]) for i in range(7)]
            B_bk = [Buf() for _ in range(7)]
            B_half = [B_bk[6], B_bk[6]]
            pb2 = pp4("pb2", [128, 512], F32)
            ptb = pb2[:].bitcast(BF16).rearrange("p (h c) -> p h c", h=8)
            B_ptb = Buf()
            print("GDN sbuf bytes remaining", nc.sbuf_bytes_remaining)
            pro_done = [False] * NGB
            epi_done = [False] * NGB
            prep_done = [False] * NC
            scan_done = [False] * NC
            cnt4 = {"x": 0, "o2": 0}

            def gen_pro(blk):
                bp = blk % 2
                c0 = blk * GB
                for kind in range(3):
                    for h in range(8):
                        row0 = (14 + kind * 8 + h) * 128
                        if blk == 0:
                            P.op("pool", lambda e, h=h: e.memset(xin[h][:, 0:3], 0.0), w=[B_xin[h]])
                            P.op("sp", lambda e, h=h, row0=row0: e.dma_start(out=xin[h][:, 3:GB + 3],
                                                                             in_=projT[row0:row0 + 128, 0:GB]),
                                 r=[B_proj], w=[B_xin[h]], dma=True)
                        else:
                            P.op("sp", lambda e, h=h, row0=row0: e.dma_start(out=xin[h][:],
                                                                             in_=projT[row0:row0 + 128, c0 - 3:c0 + GB]),
                                 r=[B_proj], w=[B_xin[h]], dma=True)
                    yield
                    for h2 in range(0, 8, 2):
                        for h in (h2, h2 + 1):
                            gi = kind * 8 + h
                            hf = h % 2
                            cvh = pb2[:, hf * 256:hf * 256 + GB]
                            for j in range(4):
                                P.op("pe", lambda e, j=j: e.matmul(cvh, lhsT=dg[:, gi * 4 + j, :], rhs=xin[h][:, j:j + GB],
                                                                   start=(j == 0), stop=(j == 3)),
                                     r=[B_dg, B_xin[h]], w=[B_ptb], inc=(j == 3))
                        cv2 = pb2[:].rearrange("p (a b) -> p a b", a=2)
                        if kind == 2:
                            P.op("act", lambda e: e.activation(out=vsT[:, h2:h2 + 2, :], in_=cv2, func=AF.Silu),
                                 r=[B_ptb], w=[B_vsT])
                        else:
                            P.op("act", lambda e: e.activation(out=ysl16[:, kind * 8 + h2:kind * 8 + h2 + 2, :], in_=cv2,
                                                               func=AF.Silu), r=[B_ptb], w=[B_ysl16])
                        yield
                for kind in range(2):
                    for h in range(8):
                        gi = kind * 8 + h
                        P.op("act", lambda e: e.activation(out=sq4[:], in_=ysl16[:, gi, :], func=AF.Square),
                             r=[B_ysl16], w=[B_sq4])
                        yield
                        P.op("pe", lambda e: e.matmul(pb2[:, 0:GB], lhsT=ones_b[:], rhs=sq4[:], start=True, stop=True),
                             r=[B_ones, B_sq4], w=[B_ptb])
                        P.op("act", lambda e: e.activation(out=rs4[:], in_=pb2[:, 0:GB], func=AF.Ln, bias=EPS),
                             r=[B_ptb], w=[B_rs4])
                        P.op("act", lambda e: e.activation(
                            out=rs4[:], in_=rs4[:], func=AF.Exp, scale=-0.5,
                            bias=(-0.5 * float(np.log(128.0)) if kind == 0 else 0.0)), r=[B_rs4], w=[B_rs4])
                        dstT, Bd = (qnT[bp], B_qnT[bp]) if kind == 0 else (knT[bp], B_knT[bp])
                        P.op("dve", lambda e: e.tensor_tensor(out=dstT[:, h, :], in0=ysl16[:, gi, :], in1=rs4[:], op=ALU.mult),
                             r=[B_ysl16, B_rs4], w=[Bd])
                        yield
                for srcT, Bs, dstm, Bdm in ((knT[bp], B_knT[bp], ktm[bp], B_ktm[bp]), (vsT, B_vsT, vtm[bp], B_vtm[bp])):
                    for ci in range(CPB):
                        for h in range(8):
                            P.op("pe", lambda e, h=h: e.transpose(
                                out=ptb[0:64, h, :], in_=srcT[:, h, ci * 64:(ci + 1) * 64], identity=ident[:]),
                                r=[Bs, B_ident], w=[B_ptb], inc=(h == 7))
                        P.op("act", lambda e: e.activation(out=dstm[:, ci, :, :], in_=ptb[0:64, :, :], func=AF.Copy),
                             r=[B_ptb], w=[Bdm])
                        yield
                pro_done[blk] = True

            def gen_prep(n):
                blk, ci = divmod(n, CPB)
                bp = blk % 2
                p = n % 2
                q4 = n % 3
                X0, X1, X2 = (bk[0], bk[1], bk[2]) if p == 0 else (bk[3], bk[4], bk[5])
                BX0, BX1, BX2 = (B_bk[0], B_bk[1], B_bk[2]) if p == 0 else (B_bk[3], B_bk[4], B_bk[5])
                cs = slice(ci * 64, ci * 64 + 64)
                gsl = g_all[:, n, :]
                bsl = be_all[:, n, :]
                KN, QN = knT[bp], qnT[bp]
                P.op("dve", lambda e: e.tensor_tensor(out=gTri[p][:], in0=bch(tri[:]), in1=bc3(gsl, 64), op=ALU.mult),
                     r=[B_tri, B_g], w=[B_gTri[p]])
                yield
                P.op("pe", lambda e: e.matmul(X0[:], lhsT=ones_f[:], rhs=gTri[p][:].rearrange("p a b -> p (a b)"),
                                              start=True, stop=True), r=[B_onesf, B_gTri[p]], w=[BX0])
                P.op("pe", lambda e: e.matmul(X1[0:64, 0:8], lhsT=tri[:], rhs=gsl, start=True, stop=True),
                     r=[B_tri, B_g], w=[BX1])
                for h in range(8):
                    P.op("pe", lambda e, h=h: e.matmul(X2[0:64, h * 64:(h + 1) * 64], lhsT=KN[:, h, cs], rhs=KN[:, h, cs],
                                                       start=True, stop=True), r=[B_knT[bp]], w=[BX2], inc=(h == 7))
                yield
                P.op("act", lambda e: e.activation(out=gcb[p][:].rearrange("p a b -> p (a b)"), in_=X0[:], func=AF.Copy),
                     r=[BX0], w=[B_gcb[p]])
                P.op("act", lambda e: e.activation(out=egc[q4][:].rearrange("p a b -> p (a b)"), in_=X0[:], func=AF.Exp),
                     r=[BX0], w=[B_egc[q4]])
                P.op("dve", lambda e: e.tensor_copy(out=gct[p][:], in_=X1[0:64, 0:8]), r=[BX1], w=[B_gct[p]])
                yield
                for h in range(8):
                    P.op("pe", lambda e, h=h: e.matmul(X0[0:64, h * 64:(h + 1) * 64], lhsT=KN[:, h, cs], rhs=QN[:, h, cs],
                                                       start=True, stop=True), r=[B_knT[bp], B_qnT[bp]], w=[BX0], inc=(h == 7))
                P.op("dve", lambda e: e.tensor_tensor(out=dA[p][:], in0=bc3(gct[p][:], 64), in1=gcb[p][0:64, :, :],
                                                      op=ALU.subtract), r=[B_gct[p], B_gcb[p]], w=[B_dA[p]])
                P.op("dve", lambda e: e.scalar_tensor_tensor(out=dA[p][:], in0=dA[p][:], scalar=0.0, in1=bch(ng_sl[:]),
                                                             op0=ALU.min, op1=ALU.add), r=[B_dA[p], B_ngsl], w=[B_dA[p]])
                P.op("dve", lambda e: e.tensor_tensor(out=dD[p][:], in0=gcb[p][0:64, :, :], in1=bc3(gct[p][:], 64),
                                                      op=ALU.subtract), r=[B_gct[p], B_gcb[p]], w=[B_dD[p]])
                P.op("dve", lambda e: e.scalar_tensor_tensor(out=dD[p][:], in0=dD[p][:], scalar=0.0, in1=bch(ng_iu[:]),
                                                             op0=ALU.min, op1=ALU.add), r=[B_dD[p], B_ngiu], w=[B_dD[p]])
                yield
                P.op("act", lambda e: e.activation(out=dA[p][:], in_=dA[p][:], func=AF.Exp), r=[B_dA[p]], w=[B_dA[p]])
                P.op("act", lambda e: e.activation(out=dD[p][:], in_=dD[p][:], func=AF.Exp), r=[B_dD[p]], w=[B_dD[p]])
                P.op("act", lambda e: e.activation(out=s8a[p][:], in_=gct[p][:], func=AF.Exp), r=[B_gct[p]], w=[B_s8a[p]])
                P.op("dve", lambda e: e.tensor_tensor(out=s8b[p][:], in0=gcb[p][0:64, :, 63], in1=gct[p][:], op=ALU.subtract),
                     r=[B_gcb[p], B_gct[p]], w=[B_s8b[p]])
                P.op("act", lambda e: e.activation(out=s8b[p][:], in_=s8b[p][:], func=AF.Exp), r=[B_s8b[p]], w=[B_s8b[p]])
                yield
                P.op("dve", lambda e: e.tensor_tensor(out=dA[p][:], in0=dA[p][:], in1=bc3(bsl, 64), op=ALU.mult),
                     r=[B_dA[p], B_be], w=[B_dA[p]])
                P.op("dve", lambda e: e.tensor_tensor(out=AmA[p][:], in0=X2[0:64, :].rearrange("p (a b) -> p a b", a=8),
                                                      in1=dA[p][:], op=ALU.mult), r=[BX2, B_dA[p]], w=[B_AmA[p]])
                P.op("dve", lambda e: e.tensor_tensor(out=attnT[q4][:], in0=X0[0:64, :].rearrange("p (a b) -> p a b", a=8),
                                                      in1=dD[p][:], op=ALU.mult), r=[BX0, B_dD[p]], w=[B_attnT[q4]])
                yield
                for h in range(8):
                    P.op("pe", lambda e, h=h: e.transpose(out=ptb[0:64, h, 0:64], in_=AmA[p][:, h, :],
                                                          identity=ident[0:64, 0:64]),
                         r=[B_AmA[p], B_ident], w=[B_ptb], inc=(h == 7))
                S8 = os.environ.get("GDN_S8", "atvs")
                if "a" in S8:
                    P.op("act", lambda e: e.activation(out=AtA[p][:], in_=ptb[0:64, :, 0:64], func=AF.Copy),
                         r=[B_ptb], w=[B_AtA[p]])
                if "t" in S8:
                    P.op("dve", lambda e: e.tensor_tensor(out=TtA[p][:], in0=bch(id64[:]), in1=AtA[p][:],
                                                          op=ALU.subtract), r=[B_id64, B_AtA[p]], w=[B_TtA[p]])
                if "v" in S8:
                    P.op("dve", lambda e: e.tensor_tensor(out=vb[p][:], in0=vtm[bp][:, ci, :, :], in1=bc3(bsl, 128), op=ALU.mult),
                         r=[B_vtm[bp], B_be], w=[B_vb[p]])
                if "s" in S8:
                    P.op("dve", lambda e: e.tensor_tensor(out=s8a[p][:], in0=s8a[p][:], in1=bsl, op=ALU.mult),
                         r=[B_s8a[p], B_be], w=[B_s8a[p]])
                yield
                Am = [(AmA[p], B_AmA[p]), (AmB[p], B_AmB[p])]
                At = [(AtA[p], B_AtA[p]), (AtB[p], B_AtB[p])]
                Tt = [(TtA[p], B_TtA[p]), (TtB[p], B_TtB[p])]
                cur = 0
                tcur = 0
                for lvl in range(5):
                    nxt = 1 - cur
                    (Ac, BAc), (Atc, BAtc) = Am[cur], At[cur]
                    (An, BAn), (Atn, BAtn) = Am[nxt], At[nxt]
                    (Tc, BTc), (Tn, BTn) = Tt[tcur], Tt[1 - tcur]
                    for h in range(8):
                        P.op("pe", lambda e, h=h: e.matmul(X0[0:64, h * 64:(h + 1) * 64], lhsT=Atc[:, h, :], rhs=Ac[:, h, :],
                                                           start=True, stop=True), r=[BAtc, BAc], w=[BX0], inc=(h == 7))
                    if lvl < 4:
                        for h in range(8):
                            P.op("pe", lambda e, h=h: e.matmul(X1[0:64, h * 64:(h + 1) * 64], lhsT=Ac[:, h, :], rhs=Atc[:, h, :],
                                                               start=True, stop=True), r=[BAtc, BAc], w=[BX1], inc=(h == 7))
                    yield
                    P.op("act", lambda e: e.activation(out=An[:].rearrange("p a b -> p (a b)"), in_=X0[0:64, :], func=AF.Copy),
                         r=[BX0], w=[BAn])
                    if lvl < 4:
                        P.op("dve", lambda e: e.tensor_copy(out=Atn[:].rearrange("p a b -> p (a b)"), in_=X1[0:64, :]),
                             r=[BX1], w=[BAtn])
                    if lvl == 0:
                        P.op("dve", lambda e: e.tensor_tensor(out=kbg[p][:], in0=ktm[bp][:, ci, :, :], in1=bc3(s8a[p][:], 128),
                                                              op=ALU.mult), r=[B_ktm[bp], B_s8a[p]], w=[B_kbg[p]])
                    if lvl == 1:
                        P.op("dve", lambda e: e.tensor_tensor(out=kd[q4][:], in0=ktm[bp][:, ci, :, :], in1=bc3(s8b[p][:], 128),
                                                              op=ALU.mult), r=[B_ktm[bp], B_s8b[p]], w=[B_kd[q4]])
                    if lvl == 2:
                        P.op("dve", lambda e: e.tensor_tensor(out=qdT[q4][:], in0=QN[:, :, cs], in1=egc[q4][:], op=ALU.mult),
                             r=[B_qnT[bp], B_egc[q4]], w=[B_qdT[q4]])
                    yield
                    for h in range(8):
                        P.op("pe", lambda e, h=h: e.matmul(X2[0:64, h * 64:(h + 1) * 64], lhsT=An[:, h, :], rhs=Tc[:, h, :],
                                                           start=True, stop=True), r=[BAn, BTc], w=[BX2], inc=(h == 7))
                    yield
                    P.op("dve", lambda e: e.tensor_tensor(out=Tn[:].rearrange("p a b -> p (a b)"), in0=X2[0:64, :],
                                                          in1=Tc[:].rearrange("p a b -> p (a b)"), op=ALU.add),
                         r=[BX2, BTc], w=[BTn])
                    yield
                    cur = nxt
                    tcur = 1 - tcur
                TT, B_TT = Tt[tcur]
                for h in range(8):
                    Xu, BXu = (X0, BX0) if h < 4 else (X1, BX1)
                    P.op("pe", lambda e, h=h, Xu=Xu: e.matmul(Xu[0:64, (h % 4) * 128:(h % 4 + 1) * 128], lhsT=TT[:, h, :],
                                                              rhs=vb[p][:, h, :], start=True, stop=True),
                         r=[B_TT, B_vb[p]], w=[BXu], inc=(h % 4 == 3))
                for h in range(8):
                    P.op("pe", lambda e, h=h: e.matmul(X2[:, h * 64:(h + 1) * 64], lhsT=kbg[p][:, h, :], rhs=TT[:, h, :],
                                                       start=True, stop=True), r=[B_kbg[p], B_TT], w=[BX2], inc=(h == 7))
                yield
                for hb, (Xu, BXu) in enumerate(((X0, BX0), (X1, BX1))):
                    P.op("act", lambda e, hb=hb, Xu=Xu: e.activation(
                        out=u_sb[q4][:, hb * 4:(hb + 1) * 4, :].rearrange("p a b -> p (a b)"), in_=Xu[0:64, :],
                        func=AF.Copy), r=[BXu], w=[B_u[q4]])
                P.op("act", lambda e: e.activation(out=wT_sb[q4][:].rearrange("p a b -> p (a b)"), in_=X2[:], func=AF.Copy),
                     r=[BX2], w=[B_wT[q4]])
                yield
                prep_done[n] = True

            def gen_scan(n):
                blk, ci = divmod(n, CPB)
                bp = blk % 2
                q4 = n % 3
                cs = slice(ci * 64, ci * 64 + 64)
                X = bk[6]
                BXs = [B_bk[6]]
                for hb in range(2):
                    hs = range(hb * 4, hb * 4 + 4)
                    for h in hs:
                        P.op("pe", lambda e, h=h: e.matmul(X[0:64, (h % 4) * 128:(h % 4 + 1) * 128], lhsT=wT_sb[q4][:, h, :],
                                                           rhs=S_bf[:, h, :], start=True, stop=True),
                             r=[B_wT[q4], B_Sbf], w=BXs, inc=(h % 4 == 3))
                    P.op("dve", lambda e: e.tensor_tensor(
                        out=vnew[:, hb * 4:(hb + 1) * 4, :].rearrange("p a b -> p (a b)"),
                        in0=u_sb[q4][:, hb * 4:(hb + 1) * 4, :].rearrange("p a b -> p (a b)"), in1=X[0:64, :],
                        op=ALU.subtract), r=[B_u[q4]] + BXs, w=[B_vnew])
                    yield
                    for h in hs:
                        P.op("pe", lambda e, h=h: e.matmul(X[:, (h % 4) * 64:(h % 4 + 1) * 64], lhsT=S_bf[:, h, :],
                                                           rhs=qdT[q4][:, h, :], start=True, stop=False),
                             r=[B_Sbf, B_qdT[q4]], w=BXs, inc=False)
                        P.op("pe", lambda e, h=h: e.matmul(X[:, (h % 4) * 64:(h % 4 + 1) * 64], lhsT=vnew[:, h, :],
                                                           rhs=attnT[q4][:, h, :], start=False, stop=True),
                             r=[B_vnew, B_attnT[q4]], w=BXs, inc=(h % 4 == 3))
                    P.op("act", lambda e: e.activation(out=oTb[bp][:, hb * 4:(hb + 1) * 4, cs],
                                                       in_=X[:, 0:256].rearrange("p (a b) -> p a b", a=4), func=AF.Copy),
                         r=BXs, w=[B_oTb[bp]])
                    yield
                    for h in hs:
                        P.op("pe", lambda e, h=h: e.matmul(X[:, (h % 4) * 128:(h % 4 + 1) * 128], lhsT=kd[q4][:, h, :],
                                                           rhs=vnew[:, h, :], start=True, stop=True),
                             r=[B_kd[q4], B_vnew], w=BXs, inc=(h % 4 == 3))
                    P.op("dve", lambda e: e.tensor_tensor(
                        out=S[:, hb * 4:(hb + 1) * 4, :], in0=S[:, hb * 4:(hb + 1) * 4, :],
                        in1=egc[q4][:, hb * 4:(hb + 1) * 4, 63:64].broadcast_to([128, 4, 128]), op=ALU.mult),
                        r=[B_S, B_egc[q4]], w=[B_S])
                    P.op("dve", lambda e: e.tensor_tensor(
                        out=S[:, hb * 4:(hb + 1) * 4, :].rearrange("p a b -> p (a b)"),
                        in0=S[:, hb * 4:(hb + 1) * 4, :].rearrange("p a b -> p (a b)"), in1=X[:], op=ALU.add),
                        r=[B_S] + BXs, w=[B_S])
                    yield
                    P.op("act", lambda e: e.activation(out=S_bf[:, hb * 4:(hb + 1) * 4, :], in_=S[:, hb * 4:(hb + 1) * 4, :],
                                                       func=AF.Copy), r=[B_S], w=[B_Sbf])
                    yield
                scan_done[n] = True

            def gen_epi(blk):
                bp = blk % 2
                c0 = blk * GB
                P.op("sp", lambda e: e.dma_start(
                    out=gg[:], in_=projT[38 * 128:46 * 128, c0:c0 + GB].rearrange("(h p) t -> p h t", p=128)),
                    r=[B_proj], w=[B_gg], dma=True)
                yield
                P.op("act", lambda e: e.activation(out=sgg[:], in_=gg[:], func=AF.Silu), r=[B_gg], w=[B_sgg])
                yield
                for h in range(8):
                    P.op("act", lambda e: e.activation(out=sq4e[:], in_=oTb[bp][:, h, :], func=AF.Square),
                         r=[B_oTb[bp]], w=[B_sq4e])
                    yield
                    P.op("pe", lambda e: e.matmul(pb2[:, 256:256 + GB], lhsT=ones_b[:], rhs=sq4e[:], start=True, stop=True),
                         r=[B_ones, B_sq4e], w=[B_ptb])
                    P.op("act", lambda e: e.activation(out=rs4e[:], in_=pb2[:, 256:256 + GB], func=AF.Ln, scale=1.0 / 128,
                                                       bias=EPS), r=[B_ptb], w=[B_rs4e])
                    P.op("act", lambda e: e.activation(out=rs4e[:], in_=rs4e[:], func=AF.Exp, scale=-0.5),
                         r=[B_rs4e], w=[B_rs4e])
                    yield
                    P.op("dve", lambda e: e.scalar_tensor_tensor(out=o1[:], in0=oTb[bp][:, h, :], scalar=gv[:, 6:7], in1=rs4e[:],
                                                                 op0=ALU.mult, op1=ALU.mult),
                         r=[B_oTb[bp], B_gv, B_rs4e], w=[B_o1])
                    ok = cnt4["o2"] % 2
                    cnt4["o2"] += 1
                    P.op("dve", lambda e: e.tensor_tensor(out=o2[ok][:], in0=o1[:], in1=sgg[:, h, :], op=ALU.mult),
                         r=[B_o1, B_sgg], w=[B_o2[ok]])
                    P.op("pool", lambda e: e.dma_start(out=mixT[1024 + h * 128:1024 + (h + 1) * 128, c0:c0 + GB], in_=o2[ok][:]),
                         r=[B_o2[ok]], w=[B_mix], dma=True)
                    yield
                epi_done[blk] = True

            def chunks_of(b):
                return range(b * CPB, (b + 1) * CPB)

            def stream_prep(par):
                for n in range(par, NC, 2):
                    while not pro_done[n // CPB] or (n >= 3 and not scan_done[n - 3]):
                        yield "blocked"
                    yield from gen_prep(n)

            def stream_scan():
                for n in range(NC):
                    b = n // CPB
                    while not prep_done[n] or (b >= 2 and not epi_done[b - 2]):
                        yield "blocked"
                    yield from gen_scan(n)

            def stream_pro():
                for b in range(NGB):
                    while b >= 2 and not all(prep_done[c] for c in chunks_of(b - 2)):
                        yield "blocked"
                    yield from gen_pro(b)

            def stream_epi():
                for b in range(NGB):
                    while not scan_done[b * CPB + CPB - 1]:
                        yield "blocked"
                    yield from gen_epi(b)

            if os.environ.get("GDN_SEQ"):
                for b in range(NGB):
                    for _ in gen_pro(b):
                        pass
                    for n in chunks_of(b):
                        if not os.environ.get("GDN_NOPREP"):
                            kmax = int(os.environ.get("GDN_PREP_STEPS", "1000"))
                            for k_, _ in enumerate(gen_prep(n)):
                                if k_ + 1 >= kmax:
                                    break
                        if not os.environ.get("GDN_NOSCAN"):
                            for _ in gen_scan(n):
                                pass
                    if not os.environ.get("GDN_NOEPI"):
                        for _ in gen_epi(b):
                            pass
                streams = []
            else:
                named = {"pro": stream_pro(), "p0": stream_prep(0), "p1": stream_prep(1), "scan": stream_scan(),
                         "epi": stream_epi()}
                spec = os.environ.get("GDN_SCHED", "pro:1,p0:1,p1:1,scan:1,epi:1")
                streams = []
                for item in spec.split(","):
                    nm_, w_ = item.split(":")
                    streams.append([named[nm_], int(w_)])
            while streams:
                progressed = False
                for st in list(streams):
                    for _rep in range(st[1]):
                        try:
                            r_ = next(st[0])
                            if r_ != "blocked":
                                progressed = True
                            else:
                                break
                        except StopIteration:
                            streams.remove(st)
                            progressed = True
                            break
                assert progressed, "GDN scheduler deadlock"

        P.barrier()
        with ExitStack() as ps:
            def sb5(name, shape, dt):
                return ps.enter_context(nc.sbuf_tensor(name, shape, dt))

            wob = sb5("wob", [128, 16, 2048], BF16)
            wof = [sb5("wof%d" % i, [128, 2, 2048], F32) for i in range(2)]
            mt = [sb5("mt%d" % i, [128, 16, 512], BF16) for i in range(2)]
            xr = [sb5("xr%d" % i, [128, 2048], F32) for i in range(2)]
            yo = [sb5("yo%d" % i, [128, 2048], F32) for i in range(2)]
            B_wobq = [Buf() for _ in range(8)]
            B_wof = [Buf(), Buf()]
            B_mt = [Buf(), Buf()]
            B_xr = [Buf(), Buf()]
            B_yo = [Buf(), Buf()]
            B_outs = [Buf() for _ in range(4)]
            po = [ps.enter_context(nc.psum_tensor("po%d" % i, [128, 512], F32)) for i in range(8)]
            B_po = [Buf() for _ in range(8)]
            for q in range(8):
                k = q % 2
                P.op("sp", lambda e, k=k, q=q: e.dma_start(out=wof[k][:], in_=wout[:, 2 * q:2 * q + 2, :]),
                     w=[B_wof[k]], dma=True)
                P.op("act", lambda e, k=k, q=q: e.activation(out=wob[:, 2 * q:2 * q + 2, :], in_=wof[k][:], func=AF.Copy),
                     r=[B_wof[k]], w=[B_wobq[q]])
            npo = 0
            for t4 in range(T // 512):
                mk = t4 % 2
                P.op("sp", lambda e, mk=mk, t4=t4: e.dma_start(
                    out=mt[mk][:], in_=mixT[:, t4 * 512:(t4 + 1) * 512].rearrange("(c p) t -> p c t", p=128)),
                    r=[B_mix], w=[B_mt[mk]], dma=True)
                for tt in range(4):
                    tg = t4 * 4 + tt
                    k = tg % 2
                    ts_ = slice(tg * 128, (tg + 1) * 128)
                    P.op("sp", lambda e, k=k, ts_=ts_: e.dma_start(out=xr[k][:], in_=x[ts_, :]), w=[B_xr[k]], dma=True)
                    for nq in range(4):
                        pk = npo % 8
                        npo += 1
                        for c in range(16):
                            P.op("pe", lambda e, pk=pk, nq=nq, c=c, tt=tt, mk=mk: e.matmul(
                                po[pk][:], lhsT=mt[mk][:, c, tt * 128:(tt + 1) * 128], rhs=wob[:, c, nq * 512:(nq + 1) * 512],
                                start=(c == 0), stop=(c == 15)), r=[B_mt[mk], B_wobq[c // 2]], w=[B_po[pk]], inc=(c == 15))
                        P.op("dve", lambda e, k=k, nq=nq, pk=pk: e.tensor_tensor(
                            out=yo[k][:, nq * 512:(nq + 1) * 512], in0=po[pk][:], in1=xr[k][:, nq * 512:(nq + 1) * 512],
                            op=ALU.add), r=[B_po[pk], B_xr[k]], w=[B_yo[k]])
                    P.op("pool", lambda e, k=k, ts_=ts_: e.dma_start(out=out[ts_, :], in_=yo[k][:]),
                         r=[B_yo[k]], w=[B_outs[tg % 4]], dma=True)
            P.wait_all("pool", B_outs + [B_mix])
        final = {P.esem[k]: P.cnt[k] for k in P.eng}
        for q in P.dsems:
            for s_, v_ in zip(P.dsems[q], P.dval[q]):
                final[s_] = v_
        for s_, v_ in P.maxwait.items():
            assert final.get(s_, 0) >= v_, ("unreachable wait", s_, v_, final.get(s_))
        print("ops:", P.cnt, "waits:", P.nwaits, "pend:", P.pend)
    return nc


COL_SLICES = {
    "cq": (0, 512), "ckv": (512, 768), "krope": (768, 832), "mgate": (832, 1856), "gq": (1856, 2880),
    "gk": (2880, 3904), "gv": (3904, 4928), "ga": (4928, 4936), "gb": (4936, 4944), "ggate": (4944, 5968),
}


def layout_inputs(T, x_b, pos_b, inputs):
    w_in = inputs["w_in"][0]
    cs = COL_SLICES

    def cols(name):
        a, b = cs[name]
        return w_in[:, a:b]

    kr = cols("krope")
    kr_perm = np.concatenate([kr[:, 32:64], kr[:, 0:32]], axis=1)
    wr = np.concatenate([cols("cq"), cols("ckv"), cols("mgate"), cols("gq"), cols("gk"), cols("gv"),
                         cols("ggate"), kr, kr_perm], axis=1)
    win = np.ascontiguousarray(wr.reshape(16, 128, NG, 128).transpose(2, 1, 0, 3))
    wab = np.concatenate([cols("ga"), cols("gb")], axis=1)
    wab = np.ascontiguousarray(wab.reshape(16, 128, 16).transpose(1, 0, 2))
    ngain = np.ascontiguousarray(inputs["norm_gain"][0].reshape(16, 128).T)
    mla_q = inputs["mla_q_norm_gain"][0]
    mla_k = inputs["mla_k_norm_gain"][0]
    wuq_o = inputs["w_uq"][0]
    parts = []
    for h in range(8):
        blk = wuq_o[:, h * 192:(h + 1) * 192]
        rp = blk[:, 128:192]
        parts += [blk[:, 0:128], rp, np.concatenate([rp[:, 32:64], rp[:, 0:32]], axis=1)]
    wuq = np.concatenate(parts, axis=1)
    wuq = np.ascontiguousarray(wuq.reshape(4, 128, 2048).transpose(1, 0, 2))
    wukv_o = inputs["w_ukv"][0]
    kn = [wukv_o[:, h * 256:h * 256 + 128] for h in range(8)]
    vv = [wukv_o[:, h * 256 + 128:(h + 1) * 256] for h in range(8)]
    wukv = np.concatenate(kn + vv, axis=1)
    wukv = np.ascontiguousarray(wukv.reshape(2, 128, 2048).transpose(1, 0, 2))
    qag = np.ascontiguousarray(inputs["mla_q_a_gain"][0].reshape(4, 128).T)
    kvag = np.ascontiguousarray(inputs["mla_kv_a_gain"][0].reshape(2, 128).T)
    gvec = np.zeros((128, 8), np.float32)

    def perm(v):
        return np.concatenate([v[32:64], v[0:32]])

    gvec[:, 0] = mla_q[0:128]
    gvec[0:64, 1] = mla_q[128:192]
    gvec[0:64, 2] = perm(mla_q[128:192])
    gvec[:, 3] = mla_k[0:128]
    gvec[0:64, 4] = mla_k[128:192]
    gvec[0:64, 5] = perm(mla_k[128:192])
    gvec[:, 6] = inputs["gdn_out_norm_gain"][0]
    cwo = inputs["gdn_conv_w"][0]
    cw = np.ascontiguousarray(cwo.reshape(4, 24, 128).transpose(2, 1, 0))
    m = {"x": np.ascontiguousarray(x_b[:T]), "win": win, "wab": wab, "ngain": ngain,
         "pos64": np.ascontiguousarray(np.broadcast_to(pos_b[None, :T], (64, T))),
         "wuq": wuq, "wukv": wukv, "qag": qag, "kvag": kvag, "gvec": gvec, "cw": cw,
         "alog64": np.ascontiguousarray(np.broadcast_to(inputs["gdn_a_log"][0][None, :], (64, 8))),
         "dtb64": np.ascontiguousarray(np.broadcast_to(inputs["gdn_dt_bias"][0][None, :], (64, 8))),
         "wout": np.ascontiguousarray(inputs["w_out"][0].reshape(16, 128, 2048).transpose(1, 0, 2))}
    return m


_NC_CACHE = {}


def kernel(**inputs):
    T = inputs["x"].shape[1]
    B = inputs["x"].shape[0]
    if T not in _NC_CACHE:
        _NC_CACHE[T] = build_nc(T)
    nc = _NC_CACHE[T]
    inputs = {k: np.asarray(v) for k, v in inputs.items()}
    maps = [layout_inputs(T, inputs["x"][c % B], inputs["positions"][c % B], inputs) for c in range(8)]
    res = run_bass_kernel_spmd(nc, maps, core_ids=list(range(8)))
    return np.stack([np.asarray(res.results[b]["out"]) for b in range(B)], axis=0).astype(np.float32)
```
